# Optimizing a Trainium2 kernel written in Bass

```python
import math
import jax, jax.numpy as jnp
from jax import lax
import numpy as np

D_MODEL = 1024
BATCH = 8
SEQ = 2048
DEPTH = 4
DEC_BATCH = 128
DEC_SEQ = 4
PAST_LEN = 16384
PAGE_SIZE = 128

N_MIXERS = 3
N_RWKV = len(range(0, DEPTH, N_MIXERS))
N_HGRN = len(range(1, DEPTH, N_MIXERS))
N_GLA = len(range(2, DEPTH, N_MIXERS))

RW_HEAD = 64
RW_HEADS = D_MODEL // RW_HEAD
RW_DECAY_LORA = max(32, int(round(1.8 * D_MODEL ** 0.5 / 32)) * 32)
RW_AAA_LORA = max(32, int(round(1.8 * D_MODEL ** 0.5 / 32)) * 32)
RW_MV_LORA = max(32, int(round(1.3 * D_MODEL ** 0.5 / 32)) * 32)
RW_GATE_LORA = max(32, int(round(0.6 * D_MODEL ** 0.8 / 32)) * 32)
RW_LNX_EPS = 64e-5

HG_HEAD = 128
HG_HEADS = D_MODEL // HG_HEAD

GL_HEADS = 4
GL_DK = (D_MODEL // 2) // GL_HEADS
GL_DV = D_MODEL // GL_HEADS
GL_GATE_RANK = 16
GL_GATE_NORM = 16.0

CHUNK = 32

D_FF = ((8 * D_MODEL // 3 + 127) // 128) * 128
CONV_W = 3

LN_EPS = 1e-5
RMS_EPS = 1e-5
DN_ALPHA = (2 * DEPTH) ** 0.25
DN_BETA = (8 * DEPTH) ** -0.25

kernel_name = "rwkv7_hgrn2_gla_convffn_deepnorm_step"


def layer_norm(x, w, b):
    xf = x.astype(jnp.float32)
    mu = jnp.mean(xf, -1, keepdims=True)
    xc = xf - mu
    var = jnp.mean(xc * xc, -1, keepdims=True)
    return (xc * lax.rsqrt(var + LN_EPS) * w.astype(jnp.float32) + b.astype(jnp.float32)).astype(x.dtype)


def head_rms_gate(o, gain, gate):
    B, T, H, V = o.shape
    o = o * lax.rsqrt(jnp.mean(o * o, -1, keepdims=True) + RMS_EPS) * gain.astype(jnp.float32)
    return (o.reshape(B, T, H * V) * jax.nn.silu(gate.astype(jnp.float32))).astype(gate.dtype)


def chunked_gla(q, k, v, log_g, s0):
    B, T, H, K = q.shape
    V = v.shape[-1]
    C = min(CHUNK, T)
    n = -(-T // C)
    pad = n * C - T

    def blocks(a):
        a = jnp.pad(a.astype(jnp.float32), ((0, 0), (0, pad), (0, 0), (0, 0)))
        return a.reshape(B, n, C, H, a.shape[-1]).transpose(1, 0, 3, 2, 4)

    qc, kc, vc, gc = blocks(q), blocks(k), blocks(v), blocks(log_g)
    mask = jnp.tril(jnp.ones((C, C), dtype=bool))
    ref = (C - 1) // 2

    def step(S, inp):
        qb, kb, vb, gb = inp
        b = jnp.cumsum(gb, axis=-2)
        b_ref = b[:, :, ref:ref + 1]
        b_last = b[:, :, -1:]
        qi = qb * jnp.exp(b - b_ref)
        ki = kb * jnp.exp(b_ref - b)
        A = jnp.where(mask, jnp.einsum('bhck,bhdk->bhcd', qi, ki), 0.0)
        o = jnp.einsum('bhcd,bhdv->bhcv', A, vb) + jnp.einsum('bhck,bhkv->bhcv', qb * jnp.exp(b), S)
        S = jnp.exp(b_last[:, :, 0])[..., None] * S + jnp.einsum('bhck,bhcv->bhkv', kb * jnp.exp(b_last - b), vb)
        return S, o

    S, o = lax.scan(step, s0.astype(jnp.float32), (qc, kc, vc, gc))
    o = o.transpose(1, 0, 3, 2, 4).reshape(B, n * C, H, V)[:, :T]
    return o, S


def rwkv7_time_mix(x, shift_prev, s0, v_first, vmix, mix, wr, wk, wv, wo, w0, w1, w2, a0, a1, a2,
                   g1, g2, k_k, k_a, r_k, lnx_w, lnx_b):
    B, T, D = x.shape
    H, N = RW_HEADS, RW_HEAD
    x_prev = jnp.concatenate([shift_prev[:, None, :].astype(x.dtype), x[:, :-1]], axis=1)
    xx = x_prev - x
    xr, xw, xk, xv, xa, xg = (x + xx * mix[j] for j in range(6))
    r = xr @ wr
    k = xk @ wk
    v = xv @ wv
    w_log = -jnp.exp(-jax.nn.softplus(-(w0 + jnp.tanh(xw @ w1) @ w2).astype(jnp.float32)) - 0.5)
    a = jax.nn.sigmoid(a0 + (xa @ a1) @ a2)
    if vmix is None:
        v_first = v
    else:
        v0, v1, v2 = vmix
        v = v + (v_first - v) * jax.nn.sigmoid(v0 + (xv @ v1) @ v2)
    g = jax.nn.sigmoid(xg @ g1) @ g2

    def hs(t):
        return t.astype(jnp.float32).reshape(B, T, H, N)

    r, k, v, a, w_log = hs(r), hs(k), hs(v), hs(a), hs(w_log)
    kk = k * k_k.astype(jnp.float32).reshape(H, N)
    kk = kk / jnp.maximum(jnp.sqrt(jnp.sum(kk * kk, -1, keepdims=True)), 1e-12)
    k = k * (1.0 + (a - 1.0) * k_a.astype(jnp.float32).reshape(H, N))
    decay = jnp.exp(w_log)

    def step(S, inp):
        r_t, d_t, k_t, v_t, kk_t, a_t = inp
        sa = jnp.einsum('bhvk,bhk->bhv', S, -kk_t)
        S = S * d_t[:, :, None, :] + sa[..., None] * (kk_t * a_t)[:, :, None, :] + v_t[..., None] * k_t[:, :, None, :]
        return S, jnp.einsum('bhvk,bhk->bhv', S, r_t)

    tm = lambda t: jnp.swapaxes(t, 0, 1)
    S, y = lax.scan(step, s0.astype(jnp.float32), (tm(r), tm(decay), tm(k), tm(v), tm(kk), tm(a)))
    y = tm(y)
    mu = jnp.mean(y, -1, keepdims=True)
    yc = y - mu
    y = yc * lax.rsqrt(jnp.mean(yc * yc, -1, keepdims=True) + RW_LNX_EPS)
    y = y.reshape(B, T, D) * lnx_w.astype(jnp.float32) + lnx_b.astype(jnp.float32)
    y = y + (jnp.sum(r * k * r_k.astype(jnp.float32), -1, keepdims=True) * v).reshape(B, T, D)
    out = (y * g.astype(jnp.float32)).astype(x.dtype) @ wo
    return out, x[:, -1], S, v_first


def hgrn2_mix(x, s0, lb, wq, wf, wi, wg, wo, norm_w):
    B, T, D = x.shape
    H, K = HG_HEADS, HG_HEAD
    q = jax.nn.silu((x @ wq).astype(jnp.float32)) * K ** -0.5
    f = (x @ wf).astype(jnp.float32)
    lb = lb.astype(jnp.float32)
    log_g = jnp.logaddexp(jnp.log(lb), jnp.log1p(-lb) + jax.nn.log_sigmoid(f))
    k = (1.0 - lb) * jax.nn.sigmoid(-f)
    i = (x @ wi).astype(jnp.float32)
    hs = lambda t: t.reshape(B, T, H, K)
    o, S = chunked_gla(hs(q), hs(k), hs(i), hs(log_g), s0)
    o = head_rms_gate(o, norm_w, x @ wg)
    return o @ wo, S


def gla_mix(x, s0, wq, wk, wv, wg, gk1, gk2, gk_b, wo, norm_w):
    B, T, D = x.shape
    q = (x @ wq).astype(jnp.float32) * GL_DK ** -0.5
    k = x @ wk
    v = x @ wv
    log_g = jax.nn.log_sigmoid(((x @ gk1) @ gk2 + gk_b).astype(jnp.float32)) / GL_GATE_NORM
    hk = lambda t: t.reshape(B, T, GL_HEADS, GL_DK)
    o, S = chunked_gla(hk(q), hk(k), v.reshape(B, T, GL_HEADS, GL_DV), hk(log_g), s0)
    o = head_rms_gate(o, norm_w, x @ wg)
    return o @ wo, S


def conv_ffn(x, buf, wu, wg, conv_w, conv_b, wd):
    T = x.shape[1]
    u = x @ wu
    z = jnp.concatenate([buf.astype(x.dtype), x @ wg], axis=1)
    zc = conv_b + sum(conv_w[j] * z[:, j:j + T] for j in range(CONV_W))
    return (jax.nn.silu(zc) * u) @ wd, z[:, T:]


def run_trunk(x, st_rw, st_shift, st_hg, st_gl, st_conv, rw_w, rw_vmix, hg_w, hg_lb_param, gl_w, ffn_w, ln_w):
    ln1_w, ln1_b, ln2_w, ln2_b = ln_w
    lb_soft = jax.nn.softmax(hg_lb_param.astype(jnp.float32), axis=0)
    lower_bounds = jnp.cumsum(lb_soft, axis=0) - lb_soft[0]
    new_rw, new_shift, new_hg, new_gl, new_conv = [], [], [], [], []
    v_first = None
    for i in range(DEPTH):
        j = i // N_MIXERS
        kind = i % N_MIXERS
        if kind == 0:
            vm = None if j == 0 else (rw_vmix[0][j - 1], rw_vmix[1][j - 1], rw_vmix[2][j - 1])
            h, sh, S, v_first = rwkv7_time_mix(x, st_shift[j], st_rw[j], v_first, vm, *[p[j] for p in rw_w])
            new_rw.append(S)
            new_shift.append(sh.astype(st_shift.dtype))
        elif kind == 1:
            h, S = hgrn2_mix(x, st_hg[j], lower_bounds[i], *[p[j] for p in hg_w])
            new_hg.append(S)
        else:
            h, S = gla_mix(x, st_gl[j], *[p[j] for p in gl_w])
            new_gl.append(S)
        x = layer_norm(DN_ALPHA * x + h, ln1_w[i], ln1_b[i])
        f, buf = conv_ffn(x, st_conv[i], *[p[i] for p in ffn_w])
        new_conv.append(buf.astype(st_conv.dtype))
        x = layer_norm(DN_ALPHA * x + f, ln2_w[i], ln2_b[i])
    return (x, jnp.stack(new_rw).astype(st_rw.dtype), jnp.stack(new_shift), jnp.stack(new_hg).astype(st_hg.dtype),
            jnp.stack(new_gl).astype(st_gl.dtype), jnp.stack(new_conv))


def setup_inputs(seed: int = 0) -> dict:
    key = jax.random.key(seed)
    ks = iter(jax.random.split(key, 64))

    def nrm(shape, scale=1.0):
        return jax.random.normal(next(ks), shape, jnp.float32) * scale

    def uni(shape, lo, hi):
        return jax.random.uniform(next(ks), shape, jnp.float32, lo, hi)

    D, F = D_MODEL, D_FF
    NR, NH, NG = N_RWKV, N_HGRN, N_GLA
    sd = D ** -0.5
    return {
        "x_prompt": nrm((BATCH, SEQ, D)),
        "x_sample": nrm((DEC_BATCH, DEC_SEQ, D)),
        "state_rwkv": nrm((NR, DEC_BATCH, RW_HEADS, RW_HEAD, RW_HEAD), 0.5),
        "state_rwkv_shift": nrm((NR, DEC_BATCH, D)),
        "state_hgrn": nrm((NH, DEC_BATCH, HG_HEADS, HG_HEAD, HG_HEAD), 0.5),
        "state_gla": nrm((NG, DEC_BATCH, GL_HEADS, GL_DK, GL_DV), 0.5),
        "state_ffn_conv": nrm((DEPTH, DEC_BATCH, CONV_W - 1, F)),
        "rw_mix": uni((NR, 6, D), 0.0, 1.0),
        "rw_wr": nrm((NR, D, D), sd),
        "rw_wk": nrm((NR, D, D), sd),
        "rw_wv": nrm((NR, D, D), sd),
        "rw_wo": nrm((NR, D, D), sd * DN_BETA),
        "rw_w0": uni((NR, D), -6.0, -1.0),
        "rw_w1": nrm((NR, D, RW_DECAY_LORA), sd),
        "rw_w2": nrm((NR, RW_DECAY_LORA, D), 0.1 * RW_DECAY_LORA ** -0.5),
        "rw_a0": nrm((NR, D), 0.1),
        "rw_a1": nrm((NR, D, RW_AAA_LORA), sd),
        "rw_a2": nrm((NR, RW_AAA_LORA, D), 0.1 * RW_AAA_LORA ** -0.5),
        "rw_g1": nrm((NR, D, RW_GATE_LORA), sd),
        "rw_g2": nrm((NR, RW_GATE_LORA, D), RW_GATE_LORA ** -0.5),
        "rw_k_k": 0.85 + nrm((NR, D), 0.05),
        "rw_k_a": 1.0 + nrm((NR, D), 0.05),
        "rw_r_k": nrm((NR, RW_HEADS, RW_HEAD), 0.1),
        "rw_lnx_w": 1.0 + nrm((NR, D), 0.02),
        "rw_lnx_b": nrm((NR, D), 0.02),
        "rw_v0": 1.0 + nrm((NR - 1, D), 0.1),
        "rw_v1": nrm((NR - 1, D, RW_MV_LORA), sd),
        "rw_v2": nrm((NR - 1, RW_MV_LORA, D), 0.1 * RW_MV_LORA ** -0.5),
        "hg_wq": nrm((NH, D, D), sd),
        "hg_wf": nrm((NH, D, D), sd),
        "hg_wi": nrm((NH, D, D), sd),
        "hg_wg": nrm((NH, D, D), sd),
        "hg_wo": nrm((NH, D, D), sd * DN_BETA),
        "hg_norm_w": 1.0 + nrm((NH, HG_HEAD), 0.02),
        "hg_lb_param": nrm((DEPTH, D), 0.1),
        "gl_wq": nrm((NG, D, GL_HEADS * GL_DK), sd),
        "gl_wk": nrm((NG, D, GL_HEADS * GL_DK), sd),
        "gl_wv": nrm((NG, D, D), sd),
        "gl_wg": nrm((NG, D, D), sd),
        "gl_gk1": nrm((NG, D, GL_GATE_RANK), sd),
        "gl_gk2": nrm((NG, GL_GATE_RANK, GL_HEADS * GL_DK), GL_GATE_RANK ** -0.5),
        "gl_gk_b": nrm((NG, GL_HEADS * GL_DK), 0.1),
        "gl_wo": nrm((NG, D, D), sd * DN_BETA),
        "gl_norm_w": 1.0 + nrm((NG, GL_DV), 0.02),
        "ffn_wu": nrm((DEPTH, D, F), sd),
        "ffn_wg": nrm((DEPTH, D, F), sd),
        "ffn_conv_w": nrm((DEPTH, CONV_W, F), CONV_W ** -0.5),
        "ffn_conv_b": nrm((DEPTH, F), 0.02),
        "ffn_wd": nrm((DEPTH, F, D), F ** -0.5 * DN_BETA),
        "ln1_w": 1.0 + nrm((DEPTH, D), 0.02),
        "ln1_b": nrm((DEPTH, D), 0.02),
        "ln2_w": 1.0 + nrm((DEPTH, D), 0.02),
        "ln2_b": nrm((DEPTH, D), 0.02),
    }


def reference(x_prompt, x_sample, state_rwkv, state_rwkv_shift, state_hgrn, state_gla, state_ffn_conv,
              rw_mix, rw_wr, rw_wk, rw_wv, rw_wo, rw_w0, rw_w1, rw_w2, rw_a0, rw_a1, rw_a2, rw_g1, rw_g2,
              rw_k_k, rw_k_a, rw_r_k, rw_lnx_w, rw_lnx_b, rw_v0, rw_v1, rw_v2,
              hg_wq, hg_wf, hg_wi, hg_wg, hg_wo, hg_norm_w, hg_lb_param,
              gl_wq, gl_wk, gl_wv, gl_wg, gl_gk1, gl_gk2, gl_gk_b, gl_wo, gl_norm_w,
              ffn_wu, ffn_wg, ffn_conv_w, ffn_conv_b, ffn_wd, ln1_w, ln1_b, ln2_w, ln2_b):
    rw_w = (rw_mix, rw_wr, rw_wk, rw_wv, rw_wo, rw_w0, rw_w1, rw_w2, rw_a0, rw_a1, rw_a2, rw_g1, rw_g2,
            rw_k_k, rw_k_a, rw_r_k, rw_lnx_w, rw_lnx_b)
    rw_vmix = (rw_v0, rw_v1, rw_v2)
    hg_w = (hg_wq, hg_wf, hg_wi, hg_wg, hg_wo, hg_norm_w)
    gl_w = (gl_wq, gl_wk, gl_wv, gl_wg, gl_gk1, gl_gk2, gl_gk_b, gl_wo, gl_norm_w)
    ffn_w = (ffn_wu, ffn_wg, ffn_conv_w, ffn_conv_b, ffn_wd)
    ln_w = (ln1_w, ln1_b, ln2_w, ln2_b)

    z_rw = jnp.zeros((N_RWKV, BATCH) + state_rwkv.shape[2:], state_rwkv.dtype)
    z_shift = jnp.zeros((N_RWKV, BATCH) + state_rwkv_shift.shape[2:], state_rwkv_shift.dtype)
    z_hg = jnp.zeros((N_HGRN, BATCH) + state_hgrn.shape[2:], state_hgrn.dtype)
    z_gl = jnp.zeros((N_GLA, BATCH) + state_gla.shape[2:], state_gla.dtype)
    z_conv = jnp.zeros((DEPTH, BATCH) + state_ffn_conv.shape[2:], state_ffn_conv.dtype)

    y_prompt, p_rwkv, p_rwkv_shift, p_hgrn, p_gla, p_ffn_conv = run_trunk(
        x_prompt, z_rw, z_shift, z_hg, z_gl, z_conv, rw_w, rw_vmix, hg_w, hg_lb_param, gl_w, ffn_w, ln_w)
    y_sample, s_rwkv, s_rwkv_shift, s_hgrn, s_gla, s_ffn_conv = run_trunk(
        x_sample, state_rwkv, state_rwkv_shift, state_hgrn, state_gla, state_ffn_conv,
        rw_w, rw_vmix, hg_w, hg_lb_param, gl_w, ffn_w, ln_w)
    return (y_prompt, y_sample, p_rwkv, p_rwkv_shift, p_hgrn, p_gla, p_ffn_conv,
            s_rwkv, s_rwkv_shift, s_hgrn, s_gla, s_ffn_conv)
```

```python
import contextlib
import numpy as np
import concourse.bass as bass
import concourse.mybir as mybir

F32 = mybir.dt.float32
BF16 = mybir.dt.bfloat16
AF = mybir.ActivationFunctionType
ALU = mybir.AluOpType
AX = mybir.AxisListType

SAME_ENGINE_SYNC = True
N_DMA_SEMS = 40


class T:
    def __init__(self, tile, name):
        self.t = tile
        self.name = name
        self.lw = None
        self.rd = {}

    def __getitem__(self, idx):
        return V(self.t[idx], self)

    def sub(self, name):
        return T(self.t, self.name + "." + name)


class V:
    def __init__(self, ap, owner):
        self.ap = ap
        self.o = owner

    def __getitem__(self, idx):
        return V(self.ap[idx], self.o)

    def rr(self, pat, **kw):
        return V(self.ap.rearrange(pat, **kw), self.o)

    def bc(self, shape):
        return V(self.ap.broadcast_to(shape), self.o)

    def us(self, axis):
        return V(self.ap.unsqueeze(axis), self.o)

    def bitcast(self, dt):
        return V(self.ap.bitcast(dt), self.o)

    @property
    def shape(self):
        return self.ap.shape


def _own(vs):
    r = []
    for v in vs:
        if isinstance(v, V):
            if isinstance(v.o, (list, tuple)):
                r.extend(v.o)
            else:
                r.append(v.o)
    return r


class KB:
    def __init__(self, nc):
        self.nc = nc
        self.es = contextlib.ExitStack()
        self.eng = {"pe": nc.tensor, "act": nc.scalar, "dve": nc.vector, "pool": nc.gpsimd, "sp": nc.sync}
        self.sem = {}
        self.cnt = {}
        self.waited = {e: {} for e in self.eng}
        for e in self.eng:
            self.sem[e] = self.es.enter_context(nc.semaphore("s_" + e))
            self.cnt[e] = 0
        self.dsem = [self.es.enter_context(nc.semaphore("d%d" % i)) for i in range(N_DMA_SEMS)]
        self.dcnt = [0] * N_DMA_SEMS
        self.ndma = 0
        self.n_ins = 0
        self.n_wait = 0
        self.out_events = []

    def sb(self, name, shape, dt=F32):
        return T(self.es.enter_context(self.nc.sbuf_tensor(name, list(shape), dt)), name)

    def ps(self, name, shape, dt=F32):
        return T(self.es.enter_context(self.nc.psum_tensor(name, list(shape), dt)), name)

    def _wait(self, e, ev):
        en, sem, val, sid = ev
        if en == e and (e == "pe" or not SAME_ENGINE_SYNC):
            return
        w = self.waited[e]
        if w.get(sid, 0) >= val:
            return
        w[sid] = val
        self.eng[e].wait_ge(sem, val)
        self.n_wait += 1

    def _sync(self, e, reads, writes):
        for t in reads:
            if t.lw is not None:
                self._wait(e, t.lw)
        for t in writes:
            if t.lw is not None:
                self._wait(e, t.lw)
            for ev in t.rd.values():
                self._wait(e, ev)

    def _post(self, e, ev, reads, writes, rkey=None):
        for t in reads:
            t.rd[rkey or e] = ev
        for t in writes:
            t.lw = ev
            t.rd = {}

    def emit(self, e, fn, reads, writes):
        reads = _own(reads)
        writes = _own(writes)
        self._sync(e, reads, writes)
        ins = fn()
        self.cnt[e] += 1
        ins.then_inc(self.sem[e], 1)
        ev = (e, self.sem[e], self.cnt[e], "e_" + e)
        self._post(e, ev, reads, writes)
        self.n_ins += 1
        return ev

    def mm(self, out, lhsT, rhs, start=True, stop=True, **kw):
        return self.emit("pe", lambda: self.nc.tensor.matmul(out.ap, lhsT=lhsT.ap, rhs=rhs.ap, start=start, stop=stop, **kw),
                         [lhsT, rhs], [out])

    def tr(self, out, in_, ident):
        return self.emit("pe", lambda: self.nc.tensor.transpose(out.ap, in_.ap, ident.ap), [in_, ident], [out])

    def act(self, out, in_, func, bias=0.0, scale=1.0, e="act"):
        b = bias.ap if isinstance(bias, V) else bias
        s = scale.ap if isinstance(scale, V) else scale
        return self.emit("act", lambda: self.nc.scalar.activation(out=out.ap, in_=in_.ap, func=func, bias=b, scale=s),
                         [in_, bias, scale], [out])

    def tt(self, e, out, in0, in1, op):
        return self.emit(e, lambda: self.eng[e].tensor_tensor(out=out.ap, in0=in0.ap, in1=in1.ap, op=op), [in0, in1], [out])

    def ts(self, e, out, in0, s1, op0, s2=None, op1=None):
        a1 = s1.ap if isinstance(s1, V) else s1
        a2 = s2.ap if isinstance(s2, V) else s2
        kw = {}
        if op1 is not None:
            kw["op1"] = op1
        return self.emit(e, lambda: self.eng[e].tensor_scalar(out=out.ap, in0=in0.ap, scalar1=a1, scalar2=a2, op0=op0, **kw),
                         [in0, s1, s2], [out])

    def stt(self, out, in0, scalar, in1, op0, op1):
        a = scalar.ap if isinstance(scalar, V) else scalar
        return self.emit("dve", lambda: self.nc.vector.scalar_tensor_tensor(out=out.ap, in0=in0.ap, scalar=a, in1=in1.ap, op0=op0, op1=op1),
                         [in0, scalar, in1], [out])

    def copy(self, e, out, in_):
        if e == "act":
            return self.act(out, in_, AF.Copy)
        return self.emit(e, lambda: self.eng[e].tensor_copy(out=out.ap, in_=in_.ap), [in_], [out])

    def memset(self, e, out, val):
        return self.emit(e, lambda: self.eng[e].memset(out.ap, val), [], [out])

    def scan(self, out, d0, d1, init, op0, op1):
        i = init.ap if isinstance(init, V) else init
        return self.emit("dve", lambda: self.nc.vector.tensor_tensor_scan(out=out.ap, data0=d0.ap, data1=d1.ap, initial=i, op0=op0, op1=op1),
                         [d0, d1, init], [out])

    def recip(self, out, in_):
        return self.emit("dve", lambda: self.nc.vector.reciprocal(out=out.ap, in_=in_.ap), [in_], [out])

    def reduce(self, out, in_, op=ALU.add, axis=AX.X):
        return self.emit("dve", lambda: self.nc.vector.tensor_reduce(out=out.ap, in_=in_.ap, axis=axis, op=op), [in_], [out])

    def dma(self, out, in_, is_output=False, extra_reads=(), extra_writes=(), q="sp", **kw):
        e = "pool" if (is_output and q == "sp") else q
        reads = _own([in_]) + list(extra_reads)
        writes = _own([out]) + list(extra_writes)
        self._sync(e, reads, writes)
        j = self.ndma % N_DMA_SEMS
        self.ndma += 1
        sem = self.dsem[j]
        if self.dcnt[j] > 0:
            self._wait(e, ("dma", sem, self.dcnt[j], "d%d" % j))
        self.dcnt[j] += 16
        oa = out.ap if isinstance(out, V) else out
        ia = in_.ap if isinstance(in_, V) else in_
        ins = self.eng[e].dma_start(out=oa, in_=ia, **kw)
        ins.then_inc(sem, 16)
        ev = ("dma", sem, self.dcnt[j], "d%d" % j)
        self._post(e, ev, reads, writes, rkey="dma%d" % self.ndma)
        if is_output:
            self.out_events.append(ev)
        self.n_ins += 1
        return ev

    def finish(self):
        for ev in self.out_events:
            self._wait("sp", ev)
        for j in range(N_DMA_SEMS):
            if self.dcnt[j] > 0:
                self._wait("sp", ("dma", self.dsem[j], self.dcnt[j], "d%d" % j))
        for e in ("pe", "act", "dve", "pool"):
            if self.cnt[e] > 0:
                self._wait("sp", (e, self.sem[e], self.cnt[e], "e_" + e))

    def close(self):
        self.es.close()


import math
from concourse.bass_utils import run_bass_kernel_spmd

DN_ALPHA = 8.0 ** 0.25
C0 = math.exp(-0.5)
D = 1024
FF = 2816
NF = 22
LN_EPS = 1e-5
RMS_EPS = 1e-5
LNX_EPS = 64e-5
MUL, ADD, SUB = ALU.mult, ALU.add, ALU.subtract

WEIGHT_SHAPES = {
    "rw_mix": (2, 6, 1024), "rw_wr": (2, 1024, 1024), "rw_wk": (2, 1024, 1024), "rw_wv": (2, 1024, 1024),
    "rw_wo": (2, 1024, 1024), "rw_w0": (2, 1024), "rw_w1": (2, 1024, 64), "rw_w2": (2, 64, 1024),
    "rw_a0": (2, 1024), "rw_a1": (2, 1024, 64), "rw_a2": (2, 64, 1024), "rw_g1": (2, 1024, 160),
    "rw_g2": (2, 160, 1024), "rw_k_k": (2, 1024), "rw_k_a": (2, 1024), "rw_r_k": (2, 16, 64),
    "rw_lnx_w": (2, 1024), "rw_lnx_b": (2, 1024), "rw_v0": (1, 1024), "rw_v1": (1, 1024, 32),
    "rw_v2": (1, 32, 1024), "hg_wq": (1, 1024, 1024), "hg_wf": (1, 1024, 1024), "hg_wi": (1, 1024, 1024),
    "hg_wg": (1, 1024, 1024), "hg_wo": (1, 1024, 1024), "hg_norm_w": (1, 128), "hg_lb_param": (4, 1024),
    "gl_wq": (1, 1024, 512), "gl_wk": (1, 1024, 512), "gl_wv": (1, 1024, 1024), "gl_wg": (1, 1024, 1024),
    "gl_gk1": (1, 1024, 16), "gl_gk2": (1, 16, 512), "gl_gk_b": (1, 512), "gl_wo": (1, 1024, 1024),
    "gl_norm_w": (1, 256), "ffn_wu": (4, 1024, 2816), "ffn_wg": (4, 1024, 2816), "ffn_conv_w": (4, 3, 2816),
    "ffn_conv_b": (4, 2816), "ffn_wd": (4, 2816, 1024), "ln1_w": (4, 1024), "ln1_b": (4, 1024),
    "ln2_w": (4, 1024), "ln2_b": (4, 1024),
}
IN_SHAPES = {
    "x_prompt": (2048, 1024), "x_sample": (64, 1024), "state_rwkv": (2, 16, 16, 64, 64),
    "state_rwkv_shift": (2, 16, 1024), "state_hgrn": (16, 8, 128, 128), "state_gla": (16, 4, 128, 256),
    "state_ffn_conv": (4, 16, 2, 2816),
}
OUT_SHAPES = {
    "y_prompt": (2048, 1024), "y_sample": (64, 1024), "p_rwkv": (2, 16, 64, 64), "p_shift": (2, 1024),
    "p_hgrn": (8, 128, 128), "p_gla": (4, 128, 256), "p_conv": (4, 2, 2816),
    "s_rwkv": (2, 16, 16, 64, 64), "s_shift": (2, 16, 1024), "s_hgrn": (16, 8, 128, 128),
    "s_gla": (16, 4, 128, 256), "s_conv": (4, 16, 2, 2816),
}


class Arena:
    def __init__(self, k, name, nkb):
        self.t = k.es.enter_context(k.nc.sbuf_tensor(name, [128, nkb * 256], F32))
        self.slots = [T(self.t, "%s%d" % (name, i)) for i in range(nkb)]
        self.nkb = nkb
        self.p = 0

    def reset(self, p=0):
        self.p = p

    def take(self, nbytes, dt=F32, pat=None, parts=128, **kw):
        nkb = (nbytes + 1023) // 1024
        off = self.p
        self.p += nkb
        assert self.p <= self.nkb, "arena overflow %d > %d" % (self.p, self.nkb)
        ap = self.t[0:parts, off * 256: off * 256 + nbytes // 4]
        if dt == BF16:
            ap = ap.bitcast(BF16)
        if pat:
            ap = ap.rearrange(pat, **kw)
        return V(ap, self.slots[off:off + nkb])

    def buf(self, n, w, dt=F32):
        es = 2 if dt == BF16 else 4
        off = self.p
        full = self.take(n * w * es, dt, "p (n w) -> p n w", n=n)
        ch = []
        for c in range(n):
            b0 = c * w * es
            b1 = (c + 1) * w * es
            ch.append(V(full.ap[:, c, :], self.slots[off + b0 // 1024: off + (b1 + 1023) // 1024]))
        return full, ch


def build(nc, dbg_n=0):
    k = KB(nc)
    A = {}
    for n, s in list(IN_SHAPES.items()) + list(WEIGHT_SHAPES.items()):
        A[n] = nc.dram_tensor(n, list(s), F32, kind="ExternalInput").ap()
    for n, s in OUT_SHAPES.items():
        A[n] = nc.dram_tensor(n, list(s), F32, kind="ExternalOutput").ap()
    if dbg_n:
        A["dbg"] = nc.dram_tensor("dbg", [dbg_n, 2112, 1024], F32, kind="ExternalOutput").ap()

    psf_t = k.es.enter_context(nc.psum_tensor("psf", [128, 3584], F32))
    banks = [T(psf_t, "bank%d" % i) for i in range(7)]
    psb = k.ps("psb", [128, 1024], BF16)
    bp = [0]

    def bank(n=1, parts=slice(0, 128)):
        p = bp[0]
        if n == 2:
            if p % 2:
                p += 1
            if p + 2 > 6:
                p = 0
        else:
            if p >= 7:
                p = 0
        bp[0] = p + n
        return V(psf_t[parts, p * 512:(p + n) * 512], banks[p:p + n])

    identf = k.sb("identf", [128, 128])
    identb = k.sb("identb", [128, 128], BF16)
    onesf = k.sb("onesf", [128, 512])
    bones = k.sb("bones", [128, 128])
    onesb = k.sb("onesb", [128, 128], BF16)
    bonesb = k.sb("bonesb", [128, 128], BF16)
    P = k.sb("P", [128, 1024])
    x32t = k.sb("x32", [128, 8 * 512])
    xbft = k.sb("xbf", [128, 8 * 512], BF16)
    vft = k.sb("vf", [128, 8 * 512], BF16)
    WB = [k.sb("wb%d" % i, [128, 512], BF16) for i in range(4)]
    ST32 = [k.sb("st32_%d" % i, [128, 1024]) for i in range(4)]
    STB = [k.sb("stb_%d" % i, [128, 1024], BF16) for i in range(4)]
    shiftP = [k.sb("shp%d" % i, [128, 8]) for i in range(2)]
    ZH = k.sb("zh", [128, 4 * NF * 2])
    ZHS = k.sb("zhs", [128, NF * 32])
    msk = {}
    for kind, C, blk in (("p", 64, 64), ("s", 4, 32)):
        for nm in ("su", "ui", "sl", "id"):
            msk[kind + nm] = k.sb("m_%s_%s" % (kind, nm), [128, C])
    rmask = {"p": k.sb("rm_p", [128, 1024]), "s": k.sb("rm_s", [128, 256])}
    AR = Arena(k, "ar", 123)

    k.memset("dve", onesf[:], 1.0)
    k.memset("dve", onesb[:], 1.0)
    k.memset("dve", bonesb[:], 0.0)
    k.memset("dve", bonesb[0:64, 0:64], 1.0)
    k.memset("dve", bonesb[64:128, 64:128], 1.0)
    k.memset("dve", bones[:], 0.0)
    k.memset("dve", bones[0:64, 0:64], 1.0)
    k.memset("dve", bones[64:128, 64:128], 1.0)
    k.emit("pool", lambda: nc.gpsimd.affine_select(out=identf.t[:], in_=onesf.t[:, 0:128], pattern=[[-1, 128]],
                                                   compare_op=ALU.is_equal, fill=0.0, base=0, channel_multiplier=1),
           [onesf[:]], [identf[:]])
    k.copy("dve", identb[:], identf[:])
    for kind, C, blk in (("p", 64, 64), ("s", 4, 32)):
        for nm, pat, cm, op, base in (("su", 1, -1, ALU.is_gt, 0), ("ui", 1, -1, ALU.is_ge, 0),
                                      ("sl", -1, 1, ALU.is_gt, 0), ("id", 1, -1, ALU.is_equal, 0)):
            m = msk[kind + nm]
            for b in range(128 // blk):
                k.emit("pool", lambda m=m, b=b, blk=blk, C=C, pat=pat, cm=cm, op=op: nc.gpsimd.affine_select(
                    out=m.t[b * blk:(b + 1) * blk, :], in_=onesf.t[b * blk:(b + 1) * blk, 0:C], pattern=[[pat, C]],
                    compare_op=op, fill=0.0, base=0, channel_multiplier=cm), [onesf[:]], [m[:]])
    k.memset("dve", rmask["p"][:], 1.0)
    k.memset("dve", rmask["p"][:].rr("p (n c) -> p n c", c=64)[:, :, 0:1], 0.0)
    k.memset("dve", rmask["s"][:], 1.0)
    k.memset("dve", rmask["s"][:].rr("p (n c) -> p n c", c=4)[:, :, 0:1], 0.0)
    for t_ in ST32:
        k.memset("dve", t_[:], 0.0)
    for t_ in STB:
        k.memset("dve", t_[:], 0.0)
    for t_ in shiftP:
        k.memset("dve", t_[:], 0.0)
    k.memset("dve", ZH[:], 0.0)

    plist = []

    def addp(name, ap2d, nrows):
        plist.append((name, ap2d, nrows))

    for l in range(2):
        addp("mix%d" % l, A["rw_mix"][l].rearrange("j (c p) -> (j c) p", p=128), 48)
        for nm, src in (("w0", "rw_w0"), ("a0", "rw_a0"), ("kk", "rw_k_k"), ("ka", "rw_k_a"),
                        ("lnxw", "rw_lnx_w"), ("lnxb", "rw_lnx_b")):
            addp("%s%d" % (nm, l), A[src][l].rearrange("(c p) -> c p", p=128), 8)
        addp("rk%d" % l, A["rw_r_k"][l].rearrange("(c a) b -> c (a b)", a=2), 8)
    addp("v0", A["rw_v0"][0].rearrange("(c p) -> c p", p=128), 8)
    addp("hgn", A["hg_norm_w"], 1)
    addp("lbp", A["hg_lb_param"].rearrange("l (c p) -> (l c) p", p=128), 32)
    addp("gkb", A["gl_gk_b"][0].rearrange("(c p) -> c p", p=128), 4)
    addp("gln", A["gl_norm_w"][0].rearrange("(c p) -> c p", p=128), 2)
    for l in range(4):
        addp("cw%d" % l, A["ffn_conv_w"][l].rearrange("j (c p) -> (j c) p", p=128), 66)
        addp("cb%d" % l, A["ffn_conv_b"][l].rearrange("(c p) -> c p", p=128), 22)
        for nm in ("ln1_w", "ln1_b", "ln2_w", "ln2_b"):
            addp("%s%d" % (nm, l), A[nm][l].rearrange("(c p) -> c p", p=128), 8)
    pcol = {}
    slot, row = 0, 0
    place = []
    for name, ap2d, nrows in plist:
        if row + nrows > 128:
            slot += 1
            row = 0
        pcol[name] = slot * 128 + row
        place.append((slot, row, ap2d, nrows))
        row += nrows
    nslots = slot + 1
    assert nslots * 128 + 64 <= 1024
    AR.reset()
    PR = AR.take(nslots * 512, F32, "p (s m) -> p s m", s=nslots)
    k.memset("dve", PR, 0.0)
    for (s_, r_, ap2d, nrows) in place:
        k.dma(PR[r_:r_ + nrows, s_, :], ap2d)
    for s_ in range(nslots):
        b = bank()
        k.tr(b[:, 0:128], PR[:, s_, :], identf[:])
        k.copy("dve", P[:, s_ * 128:(s_ + 1) * 128], b[:, 0:128])
    dcol = nslots * 128

    def pc(name, off=0, n=1):
        c = pcol[name] + off
        return P[:, c:c + n]

    for l in range(2):
        pcol["oka%d" % l] = dcol
        k.ts("dve", P[:, dcol:dcol + 8], pc("ka%d" % l, 0, 8), -1.0, MUL, 1.0, ADD)
        dcol += 8
    pcol["ngkb"] = dcol
    k.ts("dve", P[:, dcol:dcol + 4], pc("gkb", 0, 4), -1.0, MUL)
    dcol += 4
    pcol["lbe"] = dcol
    k.act(P[:, dcol:dcol + 32], pc("lbp", 0, 32), AF.Exp)
    lbe = P[:, dcol:dcol + 32]
    dcol += 32
    pcol["lb1"] = dcol
    pcol["olb1"] = dcol + 8
    lsum = P[:, dcol + 16:dcol + 24]
    k.tt("dve", lsum, lbe[:, 0:8], lbe[:, 8:16], ADD)
    k.tt("dve", lsum, lsum, lbe[:, 16:24], ADD)
    k.tt("dve", lsum, lsum, lbe[:, 24:32], ADD)
    k.recip(lsum, lsum)
    k.tt("dve", P[:, dcol:dcol + 8], lbe[:, 8:16], lsum, MUL)
    k.ts("dve", P[:, dcol + 8:dcol + 16], P[:, dcol:dcol + 8], -1.0, MUL, 1.0, ADD)
    dcol += 24
    assert dcol <= 1024

    x32 = x32t[:].rr("p (c w) -> p c w", c=8)
    xbf = xbft[:].rr("p (c w) -> p c w", c=8)
    vf = vft[:].rr("p (c w) -> p c w", c=8)

    wctr = [0]

    class WV:
        def __init__(self, ap, trk):
            self.ap = ap
            self.trk = trk

        def __getitem__(self, i):
            return WV(self.ap[i], self.trk)

        def rearrange(self, pat, **kw):
            return WV(self.ap.rearrange(pat, **kw), self.trk)

    class WTn:
        def __init__(self, ap, trks):
            self.ap = ap
            self.trks = trks

        def __getitem__(self, l):
            return WV(self.ap[l], self.trks[l])

    MATW = ["rw_wr", "rw_wk", "rw_wv", "rw_wo", "rw_w1", "rw_w2", "rw_a1", "rw_a2", "rw_g1", "rw_g2", "rw_v1", "rw_v2",
            "hg_wq", "hg_wf", "hg_wi", "hg_wg", "hg_wo", "gl_wq", "gl_wk", "gl_wv", "gl_wg", "gl_gk1", "gl_gk2", "gl_wo",
            "ffn_wu", "ffn_wg", "ffn_wd"]
    shadow = {}
    for n_ in MATW:
        sh = nc.dram_tensor(n_ + "_bf", list(WEIGHT_SHAPES[n_]), BF16, kind="Internal").ap()
        shadow[n_] = WTn(sh, [T(None, "%s_%d" % (n_, l_)) for l_ in range(WEIGHT_SHAPES[n_][0])])

    def cast_w(n_, l_):
        src = A[n_][l_]
        dst = shadow[n_].ap[l_]
        cols = WEIGHT_SHAPES[n_][2]
        if cols > 2048:
            src = src.rearrange("k (a b) -> (k a) b", a=4)
            dst = dst.rearrange("k (a b) -> (k a) b", a=4)
        k.dma(dst, src, extra_writes=[shadow[n_].trks[l_]], q="pool")

    RWN = ["rw_w1", "rw_a1", "rw_g1", "rw_wr", "rw_wk", "rw_wv", "rw_w2", "rw_a2", "rw_g2", "rw_wo"]
    FFN = ["ffn_wu", "ffn_wg", "ffn_wd"]
    for n_ in RWN:
        cast_w(n_, 0)
    for n_ in FFN:
        cast_w(n_, 0)
    for n_ in ["hg_wq", "hg_wf", "hg_wg", "hg_wi", "hg_wo"]:
        cast_w(n_, 0)
    for n_ in FFN:
        cast_w(n_, 1)
    for n_ in ["gl_wq", "gl_gk1", "gl_gk2", "gl_wk", "gl_wg", "gl_wv", "gl_wo"]:
        cast_w(n_, 0)
    for n_ in FFN:
        cast_w(n_, 2)
    for n_ in RWN:
        cast_w(n_, 1)
    for n_ in ["rw_v1", "rw_v2"]:
        cast_w(n_, 0)
    for n_ in FFN:
        cast_w(n_, 3)
    for n_ in MATW:
        A[n_] = shadow[n_]

    def wload(wv, p, kk, m):
        i = wctr[0]
        wctr[0] += 1
        assert kk * m <= 512
        wb = WB[i % 4][0:p, 0:kk * m].rr("p (k m) -> p k m", k=kk)
        k.dma(wb, wv.ap, extra_reads=[wv.trk])
        return wb

    def wmat(w2d, k0, c0, m):
        return wload(w2d[k0 * 128:(k0 + 1) * 128, c0:c0 + m].rearrange("p (o m) -> p o m", o=1), 128, 1, m)[:, 0, :]

    F32R = mybir.dt.float32r

    def mmr(out, lhsT, rhs, **kw):
        return k.mm(out, lhsT, rhs, **kw)

    def r32(v):
        return v

    evac_rr = [0]

    def evac(out, in_):
        evac_rr[0] += 1
        k.copy("act" if evac_rr[0] % 2 else "dve", out, in_)

    def proj_fm(w2d, nk, ncols, rhs_fn, W, consume, col0=0):
        c = 0
        while c < ncols:
            g = min(512, ncols - c)
            nm = g // 128
            bs = [bank() for _ in range(nm)]
            for kc in range(nk):
                wb = wmat(w2d, kc, col0 + c, g)
                for mi in range(nm):
                    k.mm(bs[mi][:, 0:W], wb[:, mi * 128:(mi + 1) * 128], rhs_fn(kc), start=(kc == 0), stop=(kc == nk - 1))
            for mi in range(nm):
                consume((c // 128) + mi, bs[mi][:, 0:W])
            c += g

    def ln_sublayer(h_chunks, W, wname, bname):
        Z, Zc = AR.buf(8, W, F32)
        sq = AR.take(W * 4)
        for c in range(8):
            k.stt(Zc[c], x32[:, c, 0:W], DN_ALPHA, h_chunks[c], MUL, ADD)
        mu = bank()
        for c in range(8):
            k.mm(mu[:, 0:W], onesf[:, 0:128], Zc[c], start=(c == 0), stop=(c == 7))
        for c in range(8):
            k.stt(Zc[c], mu[:, 0:W], -1.0 / D, Zc[c], MUL, ADD)
        var = bank()
        for c in range(8):
            sqb = sq.bitcast(BF16)[:, 0:W]
            k.act(sqb, Zc[c], AF.Square)
            k.mm(var[:, 0:W], onesb[:], sqb, start=(c == 0), stop=(c == 7))
        rstd = AR.take(W * 4)
        k.act(rstd, var[:, 0:W], AF.Sqrt, bias=LN_EPS, scale=1.0 / D)
        k.recip(rstd, rstd)
        for c in range(8):
            k.tt("dve", Zc[c], Zc[c], rstd, MUL)
            k.ts("dve", x32[:, c, 0:W], Zc[c], pc(wname, c), MUL, pc(bname, c), ADD)
            k.copy("act", xbf[:, c, 0:W], x32[:, c, 0:W])


    cst = k.sb("cst", [128, 8])
    k.memset("dve", cst[:, 0:1], LN_EPS)
    k.memset("dve", cst[:, 1:2], RMS_EPS)
    k.memset("dve", cst[:, 2:3], LNX_EPS)
    k.memset("dve", cst[:, 3:4], 1.0)
    k.memset("dve", cst[:, 4:5], 0.0)
    k.memset("dve", cst[:, 7:8], 1e-18)
    k.memset("dve", cst[:, 5:7], 0.0)
    k.memset("dve", cst[0:64, 5:6], 1.0)
    k.memset("dve", cst[64:128, 6:7], 1.0)

    class Tl:
        pass

    def acc_group(w2d, nk, col, g, rhs_fn, W, extra=None):
        nm = (g + 127) // 128
        bs = [bank() for _ in range(nm)]
        for kc in range(nk):
            wb = wmat(w2d, kc, col, g)
            for mi in range(nm):
                mw = min(128, g - mi * 128)
                k.mm(bs[mi][0:mw, 0:W], wb[:, mi * 128:mi * 128 + mw], rhs_fn(kc), start=(kc == 0), stop=(kc == nk - 1))
            if extra is not None:
                extra(kc, wb)
        return bs

    ln_state = {}

    def ln_begin(W):
        Z, Zc = AR.buf(8, W, F32)
        ln_state["Zc"] = Zc
        ln_state["W"] = W

    def ln_add(c, h_ps, col0=0):
        W = ln_state["W"]
        k.stt(ln_state["Zc"][c], x32[:, c, col0:col0 + W], DN_ALPHA, h_ps, MUL, ADD)

    def ln_finish(wname, bname, col0=0):
        W = ln_state["W"]
        Zc = ln_state["Zc"]
        sq = [AR.take(W * 2, BF16) for _ in range(2)]
        mean = AR.take(W * 4)
        rstd = AR.take(W * 4)
        mu = bank()
        ss = bank()
        for c in range(8):
            k.act(sq[c % 2], Zc[c], AF.Square)
            k.mm(mu[:, 0:W], onesf[:, 0:128], Zc[c], start=(c == 0), stop=(c == 7))
            k.mm(ss[:, 0:W], onesb[:], sq[c % 2], start=(c == 0), stop=(c == 7))
        k.act(mean, mu[:, 0:W], AF.Identity, scale=1.0 / D)
        k.act(rstd, mean, AF.Square)
        k.stt(rstd, ss[:, 0:W], 1.0 / D, rstd, MUL, SUB)
        k.act(rstd, rstd, AF.Ln, bias=cst[:, 0:1])
        k.act(rstd, rstd, AF.Exp, scale=-0.5)
        for c in range(8):
            k.tt("dve", Zc[c], Zc[c], mean, SUB)
            k.tt("dve", Zc[c], Zc[c], rstd, MUL)
            k.act(x32[:, c, col0:col0 + W], Zc[c], AF.Identity, bias=pc(bname, c), scale=pc(wname, c))
            k.act(xbf[:, c, col0:col0 + W], Zc[c], AF.Identity, bias=pc(bname, c), scale=pc(wname, c))

    def load_tile(tl):
        AR.reset()
        if tl.kind == "p":
            for j in range(4):
                XL = AR.take(4096)
                k.dma(XL, A["x_prompt"][tl.t0 + j * 128: tl.t0 + (j + 1) * 128, :])
                for hb in range(2):
                    b = bank()
                    for cc in range(4):
                        c = hb * 4 + cc
                        k.tr(b[:, cc * 128:(cc + 1) * 128], XL[:, c * 128:(c + 1) * 128], identf[:])
                    evac(x32[:, hb * 4:(hb + 1) * 4, j * 128:(j + 1) * 128], b[:, 0:512].rr("p (c w) -> p c w", c=4))
        else:
            XL = AR.take(4096)
            k.dma(XL[0:64, :], A["x_sample"][:, :])
            b = bank()
            for c in range(8):
                k.tr(b[:, c * 64:(c + 1) * 64], XL[0:64, c * 128:(c + 1) * 128], identf[0:64, 0:64])
            evac(x32[:, :, 0:64], b[:, 0:512].rr("p (c w) -> p c w", c=8))
        k.copy("act", xbf[:, :, 0:tl.W], x32[:, :, 0:tl.W])

    def store_rows(dram_rows, src_fn, n, ncol_chunks=8, is_output=True):
        YO = AR.take(ncol_chunks * 512)
        c = 0
        while c < ncol_chunks:
            g = min(4, ncol_chunks - c)
            b = bank()
            for cc in range(g):
                k.tr(b[0:n, cc * 128:(cc + 1) * 128], src_fn(c + cc), identf[:])
            evac(YO[0:n, c * 128:(c + g) * 128], b[0:n, 0:g * 128])
            c += g
        k.dma(dram_rows, YO[0:n, 0:ncol_chunks * 128], is_output=is_output)

    def store_tile(tl, dram, row0):
        AR.reset()
        if tl.kind == "p":
            for j in range(4):
                store_rows(dram[row0 + tl.t0 + j * 128: row0 + tl.t0 + (j + 1) * 128, :],
                           lambda c, j=j: x32[:, c, j * 128:(j + 1) * 128], 128)
        else:
            store_rows(dram[row0:row0 + 64, :], lambda c: x32[:, c, 0:64], 64)

    def ffn_sublayer(l, tl):
        W, nseq, L = tl.W, tl.nseq, tl.L
        AR.reset()
        MT, MTc = AR.buf(NF, W, BF16)
        zcs = [AR.take((W + 2 * nseq) * 4, F32, "p (s l) -> p s l", s=nseq) for _ in range(3)]
        ubs = [AR.take(W * 4, F32, "p (s l) -> p s l", s=nseq) for _ in range(3)]
        t1 = AR.take(W * 4, F32, "p (s l) -> p s l", s=nseq)
        s1 = AR.take(W * 4, F32, "p (s l) -> p s l", s=nseq)
        want_tm = (tl.kind == "s") or tl.last
        if want_tm:
            nsel = 64 if tl.kind == "s" else 2
            ZTM = AR.take(FF * 4)
            sel0 = 0 if tl.kind == "s" else W - 2
        if tl.kind == "s":
            ZR = AR.take(FF * 4)
            k.dma(ZR[0:32, :], A["state_ffn_conv"][l].rearrange("s j f -> (s j) f"))
            for f0 in range(0, NF, 16):
                g = min(16, NF - f0)
                b = bank()
                for ff in range(g):
                    k.tr(b[:, ff * 32:(ff + 1) * 32], ZR[0:32, (f0 + ff) * 128:(f0 + ff + 1) * 128], identf[0:32, 0:32])
                evac(ZHS[:, f0 * 32:(f0 + g) * 32], b[:, 0:g * 32])
        cw = pcol["cw%d" % l]
        cb = pcol["cb%d" % l]
        col = 0
        if tl.kind == "s":
            def a4(nb):
                return AR.take(nb, F32, "p (m s l) -> p m s l", m=4, s=16)
            ZC4, U4, T4, TM4, S4 = a4(4 * 16 * 6 * 4), a4(1024), a4(1024), a4(1024), a4(1024)
            while col < FF:
                g = min(512, FF - col)
                nm = g // 128
                f0 = col // 128
                bu, bz, ztb = bank(), bank(), bank()
                for (wname, bk, tm) in (("ffn_wu", bu, False), ("ffn_wg", bz, True)):
                    for kc in range(8):
                        wb = wmat(A[wname][l], kc, col, g)
                        for mi in range(nm):
                            k.mm(bk[:, mi * 64:(mi + 1) * 64], wb[:, mi * 128:(mi + 1) * 128], xbf[:, kc, 0:64],
                                 start=(kc == 0 and mi == 0), stop=(kc == 7), skip_group_check=True)
                        if tm:
                            k.mm(ztb[0:64, 0:g], xbf[:, kc, 0:64], wb, start=(kc == 0), stop=(kc == 7))
                evac(ZTM[0:64, col:col + g], ztb[0:64, 0:g])

                def v4(b_):
                    return b_[:, 0:nm * 64].rr("p (m s l) -> p m s l", m=nm, s=16)

                def pb(base):
                    return P[:, base + f0:base + f0 + nm].us(2).us(3).bc([128, nm, 16, 4])
                k.copy("act", ZC4[:, 0:nm, :, 2:6], v4(bz))
                k.copy("act", U4[:, 0:nm], v4(bu))
                k.copy("dve", ZC4[:, 0:nm, :, 0:2], ZHS[:, f0 * 32:(f0 + nm) * 32].rr("p (m s j) -> p m s j", m=nm, j=2))
                k.tt("dve", T4[:, 0:nm], ZC4[:, 0:nm, :, 2:6], pb(cw + 2 * NF), MUL)
                k.tt("dve", T4[:, 0:nm], T4[:, 0:nm], pb(cb), ADD)
                k.tt("dve", TM4[:, 0:nm], ZC4[:, 0:nm, :, 1:5], pb(cw + NF), MUL)
                k.tt("dve", T4[:, 0:nm], T4[:, 0:nm], TM4[:, 0:nm], ADD)
                k.tt("dve", TM4[:, 0:nm], ZC4[:, 0:nm, :, 0:4], pb(cw), MUL)
                k.tt("dve", T4[:, 0:nm], T4[:, 0:nm], TM4[:, 0:nm], ADD)
                k.act(S4[:, 0:nm], T4[:, 0:nm], AF.Silu)
                k.tt("dve", MT[:, f0:f0 + nm, :].rr("p m (s l) -> p m s l", s=16), S4[:, 0:nm], U4[:, 0:nm], MUL)
                col += g
        while col < FF:
            g = min(384, FF - col)
            bu = acc_group(A["ffn_wu"][l], 8, col, g, lambda kc: xbf[:, kc, 0:W], W)
            ztb = bank() if want_tm else None

            def extra(kc, wb):
                k.mm(ztb[0:nsel, 0:g], xbf[:, kc, sel0:sel0 + nsel], wb, start=(kc == 0), stop=(kc == 7))
            bz = acc_group(A["ffn_wg"][l], 8, col, g, lambda kc: xbf[:, kc, 0:W], W, extra if want_tm else None)
            if want_tm:
                evac(ZTM[0:nsel, col:col + g], ztb[0:nsel, 0:g])
            for mi in range(g // 128):
                k.copy("act", zcs[mi][:, :, 2:L + 2], bz[mi][:, 0:W].rr("p (s l) -> p s l", s=nseq))
                k.copy("act", ubs[mi], bu[mi][:, 0:W].rr("p (s l) -> p s l", s=nseq))
            for mi in range(g // 128):
                fi = col // 128 + mi
                zc = zcs[mi]
                if tl.kind == "p":
                    zh = ZH[:, (l * NF + fi) * 2:(l * NF + fi) * 2 + 2]
                    k.copy("dve", zc[:, 0, 0:2], zh)
                    k.copy("dve", zh, zc[:, 0, L:L + 2])
                else:
                    k.copy("dve", zc[:, :, 0:2], ZHS[:, fi * 32:(fi + 1) * 32].rr("p (s j) -> p s j", j=2))
                k.ts("dve", t1, zc[:, :, 2:L + 2], P[:, cw + 2 * NF + fi:cw + 2 * NF + fi + 1], MUL, P[:, cb + fi:cb + fi + 1], ADD)
                k.stt(t1, zc[:, :, 1:L + 1], P[:, cw + NF + fi:cw + NF + fi + 1], t1, MUL, ADD)
                k.stt(t1, zc[:, :, 0:L], P[:, cw + fi:cw + fi + 1], t1, MUL, ADD)
                k.act(s1, t1, AF.Silu)
                k.tt("dve", MTc[fi].rr("p (s l) -> p s l", s=nseq), s1, ubs[mi], MUL)
            col += g
        if want_tm:
            if tl.kind == "s":
                for j in range(2):
                    k.dma(A["s_conv"][l, :, j, :], ZTM[2 + j:64:4, :], is_output=True)
            else:
                k.dma(A["p_conv"][l], ZTM[0:2, :], is_output=True)
        ln_begin(W)
        for col in (0, 512):
            bs = acc_group(A["ffn_wd"][l], NF, col, 512, lambda kc: MTc[kc], W)
            for mi in range(4):
                ln_add(col // 128 + mi, bs[mi][:, 0:W])
        ln_finish("ln2_w%d" % l, "ln2_b%d" % l)

    def dbg_dump(idx, tl, row0):
        if not dbg_n or idx >= dbg_n:
            return
        store_tile(tl, A["dbg"][idx], row0)


    def groups_of(kind, Wm):
        gs = []
        if kind == "p":
            for g in range(Wm // 128):
                gs.append([(128 * g, 0, None), (128 * g + 64, 64, None)])
        else:
            for s0 in range(0, 16, 3):
                gs.append([(4 * (s0 + jj), 32 * jj, s0 + jj) for jj in range(min(3, 16 - s0))])
        return gs

    def to_tokmajor(dst, srcs, kind, g, ch):
        n = len(srcs)
        for i in range(n):
            if kind == "p":
                k.tr(psb[:, i * 128:(i + 1) * 128], srcs[i][:, 128 * g:128 * g + 128], identb[:])
            else:
                for (c0, pb, sidx) in ch:
                    k.tr(psb[pb:pb + 4, i * 128:(i + 1) * 128], srcs[i][:, c0:c0 + 4], identb[:])
        evac(dst[:, 0:n * 128], psb[:, 0:n * 128])

    def gla_like(l, tl, cfg):
        W, kind = tl.W, tl.kind
        H, VC = cfg["H"], cfg["VC"]
        hg = cfg["hg"]
        C = 64 if kind == "p" else 4
        nch = W // C
        ref = (C - 1) // 2
        scale = 128.0 ** -0.5
        AR.reset()
        QI, QIc = AR.buf(H, W, BF16)
        KI, KIc = AR.buf(H, W, BF16)
        QE, QEc = AR.buf(H, W, BF16)
        KD, KDc = AR.buf(H, W, BF16)
        GS, GSc = AR.buf(8, W, BF16)
        OT, OTc = AR.buf(8, W, F32)
        OG, OGc = AR.buf(8, W, BF16)
        ELt = AR.take(H * nch * 4, F32, "p (h n) -> p h n", h=H)
        tA, tB, tC, tD = [AR.take(W * 4) for _ in range(4)]
        grs = groups_of(kind, W)
        VT = [AR.take(2048, BF16) for _ in grs]
        KDT = [AR.take(H * 256, BF16) for _ in grs]
        ATsb = AR.take(H * C * 2, BF16, "p (h c) -> p h c", h=H)
        HK = AR.take(W * 2, BF16)
        FB, FBc = AR.buf(8 if hg else 4, W, F32)
        xr = lambda kc: xbf[:, kc, 0:W]

        for col in range(0, H * 128, 512):
            bs = acc_group(cfg["wq"], 8, col, 512, xr, W)
            for mi in range(4):
                h = col // 128 + mi
                if hg:
                    k.act(tA, bs[mi][:, 0:W], AF.Silu)
                    k.ts("dve", OGc[h], tA, scale, MUL)
                else:
                    k.act(OGc[h], bs[mi][:, 0:W], AF.Identity, scale=scale)

        def finalize(h, lg, kraw, qraw):
            b = tD
            k.scan(b, rmask[kind][:, 0:W], lg, 0.0, MUL, ADD)
            b3 = b.rr("p (n c) -> p n c", c=C)
            tA3 = tA.rr("p (n c) -> p n c", c=C)
            k.tt("dve", tA3, b3, b3[:, :, ref:ref + 1].bc([128, nch, C]), SUB)
            k.act(tB, tA, AF.Exp)
            k.tt("dve", QIc[h], qraw, tB, MUL)
            k.act(tB, tA, AF.Exp, scale=-1.0)
            k.tt("dve", KIc[h], kraw, tB, MUL)
            k.act(tB, b, AF.Exp)
            k.tt("dve", QEc[h], qraw, tB, MUL)
            k.tt("dve", tA3, b3[:, :, C - 1:C].bc([128, nch, C]), b3, SUB)
            k.act(tB, tA, AF.Exp)
            k.tt("dve", KDc[h], kraw, tB, MUL)
            k.act(ELt[:, h, :], b3[:, :, C - 1], AF.Exp)

        def gate_block(col):
            bs = acc_group(cfg["wg"], 8, col, 512, xr, W)
            for mi in range(4):
                k.act(GSc[col // 128 + mi], bs[mi][:, 0:W], AF.Silu)

        def v_block(col):
            vb = [bank() for _ in grs]
            for kc in range(8):
                wb = wmat(cfg["wv"], kc, col, 512)
                for g, ch in enumerate(grs):
                    if kind == "p":
                        k.mm(vb[g][:, 0:512], xbf[:, kc, 128 * g:128 * g + 128], wb, start=(kc == 0), stop=(kc == 7))
                    else:
                        for (c0, pb, sidx) in ch:
                            k.mm(vb[g][pb:pb + C, 0:512], xbf[:, kc, c0:c0 + C], wb, start=(kc == 0), stop=(kc == 7))
            for g in range(len(grs)):
                evac(VT[g][:, col:col + 512], vb[g][:, 0:512])

        if hg:
            for col in range(0, 1024, 512):
                bs = acc_group(cfg["wk"], 8, col, 512, xr, W)
                for mi in range(4):
                    k.act(FBc[col // 128 + mi], bs[mi][:, 0:W], AF.Sigmoid)
            gate_block(0)
            gate_block(512)
            v_block(0)
            v_block(512)
            for h in range(8):
                k.ts("dve", tA, FBc[h], pc("olb1", h), MUL, pc("lb1", h), ADD)
                k.act(tB, tA, AF.Ln)
                k.ts("dve", tC, tA, -1.0, MUL, 1.0, ADD)
                finalize(h, tB, tC, OGc[h])
        else:
            g1 = wload(A["gl_gk1"][0].rearrange("(k p) m -> p k m", p=128), 128, 8, 16)
            b = bank()
            for kc in range(8):
                k.mm(b[0:16, 0:W], g1[:, kc, :], xr(kc), start=(kc == 0), stop=(kc == 7))
            k.copy("act", HK[0:16, :], b[0:16, 0:W])
            g2 = wload(A["gl_gk2"][0].rearrange("p (o m) -> p o m", o=1), 16, 1, 512)[:, 0, :]
            LG, LGc = AR.buf(4, W, F32)
            for h in range(4):
                b = bank()
                k.mm(b[:, 0:W], g2[:, h * 128:(h + 1) * 128], HK[0:16, :])
                k.act(tA, b[:, 0:W], AF.Exp, bias=pc("ngkb", h), scale=-1.0)
                k.act(tB, tA, AF.Ln, bias=cst[:, 3:4])
                k.ts("dve", LGc[h], tB, -1.0 / 16.0, MUL)
            bs = acc_group(cfg["wk"], 8, 0, 512, xr, W)
            for h in range(4):
                k.copy("act", FBc[h], bs[h][:, 0:W])
            gate_block(0)
            gate_block(512)
            v_block(0)
            v_block(512)
            for h in range(4):
                finalize(h, LGc[h], FBc[h], OGc[h])

        for g in range(len(grs)):
            to_tokmajor(KDT[g], KDc, kind, g, grs[g])

        hv = H * VC * 128
        st_dram_in, st_dram_out_p, st_dram_out_s = cfg["st_in"], cfg["st_out_p"], cfg["st_out_s"]

        def st_view(i):
            return (ST32[i][:, 0:hv].rr("p (h v) -> p h v", h=H), STB[i][:, 0:hv].rr("p (h v) -> p h v", h=H))

        for g, ch in enumerate(grs):
            if kind == "s":
                for (c0, pb, sidx) in ch:
                    S32, Sbf = st_view(sidx % 4)
                    k.dma(S32, st_dram_in[sidx].rearrange("h k v -> k h v"))
                    k.copy("act", Sbf, S32)
            atp = bank()
            for (c0, pb, sidx) in ch:
                for h in range(H):
                    k.mm(atp[pb:pb + C, h * C:(h + 1) * C], KIc[h][:, c0:c0 + C], QIc[h][:, c0:c0 + C])
            k.tt("dve", ATsb, atp[:, 0:H * C].rr("p (h c) -> p h c", h=H), msk[kind + "ui"][:].us(1).bc([128, H, C]), MUL)
            for ci, (c0, pb, sidx) in enumerate(ch):
                n = c0 // C
                if kind == "p":
                    S32, Sbf = st_view(cfg["st"])
                else:
                    S32, Sbf = st_view(sidx % 4)
                ops = bank()
                for h in range(H):
                    for jv in range(VC):
                        cc = h * VC + jv
                        k.mm(ops[:, cc * C:(cc + 1) * C], VT[g][pb:pb + C, cc * 128:(cc + 1) * 128], ATsb[pb:pb + C, h, :],
                             start=(cc == 0), stop=False, skip_group_check=True)
                for h in range(H):
                    for jv in range(VC):
                        cc = h * VC + jv
                        k.mm(ops[:, cc * C:(cc + 1) * C], Sbf[:, h, jv * 128:(jv + 1) * 128], QEc[h][:, c0:c0 + C],
                             start=False, stop=True, skip_group_check=True)
                evac(OT[:, :, c0:c0 + C], ops[:, 0:8 * C].rr("p (c w) -> p c w", c=8))
                sps = bank(2)
                for h in range(H):
                    k.mm(sps[:, h * VC * 128:(h + 1) * VC * 128], KDT[g][pb:pb + C, h * 128:(h + 1) * 128],
                         VT[g][pb:pb + C, h * VC * 128:(h + 1) * VC * 128])
                for h in range(H):
                    k.stt(S32[:, h, :], S32[:, h, :], ELt[:, h, n:n + 1], sps[:, h * VC * 128:(h + 1) * VC * 128], MUL, ADD)
                if kind == "p":
                    k.copy("act", Sbf, S32)
                if kind == "s":
                    k.dma(st_dram_out_s[sidx].rearrange("h k v -> k h v"), S32, is_output=True)
                elif tl.last and g == len(grs) - 1 and ci == len(ch) - 1:
                    k.dma(st_dram_out_p.rearrange("h k v -> k h v"), S32, is_output=True)

        for h in range(H):
            ms = bank()
            for jv in range(VC):
                tAb = tA.bitcast(BF16)[:, 0:W]
                k.act(tAb, OTc[h * VC + jv], AF.Square)
                k.mm(ms[:, 0:W], onesb[:], tAb, start=(jv == 0), stop=(jv == VC - 1))
            k.act(tB, ms[:, 0:W], AF.Sqrt, bias=cst[:, 1:2], scale=1.0 / (VC * 128))
            k.recip(tB, tB)
            for jv in range(VC):
                cc = h * VC + jv
                k.stt(tC, OTc[cc], pc(cfg["norm"], jv), tB, MUL, MUL)
                k.tt("dve", OGc[cc], tC, GSc[cc], MUL)
        AR.reset(0)
        ln_begin(W)
        for col in (0, 512):
            bs = acc_group(cfg["wo"], 8, col, 512, lambda kc: OGc[kc], W)
            for mi in range(4):
                ln_add(col // 128 + mi, bs[mi][:, 0:W])
        ln_finish("ln1_w%d" % l, "ln1_b%d" % l)

    HGCFG = dict(hg=True, H=8, VC=1, wq=A["hg_wq"][0], wk=A["hg_wf"][0], wv=A["hg_wi"][0], wg=A["hg_wg"][0],
                 wo=A["hg_wo"][0], norm="hgn", st=1, st_in=A["state_hgrn"], st_out_p=A["p_hgrn"], st_out_s=A["s_hgrn"])
    GLCFG = dict(hg=False, H=4, VC=2, wq=A["gl_wq"][0], wk=A["gl_wk"][0], wv=A["gl_wv"][0], wg=A["gl_wg"][0],
                 wo=A["gl_wo"][0], norm="gln", st=2, st_in=A["state_gla"], st_out_p=A["p_gla"], st_out_s=A["s_gla"])


    def rwkv_mixer(l, j, tl, col0, Wm):
        kind = tl.kind
        C = 64 if kind == "p" else 4
        nch = Wm // C
        AR.reset()
        RT, RTc = AR.buf(8, Wm, BF16)
        KH, KHc = AR.buf(8, Wm, BF16)
        BH, BHc = AR.buf(8, Wm, BF16)
        KT, KTc = AR.buf(8, Wm, BF16)
        VB, VBc = AR.buf(8, Wm, BF16)
        G, Gc = AR.buf(8, Wm, BF16)
        BV, BVc = AR.buf(8, Wm, F32)
        PCt = AR.take(8 * nch * 4, F32, "p (c n) -> p c n", c=8)
        mark = AR.p
        XX, XXc = AR.buf(8, Wm, BF16)
        XR, XRc = AR.buf(8, Wm, BF16)
        XK, XKc = AR.buf(8, Wm, BF16)
        XV, XVc = AR.buf(8, Wm, BF16)
        XT, XTc = AR.buf(8, Wm, BF16)
        HW = AR.take(Wm * 2, BF16)
        HA = AR.take(Wm * 2, BF16)
        HG0 = AR.take(Wm * 2, BF16)
        HG1 = AR.take(Wm * 2, BF16)
        HV = AR.take(Wm * 2, BF16)
        RAWr, RAWrc = AR.buf(4, Wm, F32)
        RAWk, RAWkc = AR.buf(4, Wm, F32)
        RAWv, RAWvc = AR.buf(4, Wm, F32)
        t = [AR.take(Wm * 16) for _ in range(8)]
        xv_ = x32[:, :, col0:col0 + Wm]

        if kind == "p":
            k.tt("dve", XX[:, :, 1:Wm], xv_[:, :, 0:Wm - 1], xv_[:, :, 1:Wm], SUB)
            k.tt("dve", XX[:, :, 0], shiftP[j][:, :], xv_[:, :, 0], SUB)
            k.copy("dve", shiftP[j][:, :], xv_[:, :, Wm - 1])
            if tl.last and col0 + Wm == tl.W:
                store_rows(A["p_shift"][j:j + 1, :], lambda c: shiftP[j][:, c:c + 1], 1)
        else:
            SR = AR.take(4096)
            shS = AR.take(512, F32, "p (c s) -> p c s", c=8)
            k.dma(SR[0:16, :], A["state_rwkv_shift"][j])
            b = bank()
            for c in range(8):
                k.tr(b[:, c * 16:(c + 1) * 16], SR[0:16, c * 128:(c + 1) * 128], identf[0:16, 0:16])
            evac(shS, b[:, 0:128].rr("p (c s) -> p c s", c=8))
            x4 = xv_.rr("p c (s t) -> p c s t", t=4)
            XX4 = XX.rr("p c (s t) -> p c s t", t=4)
            for c in range(8):
                k.tt("dve", XX4[:, c, :, 1:4], x4[:, c, :, 0:3], x4[:, c, :, 1:4], SUB)
            k.tt("dve", XX4[:, :, :, 0], shS, x4[:, :, :, 0], SUB)
            store_rows(A["s_shift"][j], lambda c: x32[:, c, 3:64:4], 16)
        mc = pcol["mix%d" % j]

        def mix(dst_c, jj):
            for c in range(8):
                k.stt(dst_c[c], XXc[c], P[:, mc + jj * 8 + c:mc + jj * 8 + c + 1], x32[:, c, col0:col0 + Wm], MUL, ADD)

        def lora1(w3d, m, dst, pb, func, src_c):
            wb = wload(w3d, 128, 8, m)
            b = bank()
            for kc in range(8):
                k.mm(b[pb:pb + m, 0:Wm], wb[:, kc, :], src_c[kc], start=(kc == 0), stop=(kc == 7))
            k.act(dst[pb:pb + m, :], b[pb:pb + m, 0:Wm], func)

        mix(XTc, 1)
        lora1(A["rw_w1"][j].rearrange("(k p) m -> p k m", p=128), 64, HW, 0, AF.Tanh, XTc)
        mix(XTc, 4)
        lora1(A["rw_a1"][j].rearrange("(k p) m -> p k m", p=128), 64, HA, 0, AF.Copy, XTc)
        mix(XTc, 5)
        g1v = A["rw_g1"][j].rearrange("(k p) m -> p k m", p=128)
        lora1(g1v[:, :, 0:64], 64, HG0, 0, AF.Sigmoid, XTc)
        lora1(g1v[:, :, 64:128], 64, HG0, 64, AF.Sigmoid, XTc)
        lora1(g1v[:, :, 128:160], 32, HG1, 0, AF.Sigmoid, XTc)
        mix(XRc, 0)
        mix(XKc, 2)
        mix(XVc, 3)
        if j > 0:
            lora1(A["rw_v1"][0].rearrange("(k p) m -> p k m", p=128), 32, HV, 0, AF.Copy, XVc)

        def wrow(w2d, r0, r1, col):
            return wload(w2d[r0:r1, col:col + 512].rearrange("p (o m) -> p o m", o=1), r1 - r0, 1, 512)[:, 0, :]

        def g4(v):
            return v.rr("p (c w) -> p c w", c=4)

        def pb4(name, c0):
            cc = pcol[name] + c0
            return P[:, cc:cc + 4].us(2).bc([128, 4, Wm])

        def reg(d, mi):
            return d[:, mi * Wm:(mi + 1) * Wm]

        def st(mi):
            return (mi * Wm) % 512 == 0

        W4 = 4 * Wm
        for gq in range(2):
            col = gq * 512
            c4 = gq * 4
            for (wn, xs, raw) in (("rw_wr", XRc, RAWr), ("rw_wk", XKc, RAWk), ("rw_wv", XVc, RAWv)):
                d = bank(2)
                for kc in range(8):
                    wb = wmat(A[wn][j], kc, col, 512)
                    for mi in range(4):
                        k.mm(reg(d, mi), wb[:, mi * 128:(mi + 1) * 128], xs[kc], start=(kc == 0 and st(mi)), stop=(kc == 7),
                             skip_group_check=True)
                evac(raw, g4(d[:, 0:W4]))
            r4, k4, v4 = RAWr, RAWk, RAWv
            T0, T1, T2, T3, T4, T5, T6, T7 = [g4(x) for x in t]
            d = bank(2)
            wb = wrow(A["rw_w2"][j], 0, 64, col)
            for mi in range(4):
                k.mm(reg(d, mi), wb[:, mi * 128:(mi + 1) * 128], HW[0:64, :], start=st(mi), stop=True, skip_group_check=True)
            k.tt("dve", T0, g4(d[:, 0:W4]), pb4("w0%d" % j, c4), ADD)
            k.act(T0, T0, AF.Sigmoid)
            d = bank(2)
            wb = wrow(A["rw_a2"][j], 0, 64, col)
            for mi in range(4):
                k.mm(reg(d, mi), wb[:, mi * 128:(mi + 1) * 128], HA[0:64, :], start=st(mi), stop=True, skip_group_check=True)
            k.tt("dve", T1, g4(d[:, 0:W4]), pb4("a0%d" % j, c4), ADD)
            k.act(T1, T1, AF.Sigmoid)
            if j == 0:
                k.copy("act", vf[:, c4:c4 + 4, col0:col0 + Wm], v4)
            else:
                d = bank(2)
                wb = wrow(A["rw_v2"][0], 0, 32, col)
                for mi in range(4):
                    k.mm(reg(d, mi), wb[:, mi * 128:(mi + 1) * 128], HV[0:32, :], start=st(mi), stop=True, skip_group_check=True)
                k.tt("dve", T2, g4(d[:, 0:W4]), pb4("v0", c4), ADD)
                k.act(T2, T2, AF.Sigmoid)
                k.tt("dve", T3, vf[:, c4:c4 + 4, col0:col0 + Wm], v4, SUB)
                k.tt("dve", T3, T3, T2, MUL)
                k.tt("dve", v4, v4, T3, ADD)
            k.copy("act", VB[:, c4:c4 + 4, :], v4)
            d = bank(2)
            wb = wrow(A["rw_g2"][j], 0, 128, col)
            for mi in range(4):
                k.mm(reg(d, mi), wb[:, mi * 128:(mi + 1) * 128], HG0[:, :], start=st(mi), stop=False, skip_group_check=True)
            wb = wrow(A["rw_g2"][j], 128, 160, col)
            for mi in range(4):
                k.mm(reg(d, mi), wb[:, mi * 128:(mi + 1) * 128], HG1[0:32, :], start=False, stop=True, skip_group_check=True)
            k.copy("act", G[:, c4:c4 + 4, :], g4(d[:, 0:W4]))
            k.tt("dve", T2, k4, pb4("kk%d" % j, c4), MUL)
            sb = t[7].bitcast(BF16)[:, 0:W4]
            k.act(g4(sb), T2, AF.Square)
            d = bank(2)
            for mi in range(4):
                k.mm(reg(d, mi), bonesb[:], sb[:, mi * Wm:(mi + 1) * Wm], start=st(mi), stop=True, skip_group_check=True)
            k.act(T3, g4(d[:, 0:W4]), AF.Ln, bias=cst[:, 7:8])
            k.act(T3, T3, AF.Exp, scale=-0.5)
            k.tt("dve", T2, T2, T3, MUL)
            k.tt("dve", T3, T1, pb4("ka%d" % j, c4), MUL)
            k.tt("dve", T3, T3, pb4("oka%d" % j, c4), ADD)
            k.tt("dve", T4, k4, T3, MUL)
            k.tt("dve", T5, T2, T1, MUL)
            k.scan(t[6], rmask[kind][:, 0:W4], t[0], 0.0, MUL, ADD)
            k.act(T7, T6, AF.Exp, scale=-C0)
            k.tt("dve", RT[:, c4:c4 + 4, :], r4, T7, MUL)
            k.copy("dve", PCt[:, c4:c4 + 4, :], T7.rr("p c (n q) -> p c n q", q=C)[:, :, :, C - 1])
            k.tt("dve", T3, T6, T0, SUB)
            k.act(T3, T3, AF.Exp, scale=-C0)
            k.tt("dve", KT[:, c4:c4 + 4, :], T2, T3, MUL)
            k.act(T7, T6, AF.Exp, scale=C0)
            k.tt("dve", KH[:, c4:c4 + 4, :], T4, T7, MUL)
            k.tt("dve", BH[:, c4:c4 + 4, :], T5, T7, MUL)
            k.tt("dve", T3, r4, pb4("rk%d" % j, c4), MUL)
            sb = t[7].bitcast(BF16)[:, 0:W4]
            k.tt("dve", g4(sb), T3, T4, MUL)
            d = bank(2)
            for mi in range(4):
                k.mm(reg(d, mi), bonesb[:], sb[:, mi * Wm:(mi + 1) * Wm], start=st(mi), stop=True, skip_group_check=True)
            k.tt("dve", BV[:, c4:c4 + 4, :], g4(d[:, 0:W4]), v4, MUL)

        STOP = 9
        if STOP <= 1:
            return
        AR.reset(mark)
        grs = groups_of(kind, Wm)
        VT = [AR.take(2048, BF16) for _ in grs]
        KHT = [AR.take(2048, BF16) for _ in grs]
        BHT = [AR.take(2048, BF16) for _ in grs]
        n_am = len(grs) if kind == "p" else 1
        ams = [[AR.take(16 * C * 2, BF16, "p (h c) -> p h c", h=16) for _ in range(8)] for _ in range(n_am)]
        RHSsb = AR.take(2048, BF16)
        Usb = AR.take(2048, BF16)
        YT, YTc = AR.buf(8, Wm, F32)
        YG, YGc = AR.buf(8, Wm, BF16)
        Slds = [AR.take(4096), AR.take(4096)] if kind == "s" else [None, None]
        Sst = AR.take(4096)
        for g in range(len(grs)):
            to_tokmajor(VT[g], VBc, kind, g, grs[g])
            to_tokmajor(KHT[g], KHc, kind, g, grs[g])
            to_tokmajor(BHT[g], BHc, kind, g, grs[g])

        if STOP <= 2:
            return

        def hd(h):
            return h // 2, (h % 2) * 64

        def hs(h):
            return (h % 2) * 8 + h // 2

        def v3(d):
            return d[:, 0:1024].rr("p (h c) -> p h c", h=16)[:, :, 0:C]

        def st_view(i):
            return (ST32[i][:, 0:512].rr("p (c v) -> p c v", c=8),
                    (STB[i][:, 0:512].rr("p (c v) -> p c v", c=8), STB[i][:, 512:1024].rr("p (c v) -> p c v", c=8)))

        def mask_state(H32, Hm):
            k.ts("dve", Hm[0], H32, cst[:, 5:6], MUL)
            k.ts("dve", Hm[1], H32, cst[:, 6:7], MUL)

        def phase1(items):
            def amat(ch, dst, lh, rh, mk):
                d = bank(2)
                for (c0, pb, sidx) in ch:
                    for h in range(16):
                        c, hb = hd(h)
                        k.mm(d[pb:pb + C, hs(h) * 64:hs(h) * 64 + C], lh[c][hb:hb + 64, c0:c0 + C], rh[c][hb:hb + 64, c0:c0 + C])
                k.tt("dve", dst, v3(d), msk[kind + mk][:].us(1).bc([128, 16, C]), MUL)

            def mm3(ch, lh, rh):
                d = bank(2)
                for (c0, pb, sidx) in ch:
                    for h in range(16):
                        k.mm(d[pb:pb + C, h * 64:h * 64 + C], lh[pb:pb + C, h, :], rh[pb:pb + C, h, :])
                return d

            for it in items:
                Msb, Nsb, Xs, M2, N2, AkkT, ArkT, ArbT = it["am"]
                ch = it["ch"]
                amat(ch, Msb, BHc, KTc, "su")
                amat(ch, Nsb, KTc, BHc, "sl")
                amat(ch, AkkT, KHc, KTc, "su")
                amat(ch, ArkT, KHc, RTc, "ui")
                amat(ch, ArbT, BHc, RTc, "ui")
                k.stt(Xs, Msb, -1.0, msk[kind + "id"][:].us(1).bc([128, 16, C]), MUL, ADD)
                it["p"] = [Msb, Nsb, M2, N2]
            p_ = 2
            while p_ < C:
                lastlv = (p_ * 2 >= C)
                for it in items:
                    Mp, Nn, Mo, No = it["p"]
                    d = mm3(it["ch"], Mp, Nn)
                    evac(No, v3(d))
                if not lastlv:
                    for it in items:
                        Mp, Nn, Mo, No = it["p"]
                        d = mm3(it["ch"], Nn, Mp)
                        evac(Mo, v3(d))
                for it in items:
                    Mp, Nn, Mo, No = it["p"]
                    Xs = it["am"][2]
                    d = mm3(it["ch"], No, Xs)
                    k.tt("dve", Xs, v3(d), Xs, ADD)
                    it["p"] = [Mo, No, Mp, Nn]
                p_ *= 2

        if kind == "p":
            items_all = [dict(ch=ch, am=ams[g]) for g, ch in enumerate(grs)]
            phase1(items_all)
        for g, ch in enumerate(grs):
            if kind == "p":
                it = items_all[g]
            else:
                it = dict(ch=ch, am=ams[0])
                phase1([it])
            Msb, Nsb, Xs, M2, N2, AkkT, ArkT, ArbT = it["am"]
            if kind == "s":
                for (c0, pb, sidx) in ch:
                    H32, Hbf = st_view(sidx % 4)
                    Sl = Slds[sidx % 2]
                    k.dma(Sl[0:64, :].rr("p (h k) -> p h k", h=16), A["state_rwkv"][j, sidx].rearrange("h v k -> v h k"))
                    b = bank()
                    for c in range(8):
                        k.tr(b[:, c * 64:(c + 1) * 64], Sl[0:64, c * 128:(c + 1) * 128], identf[0:64, 0:64])
                    evac(H32, b[:, 0:512].rr("p (c v) -> p c v", c=8))
                    mask_state(H32, Hbf)
            for ci, (c0, pb, sidx) in enumerate(ch):
                n = c0 // C
                if kind == "p":
                    H32, Hbf = st_view(0 if j == 0 else 3)
                else:
                    H32, Hbf = st_view(sidx % 4)
                d = bank(2)
                for h in range(16):
                    c, hb = hd(h)
                    k.mm(d[pb:pb + C, h * 64:(h + 1) * 64], KTc[c][:, c0:c0 + C], Hbf[h % 2][:, c, :],
                         start=(h % 8 == 0), stop=False, skip_group_check=True)
                for h in range(16):
                    k.mm(d[pb:pb + C, h * 64:(h + 1) * 64], AkkT[pb:pb + C, hs(h), :], VT[g][pb:pb + C, h * 64:(h + 1) * 64],
                         start=False, stop=True, skip_group_check=True)
                k.copy("act", RHSsb[pb:pb + C, :], d[pb:pb + C, 0:1024])
                d = bank(2)
                for h in range(16):
                    k.mm(d[pb:pb + C, h * 64:(h + 1) * 64], Xs[pb:pb + C, hs(h), :], RHSsb[pb:pb + C, h * 64:(h + 1) * 64])
                k.act(Usb[pb:pb + C, :], d[pb:pb + C, 0:1024], AF.Identity, scale=-1.0)
                yps = bank()
                for h in range(16):
                    c, hb = hd(h)
                    k.mm(yps[hb:hb + 64, c * C:(c + 1) * C], Hbf[h % 2][:, c, :], RTc[c][:, c0:c0 + C],
                         start=(h < 2), stop=False, skip_group_check=True)
                for h in range(16):
                    c, hb = hd(h)
                    o = yps[hb:hb + 64, c * C:(c + 1) * C]
                    k.mm(o, VT[g][pb:pb + C, h * 64:(h + 1) * 64], ArkT[pb:pb + C, hs(h), :], start=False, stop=False, skip_group_check=True)
                    k.mm(o, Usb[pb:pb + C, h * 64:(h + 1) * 64], ArbT[pb:pb + C, hs(h), :], start=False, stop=True, skip_group_check=True)
                evac(YT[:, :, c0:c0 + C], yps[:, 0:8 * C].rr("p (c w) -> p c w", c=8))
                hps = bank()
                for h in range(16):
                    c, hb = hd(h)
                    o = hps[hb:hb + 64, c * 64:(c + 1) * 64]
                    k.mm(o, KHT[g][pb:pb + C, c * 128 + hb:c * 128 + hb + 64], VT[g][pb:pb + C, h * 64:(h + 1) * 64], start=True, stop=False)
                    k.mm(o, BHT[g][pb:pb + C, c * 128 + hb:c * 128 + hb + 64], Usb[pb:pb + C, h * 64:(h + 1) * 64], start=False, stop=True)
                k.tt("dve", H32, hps[:, 0:512].rr("p (c v) -> p c v", c=8), H32, ADD)
                k.tt("dve", H32, H32, PCt[:, :, n:n + 1].bc([128, 8, 64]), MUL)
                if kind == "p":
                    mask_state(H32, Hbf)
                fin_p = (kind == "p" and tl.last and col0 + Wm == tl.W and g == len(grs) - 1 and ci == len(ch) - 1)
                if kind == "s" or fin_p:
                    dram = A["s_rwkv"][j, sidx] if kind == "s" else A["p_rwkv"][j]
                    d = bank(2)
                    for c in range(8):
                        k.tr(d[0:64, c * 128:(c + 1) * 128], H32[:, c, :], identf[:])
                    evac(Sst[0:64, :], d[0:64, 0:1024])
                    k.dma(dram.rearrange("h v k -> v h k"), Sst[0:64, :].rr("p (h k) -> p h k", h=16), is_output=True)

        if STOP <= 5:
            return
        tq0 = AR.take(Wm * 16)
        tq1 = AR.take(Wm * 16)
        Q0, Q1 = g4(tq0), g4(tq1)
        for gq in range(2):
            c4 = gq * 4
            y4 = YT[:, c4:c4 + 4, :]
            d = bank(2)
            for mi in range(4):
                k.mm(reg(d, mi), bones[:], YTc[c4 + mi], start=st(mi), stop=True, skip_group_check=True)
            k.stt(Q0, g4(d[:, 0:W4]), -1.0 / 64, y4, MUL, ADD)
            qb = tq1.bitcast(BF16)[:, 0:W4]
            k.act(g4(qb), Q0, AF.Square)
            d = bank(2)
            for mi in range(4):
                k.mm(reg(d, mi), bonesb[:], qb[:, mi * Wm:(mi + 1) * Wm], start=st(mi), stop=True, skip_group_check=True)
            k.act(Q1, g4(d[:, 0:W4]), AF.Ln, bias=cst[:, 2:3], scale=1.0 / 64)
            k.act(Q1, Q1, AF.Exp, scale=-0.5)
            k.tt("dve", Q0, Q0, Q1, MUL)
            k.tt("dve", Q0, Q0, pb4("lnxw%d" % j, c4), MUL)
            k.tt("dve", Q0, Q0, pb4("lnxb%d" % j, c4), ADD)
            k.tt("dve", Q0, Q0, BV[:, c4:c4 + 4, :], ADD)
            k.tt("dve", YG[:, c4:c4 + 4, :], Q0, G[:, c4:c4 + 4, :], MUL)
        ln_begin(Wm)
        for col in (0, 512):
            bs = acc_group(A["rw_wo"][j], 8, col, 512, lambda kc: YGc[kc], Wm)
            for mi in range(4):
                ln_add(col // 128 + mi, bs[mi][:, 0:Wm], col0)
        ln_finish("ln1_w%d" % l, "ln1_b%d" % l, col0)

    def mk_tiles():
        tiles = []
        for ti in range(4):
            tl = Tl()
            tl.kind, tl.W, tl.nseq, tl.L, tl.t0, tl.last = "p", 512, 1, 512, ti * 512, (ti == 3)
            tiles.append(tl)
        tl = Tl()
        tl.kind, tl.W, tl.nseq, tl.L, tl.t0, tl.last = "s", 64, 16, 4, 0, True
        tiles.append(tl)
        return tiles

    def run_all(layers=(0, 1, 2, 3), tiles=None):
        for tl in (tiles or mk_tiles()):
            load_tile(tl)
            row0 = tl.t0 if tl.kind == "p" else 2048
            for l in layers:
                if l % 3 == 0:
                    if tl.kind == "p":
                        for half in (0, 256):
                            rwkv_mixer(l, l // 3, tl, half, 256)
                    else:
                        rwkv_mixer(l, l // 3, tl, 0, 64)
                elif l == 1:
                    gla_like(l, tl, HGCFG)
                else:
                    gla_like(l, tl, GLCFG)
                dbg_dump2(2 * l, tl, row0)
                ffn_sublayer(l, tl)
                dbg_dump2(2 * l + 1, tl, row0)
            if tl.kind == "p":
                store_tile(tl, A["y_prompt"], 0)
            else:
                store_tile(tl, A["y_sample"], 0)

    def dbg_dump2(idx, tl, row0):
        if not dbg_n or idx >= dbg_n:
            return
        AR.reset()
        if tl.kind == "p":
            for jj in range(4):
                store_rows(A["dbg"][idx][row0 + jj * 128: row0 + (jj + 1) * 128, :],
                           lambda c, jj=jj: x32[:, c, jj * 128:(jj + 1) * 128], 128)
        else:
            store_rows(A["dbg"][idx][2048:2112, :], lambda c: x32[:, c, 0:64], 64)

    return k, A, locals()


_NC_CACHE = {}


def _get_nc():
    if "nc" not in _NC_CACHE:
        nc = bass.Bass("TRN2", target_bir_lowering=False)
        k, A, L = build(nc, dbg_n=0)
        L["run_all"]()
        k.finish()
        k.close()
        _NC_CACHE["nc"] = nc
    return _NC_CACHE["nc"]


def kernel(**inputs):
    n = 8
    f32 = np.float32
    inp = {kk: np.asarray(v) for kk, v in inputs.items()}
    in_maps = []
    for c in range(n):
        s = slice(16 * c, 16 * (c + 1))
        m = {}
        m["x_prompt"] = np.ascontiguousarray(inp["x_prompt"][c], dtype=f32)
        m["x_sample"] = np.ascontiguousarray(inp["x_sample"][s], dtype=f32).reshape(64, 1024)
        m["state_rwkv"] = np.ascontiguousarray(inp["state_rwkv"][:, s], dtype=f32)
        m["state_rwkv_shift"] = np.ascontiguousarray(inp["state_rwkv_shift"][:, s], dtype=f32)
        m["state_hgrn"] = np.ascontiguousarray(inp["state_hgrn"][0, s], dtype=f32)
        m["state_gla"] = np.ascontiguousarray(inp["state_gla"][0, s], dtype=f32)
        m["state_ffn_conv"] = np.ascontiguousarray(inp["state_ffn_conv"][:, s], dtype=f32)
        for w in WEIGHT_SHAPES:
            m[w] = np.ascontiguousarray(inp[w], dtype=f32)
        in_maps.append(m)
    nc = _get_nc()
    res = run_bass_kernel_spmd(nc, in_maps, core_ids=list(range(n)))
    R = res.results
    y_prompt = np.stack([R[c]["y_prompt"] for c in range(n)], 0)
    y_sample = np.concatenate([R[c]["y_sample"].reshape(16, 4, 1024) for c in range(n)], 0)
    p_rwkv = np.stack([R[c]["p_rwkv"] for c in range(n)], 1)
    p_shift = np.stack([R[c]["p_shift"] for c in range(n)], 1)
    p_hgrn = np.stack([R[c]["p_hgrn"] for c in range(n)], 0)[None]
    p_gla = np.stack([R[c]["p_gla"] for c in range(n)], 0)[None]
    p_conv = np.stack([R[c]["p_conv"] for c in range(n)], 1)
    s_rwkv = np.concatenate([R[c]["s_rwkv"] for c in range(n)], 1)
    s_shift = np.concatenate([R[c]["s_shift"] for c in range(n)], 1)
    s_hgrn = np.concatenate([R[c]["s_hgrn"] for c in range(n)], 0)[None]
    s_gla = np.concatenate([R[c]["s_gla"] for c in range(n)], 0)[None]
    s_conv = np.concatenate([R[c]["s_conv"] for c in range(n)], 1)
    outs = (y_prompt, y_sample, p_rwkv, p_shift, p_hgrn, p_gla, p_conv, s_rwkv, s_shift, s_hgrn, s_gla, s_conv)
    return tuple(np.ascontiguousarray(o, dtype=f32) for o in outs)
```

```python
import contextlib
import numpy as np
import concourse.bass as bass
import concourse.mybir as mybir

F32 = mybir.dt.float32
BF16 = mybir.dt.bfloat16
AF = mybir.ActivationFunctionType
ALU = mybir.AluOpType
AX = mybir.AxisListType

SAME_ENGINE_SYNC = True
N_DMA_SEMS = 40


class T:
    def __init__(self, tile, name):
        self.t = tile
        self.name = name
        self.lw = None
        self.rd = {}

    def __getitem__(self, idx):
        return V(self.t[idx], self)

    def sub(self, name):
        return T(self.t, self.name + "." + name)


class V:
    def __init__(self, ap, owner):
        self.ap = ap
        self.o = owner

    def __getitem__(self, idx):
        return V(self.ap[idx], self.o)

    def rr(self, pat, **kw):
        return V(self.ap.rearrange(pat, **kw), self.o)

    def bc(self, shape):
        return V(self.ap.broadcast_to(shape), self.o)

    def us(self, axis):
        return V(self.ap.unsqueeze(axis), self.o)

    def bitcast(self, dt):
        return V(self.ap.bitcast(dt), self.o)

    @property
    def shape(self):
        return self.ap.shape


def _own(vs):
    r = []
    for v in vs:
        if isinstance(v, V):
            if isinstance(v.o, (list, tuple)):
                r.extend(v.o)
            else:
                r.append(v.o)
    return r


class KB:
    def __init__(self, nc):
        self.nc = nc
        self.es = contextlib.ExitStack()
        self.eng = {"pe": nc.tensor, "act": nc.scalar, "dve": nc.vector, "pool": nc.gpsimd, "sp": nc.sync}
        self.sem = {}
        self.cnt = {}
        self.waited = {e: {} for e in self.eng}
        for e in self.eng:
            self.sem[e] = self.es.enter_context(nc.semaphore("s_" + e))
            self.cnt[e] = 0
        self.dsem = [self.es.enter_context(nc.semaphore("d%d" % i)) for i in range(N_DMA_SEMS)]
        self.dcnt = [0] * N_DMA_SEMS
        self.ndma = 0
        self.n_ins = 0
        self.n_wait = 0
        self.out_events = []

    def sb(self, name, shape, dt=F32):
        return T(self.es.enter_context(self.nc.sbuf_tensor(name, list(shape), dt)), name)

    def ps(self, name, shape, dt=F32):
        return T(self.es.enter_context(self.nc.psum_tensor(name, list(shape), dt)), name)

    def _wait(self, e, ev):
        en, sem, val, sid = ev
        if en == e and (e == "pe" or not SAME_ENGINE_SYNC):
            return
        w = self.waited[e]
        if w.get(sid, 0) >= val:
            return
        w[sid] = val
        self.eng[e].wait_ge(sem, val)
        self.n_wait += 1

    def _sync(self, e, reads, writes):
        for t in reads:
            if t.lw is not None:
                self._wait(e, t.lw)
        for t in writes:
            if t.lw is not None:
                self._wait(e, t.lw)
            for ev in t.rd.values():
                self._wait(e, ev)

    def _post(self, e, ev, reads, writes, rkey=None):
        for t in reads:
            t.rd[rkey or e] = ev
        for t in writes:
            t.lw = ev
            t.rd = {}

    def emit(self, e, fn, reads, writes):
        reads = _own(reads)
        writes = _own(writes)
        self._sync(e, reads, writes)
        ins = fn()
        self.cnt[e] += 1
        ins.then_inc(self.sem[e], 1)
        ev = (e, self.sem[e], self.cnt[e], "e_" + e)
        self._post(e, ev, reads, writes)
        self.n_ins += 1
        return ev

    def mm(self, out, lhsT, rhs, start=True, stop=True, **kw):
        return self.emit("pe", lambda: self.nc.tensor.matmul(out.ap, lhsT=lhsT.ap, rhs=rhs.ap, start=start, stop=stop, **kw),
                         [lhsT, rhs], [out])

    def tr(self, out, in_, ident):
        return self.emit("pe", lambda: self.nc.tensor.transpose(out.ap, in_.ap, ident.ap), [in_, ident], [out])

    def act(self, out, in_, func, bias=0.0, scale=1.0, e="act"):
        b = bias.ap if isinstance(bias, V) else bias
        s = scale.ap if isinstance(scale, V) else scale
        return self.emit("act", lambda: self.nc.scalar.activation(out=out.ap, in_=in_.ap, func=func, bias=b, scale=s),
                         [in_, bias, scale], [out])

    def tt(self, e, out, in0, in1, op):
        return self.emit(e, lambda: self.eng[e].tensor_tensor(out=out.ap, in0=in0.ap, in1=in1.ap, op=op), [in0, in1], [out])

    def ts(self, e, out, in0, s1, op0, s2=None, op1=None):
        a1 = s1.ap if isinstance(s1, V) else s1
        a2 = s2.ap if isinstance(s2, V) else s2
        kw = {}
        if op1 is not None:
            kw["op1"] = op1
        return self.emit(e, lambda: self.eng[e].tensor_scalar(out=out.ap, in0=in0.ap, scalar1=a1, scalar2=a2, op0=op0, **kw),
                         [in0, s1, s2], [out])

    def stt(self, out, in0, scalar, in1, op0, op1):
        a = scalar.ap if isinstance(scalar, V) else scalar
        return self.emit("dve", lambda: self.nc.vector.scalar_tensor_tensor(out=out.ap, in0=in0.ap, scalar=a, in1=in1.ap, op0=op0, op1=op1),
                         [in0, scalar, in1], [out])

    def copy(self, e, out, in_):
        if e == "act":
            return self.act(out, in_, AF.Copy)
        return self.emit(e, lambda: self.eng[e].tensor_copy(out=out.ap, in_=in_.ap), [in_], [out])

    def memset(self, e, out, val):
        return self.emit(e, lambda: self.eng[e].memset(out.ap, val), [], [out])

    def scan(self, out, d0, d1, init, op0, op1):
        i = init.ap if isinstance(init, V) else init
        return self.emit("dve", lambda: self.nc.vector.tensor_tensor_scan(out=out.ap, data0=d0.ap, data1=d1.ap, initial=i, op0=op0, op1=op1),
                         [d0, d1, init], [out])

    def recip(self, out, in_):
        return self.emit("dve", lambda: self.nc.vector.reciprocal(out=out.ap, in_=in_.ap), [in_], [out])

    def reduce(self, out, in_, op=ALU.add, axis=AX.X):
        return self.emit("dve", lambda: self.nc.vector.tensor_reduce(out=out.ap, in_=in_.ap, axis=axis, op=op), [in_], [out])

    def dma(self, out, in_, is_output=False, extra_reads=(), extra_writes=(), q="sp", **kw):
        e = "pool" if (is_output and q == "sp") else q
        reads = _own([in_]) + list(extra_reads)
        writes = _own([out]) + list(extra_writes)
        self._sync(e, reads, writes)
        j = self.ndma % N_DMA_SEMS
        self.ndma += 1
        sem = self.dsem[j]
        if self.dcnt[j] > 0:
            self._wait(e, ("dma", sem, self.dcnt[j], "d%d" % j))
        self.dcnt[j] += 16
        oa = out.ap if isinstance(out, V) else out
        ia = in_.ap if isinstance(in_, V) else in_
        ins = self.eng[e].dma_start(out=oa, in_=ia, **kw)
        ins.then_inc(sem, 16)
        ev = ("dma", sem, self.dcnt[j], "d%d" % j)
        self._post(e, ev, reads, writes, rkey="dma%d" % self.ndma)
        if is_output:
            self.out_events.append(ev)
        self.n_ins += 1
        return ev

    def finish(self):
        for ev in self.out_events:
            self._wait("sp", ev)
        for j in range(N_DMA_SEMS):
            if self.dcnt[j] > 0:
                self._wait("sp", ("dma", self.dsem[j], self.dcnt[j], "d%d" % j))
        for e in ("pe", "act", "dve", "pool"):
            if self.cnt[e] > 0:
                self._wait("sp", (e, self.sem[e], self.cnt[e], "e_" + e))

    def close(self):
        self.es.close()


import math
from concourse.bass_utils import run_bass_kernel_spmd

DN_ALPHA = 8.0 ** 0.25
C0 = math.exp(-0.5)
D = 1024
FF = 2816
NF = 22
LN_EPS = 1e-5
RMS_EPS = 1e-5
LNX_EPS = 64e-5
MUL, ADD, SUB = ALU.mult, ALU.add, ALU.subtract

WEIGHT_SHAPES = {
    "rw_mix": (2, 6, 1024), "rw_wr": (2, 1024, 1024), "rw_wk": (2, 1024, 1024), "rw_wv": (2, 1024, 1024),
    "rw_wo": (2, 1024, 1024), "rw_w0": (2, 1024), "rw_w1": (2, 1024, 64), "rw_w2": (2, 64, 1024),
    "rw_a0": (2, 1024), "rw_a1": (2, 1024, 64), "rw_a2": (2, 64, 1024), "rw_g1": (2, 1024, 160),
    "rw_g2": (2, 160, 1024), "rw_k_k": (2, 1024), "rw_k_a": (2, 1024), "rw_r_k": (2, 16, 64),
    "rw_lnx_w": (2, 1024), "rw_lnx_b": (2, 1024), "rw_v0": (1, 1024), "rw_v1": (1, 1024, 32),
    "rw_v2": (1, 32, 1024), "hg_wq": (1, 1024, 1024), "hg_wf": (1, 1024, 1024), "hg_wi": (1, 1024, 1024),
    "hg_wg": (1, 1024, 1024), "hg_wo": (1, 1024, 1024), "hg_norm_w": (1, 128), "hg_lb_param": (4, 1024),
    "gl_wq": (1, 1024, 512), "gl_wk": (1, 1024, 512), "gl_wv": (1, 1024, 1024), "gl_wg": (1, 1024, 1024),
    "gl_gk1": (1, 1024, 16), "gl_gk2": (1, 16, 512), "gl_gk_b": (1, 512), "gl_wo": (1, 1024, 1024),
    "gl_norm_w": (1, 256), "ffn_wu": (4, 1024, 2816), "ffn_wg": (4, 1024, 2816), "ffn_conv_w": (4, 3, 2816),
    "ffn_conv_b": (4, 2816), "ffn_wd": (4, 2816, 1024), "ln1_w": (4, 1024), "ln1_b": (4, 1024),
    "ln2_w": (4, 1024), "ln2_b": (4, 1024),
}
IN_SHAPES = {
    "x_prompt": (2048, 1024), "x_sample": (64, 1024), "state_rwkv": (2, 16, 16, 64, 64),
    "state_rwkv_shift": (2, 16, 1024), "state_hgrn": (16, 8, 128, 128), "state_gla": (16, 4, 128, 256),
    "state_ffn_conv": (4, 16, 2, 2816),
}
OUT_SHAPES = {
    "y_prompt": (2048, 1024), "y_sample": (64, 1024), "p_rwkv": (2, 16, 64, 64), "p_shift": (2, 1024),
    "p_hgrn": (8, 128, 128), "p_gla": (4, 128, 256), "p_conv": (4, 2, 2816),
    "s_rwkv": (2, 16, 16, 64, 64), "s_shift": (2, 16, 1024), "s_hgrn": (16, 8, 128, 128),
    "s_gla": (16, 4, 128, 256), "s_conv": (4, 16, 2, 2816),
}


class Arena:
    def __init__(self, k, name, nkb):
        self.t = k.es.enter_context(k.nc.sbuf_tensor(name, [128, nkb * 256], F32))
        self.slots = [T(self.t, "%s%d" % (name, i)) for i in range(nkb)]
        self.nkb = nkb
        self.p = 0

    def reset(self, p=0):
        self.p = p

    def take(self, nbytes, dt=F32, pat=None, parts=128, **kw):
        nkb = (nbytes + 1023) // 1024
        off = self.p
        self.p += nkb
        assert self.p <= self.nkb, "arena overflow %d > %d" % (self.p, self.nkb)
        ap = self.t[0:parts, off * 256: off * 256 + nbytes // 4]
        if dt == BF16:
            ap = ap.bitcast(BF16)
        if pat:
            ap = ap.rearrange(pat, **kw)
        return V(ap, self.slots[off:off + nkb])

    def buf(self, n, w, dt=F32):
        es = 2 if dt == BF16 else 4
        off = self.p
        full = self.take(n * w * es, dt, "p (n w) -> p n w", n=n)
        ch = []
        for c in range(n):
            b0 = c * w * es
            b1 = (c + 1) * w * es
            ch.append(V(full.ap[:, c, :], self.slots[off + b0 // 1024: off + (b1 + 1023) // 1024]))
        return full, ch


def build(nc, dbg_n=0):
    k = KB(nc)
    A = {}
    for n, s in list(IN_SHAPES.items()) + list(WEIGHT_SHAPES.items()):
        A[n] = nc.dram_tensor(n, list(s), F32, kind="ExternalInput").ap()
    for n, s in OUT_SHAPES.items():
        A[n] = nc.dram_tensor(n, list(s), F32, kind="ExternalOutput").ap()
    if dbg_n:
        A["dbg"] = nc.dram_tensor("dbg", [dbg_n, 2112, 1024], F32, kind="ExternalOutput").ap()

    psf_t = k.es.enter_context(nc.psum_tensor("psf", [128, 3584], F32))
    banks = [T(psf_t, "bank%d" % i) for i in range(7)]
    psb = k.ps("psb", [128, 1024], BF16)
    bp = [0]

    def bank(n=1, parts=slice(0, 128)):
        p = bp[0]
        if n == 2:
            if p % 2:
                p += 1
            if p + 2 > 6:
                p = 0
        else:
            if p >= 7:
                p = 0
        bp[0] = p + n
        return V(psf_t[parts, p * 512:(p + n) * 512], banks[p:p + n])

    identf = k.sb("identf", [128, 128])
    identb = k.sb("identb", [128, 128], BF16)
    onesf = k.sb("onesf", [128, 512])
    bones = k.sb("bones", [128, 128])
    onesb = k.sb("onesb", [128, 128], BF16)
    bonesb = k.sb("bonesb", [128, 128], BF16)
    P = k.sb("P", [128, 1024])
    x32t = k.sb("x32", [128, 8 * 512])
    xbft = k.sb("xbf", [128, 8 * 512], BF16)
    vft = k.sb("vf", [128, 8 * 512], BF16)
    WB = [k.sb("wb%d" % i, [128, 512], BF16) for i in range(4)]
    ST32 = [k.sb("st32_%d" % i, [128, 1024]) for i in range(4)]
    STB = [k.sb("stb_%d" % i, [128, 1024], BF16) for i in range(4)]
    shiftP = [k.sb("shp%d" % i, [128, 8]) for i in range(2)]
    ZH = k.sb("zh", [128, 4 * NF * 2])
    ZHS = k.sb("zhs", [128, NF * 32])
    msk = {}
    for kind, C, blk in (("p", 64, 64), ("s", 4, 32)):
        for nm in ("su", "ui", "sl", "id"):
            msk[kind + nm] = k.sb("m_%s_%s" % (kind, nm), [128, C])
    rmask = {"p": k.sb("rm_p", [128, 1024]), "s": k.sb("rm_s", [128, 256])}
    AR = Arena(k, "ar", 123)

    k.memset("dve", onesf[:], 1.0)
    k.memset("dve", onesb[:], 1.0)
    k.memset("dve", bonesb[:], 0.0)
    k.memset("dve", bonesb[0:64, 0:64], 1.0)
    k.memset("dve", bonesb[64:128, 64:128], 1.0)
    k.memset("dve", bones[:], 0.0)
    k.memset("dve", bones[0:64, 0:64], 1.0)
    k.memset("dve", bones[64:128, 64:128], 1.0)
    k.emit("pool", lambda: nc.gpsimd.affine_select(out=identf.t[:], in_=onesf.t[:, 0:128], pattern=[[-1, 128]],
                                                   compare_op=ALU.is_equal, fill=0.0, base=0, channel_multiplier=1),
           [onesf[:]], [identf[:]])
    k.copy("dve", identb[:], identf[:])
    for kind, C, blk in (("p", 64, 64), ("s", 4, 32)):
        for nm, pat, cm, op, base in (("su", 1, -1, ALU.is_gt, 0), ("ui", 1, -1, ALU.is_ge, 0),
                                      ("sl", -1, 1, ALU.is_gt, 0), ("id", 1, -1, ALU.is_equal, 0)):
            m = msk[kind + nm]
            for b in range(128 // blk):
                k.emit("pool", lambda m=m, b=b, blk=blk, C=C, pat=pat, cm=cm, op=op: nc.gpsimd.affine_select(
                    out=m.t[b * blk:(b + 1) * blk, :], in_=onesf.t[b * blk:(b + 1) * blk, 0:C], pattern=[[pat, C]],
                    compare_op=op, fill=0.0, base=0, channel_multiplier=cm), [onesf[:]], [m[:]])
    k.memset("dve", rmask["p"][:], 1.0)
    k.memset("dve", rmask["p"][:].rr("p (n c) -> p n c", c=64)[:, :, 0:1], 0.0)
    k.memset("dve", rmask["s"][:], 1.0)
    k.memset("dve", rmask["s"][:].rr("p (n c) -> p n c", c=4)[:, :, 0:1], 0.0)
    for t_ in ST32:
        k.memset("dve", t_[:], 0.0)
    for t_ in STB:
        k.memset("dve", t_[:], 0.0)
    for t_ in shiftP:
        k.memset("dve", t_[:], 0.0)
    k.memset("dve", ZH[:], 0.0)

    plist = []

    def addp(name, ap2d, nrows):
        plist.append((name, ap2d, nrows))

    for l in range(2):
        addp("mix%d" % l, A["rw_mix"][l].rearrange("j (c p) -> (j c) p", p=128), 48)
        for nm, src in (("w0", "rw_w0"), ("a0", "rw_a0"), ("kk", "rw_k_k"), ("ka", "rw_k_a"),
                        ("lnxw", "rw_lnx_w"), ("lnxb", "rw_lnx_b")):
            addp("%s%d" % (nm, l), A[src][l].rearrange("(c p) -> c p", p=128), 8)
        addp("rk%d" % l, A["rw_r_k"][l].rearrange("(c a) b -> c (a b)", a=2), 8)
    addp("v0", A["rw_v0"][0].rearrange("(c p) -> c p", p=128), 8)
    addp("hgn", A["hg_norm_w"], 1)
    addp("lbp", A["hg_lb_param"].rearrange("l (c p) -> (l c) p", p=128), 32)
    addp("gkb", A["gl_gk_b"][0].rearrange("(c p) -> c p", p=128), 4)
    addp("gln", A["gl_norm_w"][0].rearrange("(c p) -> c p", p=128), 2)
    for l in range(4):
        addp("cw%d" % l, A["ffn_conv_w"][l].rearrange("j (c p) -> (j c) p", p=128), 66)
        addp("cb%d" % l, A["ffn_conv_b"][l].rearrange("(c p) -> c p", p=128), 22)
        for nm in ("ln1_w", "ln1_b", "ln2_w", "ln2_b"):
            addp("%s%d" % (nm, l), A[nm][l].rearrange("(c p) -> c p", p=128), 8)
    pcol = {}
    slot, row = 0, 0
    place = []
    for name, ap2d, nrows in plist:
        if row + nrows > 128:
            slot += 1
            row = 0
        pcol[name] = slot * 128 + row
        place.append((slot, row, ap2d, nrows))
        row += nrows
    nslots = slot + 1
    assert nslots * 128 + 64 <= 1024
    AR.reset()
    PR = AR.take(nslots * 512, F32, "p (s m) -> p s m", s=nslots)
    k.memset("dve", PR, 0.0)
    for (s_, r_, ap2d, nrows) in place:
        k.dma(PR[r_:r_ + nrows, s_, :], ap2d)
    for s_ in range(nslots):
        b = bank()
        k.tr(b[:, 0:128], PR[:, s_, :], identf[:])
        k.copy("dve", P[:, s_ * 128:(s_ + 1) * 128], b[:, 0:128])
    dcol = nslots * 128

    def pc(name, off=0, n=1):
        c = pcol[name] + off
        return P[:, c:c + n]

    for l in range(2):
        pcol["oka%d" % l] = dcol
        k.ts("dve", P[:, dcol:dcol + 8], pc("ka%d" % l, 0, 8), -1.0, MUL, 1.0, ADD)
        dcol += 8
    pcol["ngkb"] = dcol
    k.ts("dve", P[:, dcol:dcol + 4], pc("gkb", 0, 4), -1.0, MUL)
    dcol += 4
    pcol["lbe"] = dcol
    k.act(P[:, dcol:dcol + 32], pc("lbp", 0, 32), AF.Exp)
    lbe = P[:, dcol:dcol + 32]
    dcol += 32
    pcol["lb1"] = dcol
    pcol["olb1"] = dcol + 8
    lsum = P[:, dcol + 16:dcol + 24]
    k.tt("dve", lsum, lbe[:, 0:8], lbe[:, 8:16], ADD)
    k.tt("dve", lsum, lsum, lbe[:, 16:24], ADD)
    k.tt("dve", lsum, lsum, lbe[:, 24:32], ADD)
    k.recip(lsum, lsum)
    k.tt("dve", P[:, dcol:dcol + 8], lbe[:, 8:16], lsum, MUL)
    k.ts("dve", P[:, dcol + 8:dcol + 16], P[:, dcol:dcol + 8], -1.0, MUL, 1.0, ADD)
    dcol += 24
    assert dcol <= 1024

    x32 = x32t[:].rr("p (c w) -> p c w", c=8)
    xbf = xbft[:].rr("p (c w) -> p c w", c=8)
    vf = vft[:].rr("p (c w) -> p c w", c=8)

    wctr = [0]

    class WV:
        def __init__(self, ap, trk):
            self.ap = ap
            self.trk = trk

        def __getitem__(self, i):
            return WV(self.ap[i], self.trk)

        def rearrange(self, pat, **kw):
            return WV(self.ap.rearrange(pat, **kw), self.trk)

    class WTn:
        def __init__(self, ap, trks):
            self.ap = ap
            self.trks = trks

        def __getitem__(self, l):
            return WV(self.ap[l], self.trks[l])

    MATW = ["rw_wr", "rw_wk", "rw_wv", "rw_wo", "rw_w1", "rw_w2", "rw_a1", "rw_a2", "rw_g1", "rw_g2", "rw_v1", "rw_v2",
            "hg_wq", "hg_wf", "hg_wi", "hg_wg", "hg_wo", "gl_wq", "gl_wk", "gl_wv", "gl_wg", "gl_gk1", "gl_gk2", "gl_wo",
            "ffn_wu", "ffn_wg", "ffn_wd"]
    shadow = {}
    for n_ in MATW:
        sh = nc.dram_tensor(n_ + "_bf", list(WEIGHT_SHAPES[n_]), BF16, kind="Internal").ap()
        shadow[n_] = WTn(sh, [T(None, "%s_%d" % (n_, l_)) for l_ in range(WEIGHT_SHAPES[n_][0])])

    def cast_w(n_, l_):
        src = A[n_][l_]
        dst = shadow[n_].ap[l_]
        cols = WEIGHT_SHAPES[n_][2]
        if cols > 2048:
            src = src.rearrange("k (a b) -> (k a) b", a=4)
            dst = dst.rearrange("k (a b) -> (k a) b", a=4)
        k.dma(dst, src, extra_writes=[shadow[n_].trks[l_]], q="pool")

    RWN = ["rw_w1", "rw_a1", "rw_g1", "rw_wr", "rw_wk", "rw_wv", "rw_w2", "rw_a2", "rw_g2", "rw_wo"]
    FFN = ["ffn_wu", "ffn_wg", "ffn_wd"]
    for n_ in RWN:
        cast_w(n_, 0)
    for n_ in FFN:
        cast_w(n_, 0)
    for n_ in ["hg_wq", "hg_wf", "hg_wg", "hg_wi", "hg_wo"]:
        cast_w(n_, 0)
    for n_ in FFN:
        cast_w(n_, 1)
    for n_ in ["gl_wq", "gl_gk1", "gl_gk2", "gl_wk", "gl_wg", "gl_wv", "gl_wo"]:
        cast_w(n_, 0)
    for n_ in FFN:
        cast_w(n_, 2)
    for n_ in RWN:
        cast_w(n_, 1)
    for n_ in ["rw_v1", "rw_v2"]:
        cast_w(n_, 0)
    for n_ in FFN:
        cast_w(n_, 3)
    for n_ in MATW:
        A[n_] = shadow[n_]

    def wload(wv, p, kk, m):
        i = wctr[0]
        wctr[0] += 1
        assert kk * m <= 512
        wb = WB[i % 4][0:p, 0:kk * m].rr("p (k m) -> p k m", k=kk)
        k.dma(wb, wv.ap, extra_reads=[wv.trk])
        return wb

    def wmat(w2d, k0, c0, m):
        return wload(w2d[k0 * 128:(k0 + 1) * 128, c0:c0 + m].rearrange("p (o m) -> p o m", o=1), 128, 1, m)[:, 0, :]

    F32R = mybir.dt.float32r

    def mmr(out, lhsT, rhs, **kw):
        return k.mm(out, lhsT, rhs, **kw)

    def r32(v):
        return v

    evac_rr = [0]

    def evac(out, in_):
        evac_rr[0] += 1
        k.copy("act" if evac_rr[0] % 2 else "dve", out, in_)

    def proj_fm(w2d, nk, ncols, rhs_fn, W, consume, col0=0):
        c = 0
        while c < ncols:
            g = min(512, ncols - c)
            nm = g // 128
            bs = [bank() for _ in range(nm)]
            for kc in range(nk):
                wb = wmat(w2d, kc, col0 + c, g)
                for mi in range(nm):
                    k.mm(bs[mi][:, 0:W], wb[:, mi * 128:(mi + 1) * 128], rhs_fn(kc), start=(kc == 0), stop=(kc == nk - 1))
            for mi in range(nm):
                consume((c // 128) + mi, bs[mi][:, 0:W])
            c += g

    def ln_sublayer(h_chunks, W, wname, bname):
        Z, Zc = AR.buf(8, W, F32)
        sq = AR.take(W * 4)
        for c in range(8):
            k.stt(Zc[c], x32[:, c, 0:W], DN_ALPHA, h_chunks[c], MUL, ADD)
        mu = bank()
        for c in range(8):
            k.mm(mu[:, 0:W], onesf[:, 0:128], Zc[c], start=(c == 0), stop=(c == 7))
        for c in range(8):
            k.stt(Zc[c], mu[:, 0:W], -1.0 / D, Zc[c], MUL, ADD)
        var = bank()
        for c in range(8):
            sqb = sq.bitcast(BF16)[:, 0:W]
            k.act(sqb, Zc[c], AF.Square)
            k.mm(var[:, 0:W], onesb[:], sqb, start=(c == 0), stop=(c == 7))
        rstd = AR.take(W * 4)
        k.act(rstd, var[:, 0:W], AF.Sqrt, bias=LN_EPS, scale=1.0 / D)
        k.recip(rstd, rstd)
        for c in range(8):
            k.tt("dve", Zc[c], Zc[c], rstd, MUL)
            k.ts("dve", x32[:, c, 0:W], Zc[c], pc(wname, c), MUL, pc(bname, c), ADD)
            k.copy("act", xbf[:, c, 0:W], x32[:, c, 0:W])


    cst = k.sb("cst", [128, 8])
    k.memset("dve", cst[:, 0:1], LN_EPS)
    k.memset("dve", cst[:, 1:2], RMS_EPS)
    k.memset("dve", cst[:, 2:3], LNX_EPS)
    k.memset("dve", cst[:, 3:4], 1.0)
    k.memset("dve", cst[:, 4:5], 0.0)
    k.memset("dve", cst[:, 7:8], 1e-18)
    k.memset("dve", cst[:, 5:7], 0.0)
    k.memset("dve", cst[0:64, 5:6], 1.0)
    k.memset("dve", cst[64:128, 6:7], 1.0)

    class Tl:
        pass

    def acc_group(w2d, nk, col, g, rhs_fn, W, extra=None):
        nm = (g + 127) // 128
        bs = [bank() for _ in range(nm)]
        for kc in range(nk):
            wb = wmat(w2d, kc, col, g)
            for mi in range(nm):
                mw = min(128, g - mi * 128)
                k.mm(bs[mi][0:mw, 0:W], wb[:, mi * 128:mi * 128 + mw], rhs_fn(kc), start=(kc == 0), stop=(kc == nk - 1))
            if extra is not None:
                extra(kc, wb)
        return bs

    ln_state = {}

    def ln_begin(W):
        Z, Zc = AR.buf(8, W, F32)
        ln_state["Zc"] = Zc
        ln_state["Z"] = Z
        ln_state["W"] = W

    def ln_add(c, h_ps, col0=0):
        W = ln_state["W"]
        k.stt(ln_state["Zc"][c], x32[:, c, col0:col0 + W], DN_ALPHA, h_ps, MUL, ADD)

    def ln_finish(wname, bname, col0=0):
        W = ln_state["W"]
        Zc = ln_state["Zc"]
        sq = [AR.take(W * 2, BF16) for _ in range(2)]
        mean = AR.take(W * 4)
        rstd = AR.take(W * 4)
        mu = bank()
        ss = bank()
        for c in range(8):
            k.act(sq[c % 2], Zc[c], AF.Square)
            k.mm(mu[:, 0:W], onesf[:, 0:128], Zc[c], start=(c == 0), stop=(c == 7))
            k.mm(ss[:, 0:W], onesb[:], sq[c % 2], start=(c == 0), stop=(c == 7))
        k.act(mean, mu[:, 0:W], AF.Identity, scale=1.0 / D)
        k.act(rstd, mean, AF.Square)
        k.stt(rstd, ss[:, 0:W], 1.0 / D, rstd, MUL, SUB)
        k.act(rstd, rstd, AF.Ln, bias=cst[:, 0:1])
        k.act(rstd, rstd, AF.Exp, scale=-0.5)
        if W <= 64:
            Z = ln_state["Z"]
            k.tt("dve", Z, Z, mean.us(1).bc([128, 8, W]), SUB)
            k.tt("dve", Z, Z, rstd.us(1).bc([128, 8, W]), MUL)
            wc = pcol[wname]
            bc_ = pcol[bname]
            k.tt("dve", Z, Z, P[:, wc:wc + 8].us(2).bc([128, 8, W]), MUL)
            k.tt("dve", x32[:, :, col0:col0 + W], Z, P[:, bc_:bc_ + 8].us(2).bc([128, 8, W]), ADD)
            k.copy("act", xbf[:, :, col0:col0 + W], x32[:, :, col0:col0 + W])
            return
        for c in range(8):
            k.tt("dve", Zc[c], Zc[c], mean, SUB)
            k.tt("dve", Zc[c], Zc[c], rstd, MUL)
            k.act(x32[:, c, col0:col0 + W], Zc[c], AF.Identity, bias=pc(bname, c), scale=pc(wname, c))
            k.act(xbf[:, c, col0:col0 + W], Zc[c], AF.Identity, bias=pc(bname, c), scale=pc(wname, c))

    def load_tile(tl):
        AR.reset()
        if tl.kind == "p":
            for j in range(4):
                XL = AR.take(4096)
                k.dma(XL, A["x_prompt"][tl.t0 + j * 128: tl.t0 + (j + 1) * 128, :])
                for hb in range(2):
                    b = bank()
                    for cc in range(4):
                        c = hb * 4 + cc
                        k.tr(b[:, cc * 128:(cc + 1) * 128], XL[:, c * 128:(c + 1) * 128], identf[:])
                    evac(x32[:, hb * 4:(hb + 1) * 4, j * 128:(j + 1) * 128], b[:, 0:512].rr("p (c w) -> p c w", c=4))
        else:
            XL = AR.take(4096)
            k.dma(XL[0:64, :], A["x_sample"][:, :])
            b = bank()
            for c in range(8):
                k.tr(b[:, c * 64:(c + 1) * 64], XL[0:64, c * 128:(c + 1) * 128], identf[0:64, 0:64])
            evac(x32[:, :, 0:64], b[:, 0:512].rr("p (c w) -> p c w", c=8))
        k.copy("act", xbf[:, :, 0:tl.W], x32[:, :, 0:tl.W])

    def store_rows(dram_rows, src_fn, n, ncol_chunks=8, is_output=True):
        YO = AR.take(ncol_chunks * 512)
        c = 0
        while c < ncol_chunks:
            g = min(4, ncol_chunks - c)
            b = bank()
            for cc in range(g):
                k.tr(b[0:n, cc * 128:(cc + 1) * 128], src_fn(c + cc), identf[:])
            evac(YO[0:n, c * 128:(c + g) * 128], b[0:n, 0:g * 128])
            c += g
        k.dma(dram_rows, YO[0:n, 0:ncol_chunks * 128], is_output=is_output)

    def store_tile(tl, dram, row0):
        AR.reset()
        if tl.kind == "p":
            for j in range(4):
                store_rows(dram[row0 + tl.t0 + j * 128: row0 + tl.t0 + (j + 1) * 128, :],
                           lambda c, j=j: x32[:, c, j * 128:(j + 1) * 128], 128)
        else:
            store_rows(dram[row0:row0 + 64, :], lambda c: x32[:, c, 0:64], 64)

    def ffn_sublayer(l, tl):
        W, nseq, L = tl.W, tl.nseq, tl.L
        AR.reset()
        MT, MTc = AR.buf(NF, W, BF16)
        zcs = [AR.take((W + 2 * nseq) * 4, F32, "p (s l) -> p s l", s=nseq) for _ in range(3)]
        ubs = [AR.take(W * 4, F32, "p (s l) -> p s l", s=nseq) for _ in range(3)]
        t1 = AR.take(W * 4, F32, "p (s l) -> p s l", s=nseq)
        s1 = AR.take(W * 4, F32, "p (s l) -> p s l", s=nseq)
        want_tm = (tl.kind == "s") or tl.last
        if want_tm:
            nsel = 64 if tl.kind == "s" else 2
            ZTM = AR.take(FF * 4)
            sel0 = 0 if tl.kind == "s" else W - 2
        if tl.kind == "s":
            ZR = AR.take(FF * 4)
            k.dma(ZR[0:32, :], A["state_ffn_conv"][l].rearrange("s j f -> (s j) f"))
            for f0 in range(0, NF, 16):
                g = min(16, NF - f0)
                b = bank()
                for ff in range(g):
                    k.tr(b[:, ff * 32:(ff + 1) * 32], ZR[0:32, (f0 + ff) * 128:(f0 + ff + 1) * 128], identf[0:32, 0:32])
                evac(ZHS[:, f0 * 32:(f0 + g) * 32], b[:, 0:g * 32])
        cw = pcol["cw%d" % l]
        cb = pcol["cb%d" % l]
        col = 0
        if tl.kind == "s":
            def a4(nb):
                return AR.take(nb, F32, "p (m s l) -> p m s l", m=4, s=16)
            ZC4, U4, T4, TM4, S4 = a4(4 * 16 * 6 * 4), a4(1024), a4(1024), a4(1024), a4(1024)
            while col < FF:
                g = min(512, FF - col)
                nm = g // 128
                f0 = col // 128
                bu, bz, ztb = bank(), bank(), bank()
                for (wname, bk, tm) in (("ffn_wu", bu, False), ("ffn_wg", bz, True)):
                    for kc in range(8):
                        wb = wmat(A[wname][l], kc, col, g)
                        for mi in range(nm):
                            k.mm(bk[:, mi * 64:(mi + 1) * 64], wb[:, mi * 128:(mi + 1) * 128], xbf[:, kc, 0:64],
                                 start=(kc == 0 and mi == 0), stop=(kc == 7), skip_group_check=True)
                        if tm:
                            k.mm(ztb[0:64, 0:g], xbf[:, kc, 0:64], wb, start=(kc == 0), stop=(kc == 7))
                evac(ZTM[0:64, col:col + g], ztb[0:64, 0:g])

                def v4(b_):
                    return b_[:, 0:nm * 64].rr("p (m s l) -> p m s l", m=nm, s=16)

                def pb(base):
                    return P[:, base + f0:base + f0 + nm].us(2).us(3).bc([128, nm, 16, 4])
                k.copy("act", ZC4[:, 0:nm, :, 2:6], v4(bz))
                k.copy("act", U4[:, 0:nm], v4(bu))
                k.copy("dve", ZC4[:, 0:nm, :, 0:2], ZHS[:, f0 * 32:(f0 + nm) * 32].rr("p (m s j) -> p m s j", m=nm, j=2))
                k.tt("dve", T4[:, 0:nm], ZC4[:, 0:nm, :, 2:6], pb(cw + 2 * NF), MUL)
                k.tt("dve", T4[:, 0:nm], T4[:, 0:nm], pb(cb), ADD)
                k.tt("dve", TM4[:, 0:nm], ZC4[:, 0:nm, :, 1:5], pb(cw + NF), MUL)
                k.tt("dve", T4[:, 0:nm], T4[:, 0:nm], TM4[:, 0:nm], ADD)
                k.tt("dve", TM4[:, 0:nm], ZC4[:, 0:nm, :, 0:4], pb(cw), MUL)
                k.tt("dve", T4[:, 0:nm], T4[:, 0:nm], TM4[:, 0:nm], ADD)
                k.act(S4[:, 0:nm], T4[:, 0:nm], AF.Silu)
                k.tt("dve", MT[:, f0:f0 + nm, :].rr("p m (s l) -> p m s l", s=16), S4[:, 0:nm], U4[:, 0:nm], MUL)
                col += g
        while col < FF:
            g = min(384, FF - col)
            bu = acc_group(A["ffn_wu"][l], 8, col, g, lambda kc: xbf[:, kc, 0:W], W)
            ztb = bank() if want_tm else None

            def extra(kc, wb):
                k.mm(ztb[0:nsel, 0:g], xbf[:, kc, sel0:sel0 + nsel], wb, start=(kc == 0), stop=(kc == 7))
            bz = acc_group(A["ffn_wg"][l], 8, col, g, lambda kc: xbf[:, kc, 0:W], W, extra if want_tm else None)
            if want_tm:
                evac(ZTM[0:nsel, col:col + g], ztb[0:nsel, 0:g])
            for mi in range(g // 128):
                k.copy("act", zcs[mi][:, :, 2:L + 2], bz[mi][:, 0:W].rr("p (s l) -> p s l", s=nseq))
                k.copy("act", ubs[mi], bu[mi][:, 0:W].rr("p (s l) -> p s l", s=nseq))
            for mi in range(g // 128):
                fi = col // 128 + mi
                zc = zcs[mi]
                if tl.kind == "p":
                    zh = ZH[:, (l * NF + fi) * 2:(l * NF + fi) * 2 + 2]
                    k.copy("dve", zc[:, 0, 0:2], zh)
                    k.copy("dve", zh, zc[:, 0, L:L + 2])
                else:
                    k.copy("dve", zc[:, :, 0:2], ZHS[:, fi * 32:(fi + 1) * 32].rr("p (s j) -> p s j", j=2))
                k.ts("dve", t1, zc[:, :, 2:L + 2], P[:, cw + 2 * NF + fi:cw + 2 * NF + fi + 1], MUL, P[:, cb + fi:cb + fi + 1], ADD)
                k.stt(t1, zc[:, :, 1:L + 1], P[:, cw + NF + fi:cw + NF + fi + 1], t1, MUL, ADD)
                k.stt(t1, zc[:, :, 0:L], P[:, cw + fi:cw + fi + 1], t1, MUL, ADD)
                k.act(s1, t1, AF.Silu)
                k.tt("dve", MTc[fi].rr("p (s l) -> p s l", s=nseq), s1, ubs[mi], MUL)
            col += g
        if want_tm:
            if tl.kind == "s":
                for j in range(2):
                    k.dma(A["s_conv"][l, :, j, :], ZTM[2 + j:64:4, :], is_output=True)
            else:
                k.dma(A["p_conv"][l], ZTM[0:2, :], is_output=True)
        ln_begin(W)
        for col in (0, 512):
            bs = acc_group(A["ffn_wd"][l], NF, col, 512, lambda kc: MTc[kc], W)
            for mi in range(4):
                ln_add(col // 128 + mi, bs[mi][:, 0:W])
        ln_finish("ln2_w%d" % l, "ln2_b%d" % l)

    def dbg_dump(idx, tl, row0):
        if not dbg_n or idx >= dbg_n:
            return
        store_tile(tl, A["dbg"][idx], row0)


    def groups_of(kind, Wm):
        gs = []
        if kind == "p":
            for g in range(Wm // 128):
                gs.append([(128 * g, 0, None), (128 * g + 64, 64, None)])
        else:
            for s0 in range(0, 16, 3):
                gs.append([(4 * (s0 + jj), 32 * jj, s0 + jj) for jj in range(min(3, 16 - s0))])
        return gs

    def to_tokmajor(dst, srcs, kind, g, ch):
        n = len(srcs)
        for i in range(n):
            if kind == "p":
                k.tr(psb[:, i * 128:(i + 1) * 128], srcs[i][:, 128 * g:128 * g + 128], identb[:])
            else:
                for (c0, pb, sidx) in ch:
                    k.tr(psb[pb:pb + 4, i * 128:(i + 1) * 128], srcs[i][:, c0:c0 + 4], identb[:])
        evac(dst[:, 0:n * 128], psb[:, 0:n * 128])

    def gla_like(l, tl, cfg):
        W, kind = tl.W, tl.kind
        H, VC = cfg["H"], cfg["VC"]
        hg = cfg["hg"]
        C = 64 if kind == "p" else 4
        nch = W // C
        ref = (C - 1) // 2
        scale = 128.0 ** -0.5
        AR.reset()
        QI, QIc = AR.buf(H, W, BF16)
        KI, KIc = AR.buf(H, W, BF16)
        QE, QEc = AR.buf(H, W, BF16)
        KD, KDc = AR.buf(H, W, BF16)
        GS, GSc = AR.buf(8, W, BF16)
        OT, OTc = AR.buf(8, W, F32)
        OG, OGc = AR.buf(8, W, BF16)
        ELt = AR.take(H * nch * 4, F32, "p (h n) -> p h n", h=H)
        tA, tB, tC, tD = [AR.take(W * 4) for _ in range(4)]
        grs = groups_of(kind, W)
        VT = [AR.take(2048, BF16) for _ in grs]
        KDT = [AR.take(H * 256, BF16) for _ in grs]
        ATsb = AR.take(H * C * 2, BF16, "p (h c) -> p h c", h=H)
        HK = AR.take(W * 2, BF16)
        FB, FBc = AR.buf(8 if hg else 4, W, F32)
        xr = lambda kc: xbf[:, kc, 0:W]

        for col in range(0, H * 128, 512):
            bs = acc_group(cfg["wq"], 8, col, 512, xr, W)
            for mi in range(4):
                h = col // 128 + mi
                if hg:
                    k.act(tA, bs[mi][:, 0:W], AF.Silu)
                    k.ts("dve", OGc[h], tA, scale, MUL)
                else:
                    k.act(OGc[h], bs[mi][:, 0:W], AF.Identity, scale=scale)

        def finalize(h, lg, kraw, qraw):
            b = tD
            k.scan(b, rmask[kind][:, 0:W], lg, 0.0, MUL, ADD)
            b3 = b.rr("p (n c) -> p n c", c=C)
            tA3 = tA.rr("p (n c) -> p n c", c=C)
            k.tt("dve", tA3, b3, b3[:, :, ref:ref + 1].bc([128, nch, C]), SUB)
            k.act(tB, tA, AF.Exp)
            k.tt("dve", QIc[h], qraw, tB, MUL)
            k.act(tB, tA, AF.Exp, scale=-1.0)
            k.tt("dve", KIc[h], kraw, tB, MUL)
            k.act(tB, b, AF.Exp)
            k.tt("dve", QEc[h], qraw, tB, MUL)
            k.tt("dve", tA3, b3[:, :, C - 1:C].bc([128, nch, C]), b3, SUB)
            k.act(tB, tA, AF.Exp)
            k.tt("dve", KDc[h], kraw, tB, MUL)
            k.act(ELt[:, h, :], b3[:, :, C - 1], AF.Exp)

        def gate_block(col):
            bs = acc_group(cfg["wg"], 8, col, 512, xr, W)
            for mi in range(4):
                k.act(GSc[col // 128 + mi], bs[mi][:, 0:W], AF.Silu)

        def v_block(col):
            vb = [bank() for _ in grs]
            for kc in range(8):
                wb = wmat(cfg["wv"], kc, col, 512)
                for g, ch in enumerate(grs):
                    if kind == "p":
                        k.mm(vb[g][:, 0:512], xbf[:, kc, 128 * g:128 * g + 128], wb, start=(kc == 0), stop=(kc == 7))
                    else:
                        for (c0, pb, sidx) in ch:
                            k.mm(vb[g][pb:pb + C, 0:512], xbf[:, kc, c0:c0 + C], wb, start=(kc == 0), stop=(kc == 7))
            for g in range(len(grs)):
                evac(VT[g][:, col:col + 512], vb[g][:, 0:512])

        if hg:
            for col in range(0, 1024, 512):
                bs = acc_group(cfg["wk"], 8, col, 512, xr, W)
                for mi in range(4):
                    k.act(FBc[col // 128 + mi], bs[mi][:, 0:W], AF.Sigmoid)
            gate_block(0)
            gate_block(512)
            v_block(0)
            v_block(512)
            for h in range(8):
                k.ts("dve", tA, FBc[h], pc("olb1", h), MUL, pc("lb1", h), ADD)
                k.act(tB, tA, AF.Ln)
                k.ts("dve", tC, tA, -1.0, MUL, 1.0, ADD)
                finalize(h, tB, tC, OGc[h])
        else:
            g1 = wload(A["gl_gk1"][0].rearrange("(k p) m -> p k m", p=128), 128, 8, 16)
            b = bank()
            for kc in range(8):
                k.mm(b[0:16, 0:W], g1[:, kc, :], xr(kc), start=(kc == 0), stop=(kc == 7))
            k.copy("act", HK[0:16, :], b[0:16, 0:W])
            g2 = wload(A["gl_gk2"][0].rearrange("p (o m) -> p o m", o=1), 16, 1, 512)[:, 0, :]
            LG, LGc = AR.buf(4, W, F32)
            for h in range(4):
                b = bank()
                k.mm(b[:, 0:W], g2[:, h * 128:(h + 1) * 128], HK[0:16, :])
                k.act(tA, b[:, 0:W], AF.Exp, bias=pc("ngkb", h), scale=-1.0)
                k.act(tB, tA, AF.Ln, bias=cst[:, 3:4])
                k.ts("dve", LGc[h], tB, -1.0 / 16.0, MUL)
            bs = acc_group(cfg["wk"], 8, 0, 512, xr, W)
            for h in range(4):
                k.copy("act", FBc[h], bs[h][:, 0:W])
            gate_block(0)
            gate_block(512)
            v_block(0)
            v_block(512)
            for h in range(4):
                finalize(h, LGc[h], FBc[h], OGc[h])

        for g in range(len(grs)):
            to_tokmajor(KDT[g], KDc, kind, g, grs[g])

        hv = H * VC * 128
        st_dram_in, st_dram_out_p, st_dram_out_s = cfg["st_in"], cfg["st_out_p"], cfg["st_out_s"]

        def st_view(i):
            return (ST32[i][:, 0:hv].rr("p (h v) -> p h v", h=H), STB[i][:, 0:hv].rr("p (h v) -> p h v", h=H))

        for g, ch in enumerate(grs):
            if kind == "s":
                for (c0, pb, sidx) in ch:
                    S32, Sbf = st_view(sidx % 4)
                    k.dma(S32, st_dram_in[sidx].rearrange("h k v -> k h v"))
                    k.copy("act", Sbf, S32)
            atp = bank()
            for (c0, pb, sidx) in ch:
                for h in range(H):
                    k.mm(atp[pb:pb + C, h * C:(h + 1) * C], KIc[h][:, c0:c0 + C], QIc[h][:, c0:c0 + C])
            k.tt("dve", ATsb, atp[:, 0:H * C].rr("p (h c) -> p h c", h=H), msk[kind + "ui"][:].us(1).bc([128, H, C]), MUL)
            for ci, (c0, pb, sidx) in enumerate(ch):
                n = c0 // C
                if kind == "p":
                    S32, Sbf = st_view(cfg["st"])
                else:
                    S32, Sbf = st_view(sidx % 4)
                ops = bank()
                for h in range(H):
                    for jv in range(VC):
                        cc = h * VC + jv
                        k.mm(ops[:, cc * C:(cc + 1) * C], VT[g][pb:pb + C, cc * 128:(cc + 1) * 128], ATsb[pb:pb + C, h, :],
                             start=(cc == 0), stop=False, skip_group_check=True)
                for h in range(H):
                    for jv in range(VC):
                        cc = h * VC + jv
                        k.mm(ops[:, cc * C:(cc + 1) * C], Sbf[:, h, jv * 128:(jv + 1) * 128], QEc[h][:, c0:c0 + C],
                             start=False, stop=True, skip_group_check=True)
                evac(OT[:, :, c0:c0 + C], ops[:, 0:8 * C].rr("p (c w) -> p c w", c=8))
                sps = bank(2)
                for h in range(H):
                    k.mm(sps[:, h * VC * 128:(h + 1) * VC * 128], KDT[g][pb:pb + C, h * 128:(h + 1) * 128],
                         VT[g][pb:pb + C, h * VC * 128:(h + 1) * VC * 128])
                for h in range(H):
                    k.stt(S32[:, h, :], S32[:, h, :], ELt[:, h, n:n + 1], sps[:, h * VC * 128:(h + 1) * VC * 128], MUL, ADD)
                if kind == "p":
                    k.copy("act", Sbf, S32)
                if kind == "s":
                    k.dma(st_dram_out_s[sidx].rearrange("h k v -> k h v"), S32, is_output=True)
                elif tl.last and g == len(grs) - 1 and ci == len(ch) - 1:
                    k.dma(st_dram_out_p.rearrange("h k v -> k h v"), S32, is_output=True)

        for h in range(H):
            ms = bank()
            for jv in range(VC):
                tAb = tA.bitcast(BF16)[:, 0:W]
                k.act(tAb, OTc[h * VC + jv], AF.Square)
                k.mm(ms[:, 0:W], onesb[:], tAb, start=(jv == 0), stop=(jv == VC - 1))
            k.act(tB, ms[:, 0:W], AF.Sqrt, bias=cst[:, 1:2], scale=1.0 / (VC * 128))
            k.recip(tB, tB)
            for jv in range(VC):
                cc = h * VC + jv
                k.stt(tC, OTc[cc], pc(cfg["norm"], jv), tB, MUL, MUL)
                k.tt("dve", OGc[cc], tC, GSc[cc], MUL)
        AR.reset(0)
        ln_begin(W)
        for col in (0, 512):
            bs = acc_group(cfg["wo"], 8, col, 512, lambda kc: OGc[kc], W)
            for mi in range(4):
                ln_add(col // 128 + mi, bs[mi][:, 0:W])
        ln_finish("ln1_w%d" % l, "ln1_b%d" % l)

    HGCFG = dict(hg=True, H=8, VC=1, wq=A["hg_wq"][0], wk=A["hg_wf"][0], wv=A["hg_wi"][0], wg=A["hg_wg"][0],
                 wo=A["hg_wo"][0], norm="hgn", st=1, st_in=A["state_hgrn"], st_out_p=A["p_hgrn"], st_out_s=A["s_hgrn"])
    GLCFG = dict(hg=False, H=4, VC=2, wq=A["gl_wq"][0], wk=A["gl_wk"][0], wv=A["gl_wv"][0], wg=A["gl_wg"][0],
                 wo=A["gl_wo"][0], norm="gln", st=2, st_in=A["state_gla"], st_out_p=A["p_gla"], st_out_s=A["s_gla"])


    def rwkv_mixer(l, j, tl, col0, Wm):
        kind = tl.kind
        C = 64 if kind == "p" else 4
        nch = Wm // C
        AR.reset()
        RT, RTc = AR.buf(8, Wm, BF16)
        KH, KHc = AR.buf(8, Wm, BF16)
        BH, BHc = AR.buf(8, Wm, BF16)
        KT, KTc = AR.buf(8, Wm, BF16)
        VB, VBc = AR.buf(8, Wm, BF16)
        G, Gc = AR.buf(8, Wm, BF16)
        BV, BVc = AR.buf(8, Wm, F32)
        PCt = AR.take(8 * nch * 4, F32, "p (c n) -> p c n", c=8)
        mark = AR.p
        XX, XXc = AR.buf(8, Wm, BF16)
        XR, XRc = AR.buf(8, Wm, BF16)
        XK, XKc = AR.buf(8, Wm, BF16)
        XV, XVc = AR.buf(8, Wm, BF16)
        XT, XTc = AR.buf(8, Wm, BF16)
        HW = AR.take(Wm * 2, BF16)
        HA = AR.take(Wm * 2, BF16)
        HG0 = AR.take(Wm * 2, BF16)
        HG1 = AR.take(Wm * 2, BF16)
        HV = AR.take(Wm * 2, BF16)
        RAWr, RAWrc = AR.buf(4, Wm, F32)
        RAWk, RAWkc = AR.buf(4, Wm, F32)
        RAWv, RAWvc = AR.buf(4, Wm, F32)
        t = [AR.take(Wm * 16) for _ in range(8)]
        xv_ = x32[:, :, col0:col0 + Wm]

        if kind == "p":
            k.tt("dve", XX[:, :, 1:Wm], xv_[:, :, 0:Wm - 1], xv_[:, :, 1:Wm], SUB)
            k.tt("dve", XX[:, :, 0], shiftP[j][:, :], xv_[:, :, 0], SUB)
            k.copy("dve", shiftP[j][:, :], xv_[:, :, Wm - 1])
            if tl.last and col0 + Wm == tl.W:
                store_rows(A["p_shift"][j:j + 1, :], lambda c: shiftP[j][:, c:c + 1], 1)
        else:
            SR = AR.take(4096)
            shS = AR.take(512, F32, "p (c s) -> p c s", c=8)
            k.dma(SR[0:16, :], A["state_rwkv_shift"][j])
            b = bank()
            for c in range(8):
                k.tr(b[:, c * 16:(c + 1) * 16], SR[0:16, c * 128:(c + 1) * 128], identf[0:16, 0:16])
            evac(shS, b[:, 0:128].rr("p (c s) -> p c s", c=8))
            x4 = xv_.rr("p c (s t) -> p c s t", t=4)
            XX4 = XX.rr("p c (s t) -> p c s t", t=4)
            for c in range(8):
                k.tt("dve", XX4[:, c, :, 1:4], x4[:, c, :, 0:3], x4[:, c, :, 1:4], SUB)
            k.tt("dve", XX4[:, :, :, 0], shS, x4[:, :, :, 0], SUB)
            store_rows(A["s_shift"][j], lambda c: x32[:, c, 3:64:4], 16)
        mc = pcol["mix%d" % j]

        def mix(dst_c, jj):
            for c in range(8):
                k.stt(dst_c[c], XXc[c], P[:, mc + jj * 8 + c:mc + jj * 8 + c + 1], x32[:, c, col0:col0 + Wm], MUL, ADD)

        def lora1(w3d, m, dst, pb, func, src_c):
            wb = wload(w3d, 128, 8, m)
            b = bank()
            for kc in range(8):
                k.mm(b[pb:pb + m, 0:Wm], wb[:, kc, :], src_c[kc], start=(kc == 0), stop=(kc == 7))
            k.act(dst[pb:pb + m, :], b[pb:pb + m, 0:Wm], func)

        mix(XTc, 1)
        lora1(A["rw_w1"][j].rearrange("(k p) m -> p k m", p=128), 64, HW, 0, AF.Tanh, XTc)
        mix(XTc, 4)
        lora1(A["rw_a1"][j].rearrange("(k p) m -> p k m", p=128), 64, HA, 0, AF.Copy, XTc)
        mix(XTc, 5)
        g1v = A["rw_g1"][j].rearrange("(k p) m -> p k m", p=128)
        lora1(g1v[:, :, 0:64], 64, HG0, 0, AF.Sigmoid, XTc)
        lora1(g1v[:, :, 64:128], 64, HG0, 64, AF.Sigmoid, XTc)
        lora1(g1v[:, :, 128:160], 32, HG1, 0, AF.Sigmoid, XTc)
        mix(XRc, 0)
        mix(XKc, 2)
        mix(XVc, 3)
        if j > 0:
            lora1(A["rw_v1"][0].rearrange("(k p) m -> p k m", p=128), 32, HV, 0, AF.Copy, XVc)

        def wrow(w2d, r0, r1, col):
            return wload(w2d[r0:r1, col:col + 512].rearrange("p (o m) -> p o m", o=1), r1 - r0, 1, 512)[:, 0, :]

        def g4(v):
            return v.rr("p (c w) -> p c w", c=4)

        def pb4(name, c0):
            cc = pcol[name] + c0
            return P[:, cc:cc + 4].us(2).bc([128, 4, Wm])

        def reg(d, mi):
            return d[:, mi * Wm:(mi + 1) * Wm]

        def st(mi):
            return (mi * Wm) % 512 == 0

        W4 = 4 * Wm
        for gq in range(2):
            col = gq * 512
            c4 = gq * 4
            for (wn, xs, raw) in (("rw_wr", XRc, RAWr), ("rw_wk", XKc, RAWk), ("rw_wv", XVc, RAWv)):
                d = bank(2)
                for kc in range(8):
                    wb = wmat(A[wn][j], kc, col, 512)
                    for mi in range(4):
                        k.mm(reg(d, mi), wb[:, mi * 128:(mi + 1) * 128], xs[kc], start=(kc == 0 and st(mi)), stop=(kc == 7),
                             skip_group_check=True)
                evac(raw, g4(d[:, 0:W4]))
            r4, k4, v4 = RAWr, RAWk, RAWv
            T0, T1, T2, T3, T4, T5, T6, T7 = [g4(x) for x in t]
            d = bank(2)
            wb = wrow(A["rw_w2"][j], 0, 64, col)
            for mi in range(4):
                k.mm(reg(d, mi), wb[:, mi * 128:(mi + 1) * 128], HW[0:64, :], start=st(mi), stop=True, skip_group_check=True)
            k.tt("dve", T0, g4(d[:, 0:W4]), pb4("w0%d" % j, c4), ADD)
            k.act(T0, T0, AF.Sigmoid)
            d = bank(2)
            wb = wrow(A["rw_a2"][j], 0, 64, col)
            for mi in range(4):
                k.mm(reg(d, mi), wb[:, mi * 128:(mi + 1) * 128], HA[0:64, :], start=st(mi), stop=True, skip_group_check=True)
            k.tt("dve", T1, g4(d[:, 0:W4]), pb4("a0%d" % j, c4), ADD)
            k.act(T1, T1, AF.Sigmoid)
            if j == 0:
                k.copy("act", vf[:, c4:c4 + 4, col0:col0 + Wm], v4)
            else:
                d = bank(2)
                wb = wrow(A["rw_v2"][0], 0, 32, col)
                for mi in range(4):
                    k.mm(reg(d, mi), wb[:, mi * 128:(mi + 1) * 128], HV[0:32, :], start=st(mi), stop=True, skip_group_check=True)
                k.tt("dve", T2, g4(d[:, 0:W4]), pb4("v0", c4), ADD)
                k.act(T2, T2, AF.Sigmoid)
                k.tt("dve", T3, vf[:, c4:c4 + 4, col0:col0 + Wm], v4, SUB)
                k.tt("dve", T3, T3, T2, MUL)
                k.tt("dve", v4, v4, T3, ADD)
            k.copy("act", VB[:, c4:c4 + 4, :], v4)
            d = bank(2)
            wb = wrow(A["rw_g2"][j], 0, 128, col)
            for mi in range(4):
                k.mm(reg(d, mi), wb[:, mi * 128:(mi + 1) * 128], HG0[:, :], start=st(mi), stop=False, skip_group_check=True)
            wb = wrow(A["rw_g2"][j], 128, 160, col)
            for mi in range(4):
                k.mm(reg(d, mi), wb[:, mi * 128:(mi + 1) * 128], HG1[0:32, :], start=False, stop=True, skip_group_check=True)
            k.copy("act", G[:, c4:c4 + 4, :], g4(d[:, 0:W4]))
            k.tt("dve", T2, k4, pb4("kk%d" % j, c4), MUL)
            sb = t[7].bitcast(BF16)[:, 0:W4]
            k.act(g4(sb), T2, AF.Square)
            d = bank(2)
            for mi in range(4):
                k.mm(reg(d, mi), bonesb[:], sb[:, mi * Wm:(mi + 1) * Wm], start=st(mi), stop=True, skip_group_check=True)
            k.act(T3, g4(d[:, 0:W4]), AF.Ln, bias=cst[:, 7:8])
            k.act(T3, T3, AF.Exp, scale=-0.5)
            k.tt("dve", T2, T2, T3, MUL)
            k.tt("dve", T3, T1, pb4("ka%d" % j, c4), MUL)
            k.tt("dve", T3, T3, pb4("oka%d" % j, c4), ADD)
            k.tt("dve", T4, k4, T3, MUL)
            k.tt("dve", T5, T2, T1, MUL)
            k.scan(t[6], rmask[kind][:, 0:W4], t[0], 0.0, MUL, ADD)
            k.act(T7, T6, AF.Exp, scale=-C0)
            k.tt("dve", RT[:, c4:c4 + 4, :], r4, T7, MUL)
            k.copy("dve", PCt[:, c4:c4 + 4, :], T7.rr("p c (n q) -> p c n q", q=C)[:, :, :, C - 1])
            k.tt("dve", T3, T6, T0, SUB)
            k.act(T3, T3, AF.Exp, scale=-C0)
            k.tt("dve", KT[:, c4:c4 + 4, :], T2, T3, MUL)
            k.act(T7, T6, AF.Exp, scale=C0)
            k.tt("dve", KH[:, c4:c4 + 4, :], T4, T7, MUL)
            k.tt("dve", BH[:, c4:c4 + 4, :], T5, T7, MUL)
            k.tt("dve", T3, r4, pb4("rk%d" % j, c4), MUL)
            sb = t[7].bitcast(BF16)[:, 0:W4]
            k.tt("dve", g4(sb), T3, T4, MUL)
            d = bank(2)
            for mi in range(4):
                k.mm(reg(d, mi), bonesb[:], sb[:, mi * Wm:(mi + 1) * Wm], start=st(mi), stop=True, skip_group_check=True)
            k.tt("dve", BV[:, c4:c4 + 4, :], g4(d[:, 0:W4]), v4, MUL)

        STOP = 9
        if STOP <= 1:
            return
        AR.reset(mark)
        grs = groups_of(kind, Wm)
        VT = [AR.take(2048, BF16) for _ in grs]
        KHT = [AR.take(2048, BF16) for _ in grs]
        BHT = [AR.take(2048, BF16) for _ in grs]
        n_am = len(grs) if kind == "p" else 1
        ams = [[AR.take(16 * C * 2, BF16, "p (h c) -> p h c", h=16) for _ in range(8)] for _ in range(n_am)]
        RHSsb = AR.take(2048, BF16)
        Usb = AR.take(2048, BF16)
        YT, YTc = AR.buf(8, Wm, F32)
        YG, YGc = AR.buf(8, Wm, BF16)
        Slds = [AR.take(4096), AR.take(4096)] if kind == "s" else [None, None]
        Sst = AR.take(4096)
        for g in range(len(grs)):
            to_tokmajor(VT[g], VBc, kind, g, grs[g])
            to_tokmajor(KHT[g], KHc, kind, g, grs[g])
            to_tokmajor(BHT[g], BHc, kind, g, grs[g])

        if STOP <= 2:
            return

        def hd(h):
            return h // 2, (h % 2) * 64

        def hs(h):
            return (h % 2) * 8 + h // 2

        def v3(d):
            return d[:, 0:1024].rr("p (h c) -> p h c", h=16)[:, :, 0:C]

        def st_view(i):
            return (ST32[i][:, 0:512].rr("p (c v) -> p c v", c=8),
                    (STB[i][:, 0:512].rr("p (c v) -> p c v", c=8), STB[i][:, 512:1024].rr("p (c v) -> p c v", c=8)))

        def mask_state(H32, Hm):
            k.ts("dve", Hm[0], H32, cst[:, 5:6], MUL)
            k.ts("dve", Hm[1], H32, cst[:, 6:7], MUL)

        def phase1(items):
            def amat(ch, dst, lh, rh, mk):
                d = bank(2)
                for (c0, pb, sidx) in ch:
                    for h in range(16):
                        c, hb = hd(h)
                        k.mm(d[pb:pb + C, hs(h) * 64:hs(h) * 64 + C], lh[c][hb:hb + 64, c0:c0 + C], rh[c][hb:hb + 64, c0:c0 + C])
                k.tt("dve", dst, v3(d), msk[kind + mk][:].us(1).bc([128, 16, C]), MUL)

            def mm3(ch, lh, rh):
                d = bank(2)
                for (c0, pb, sidx) in ch:
                    for h in range(16):
                        k.mm(d[pb:pb + C, h * 64:h * 64 + C], lh[pb:pb + C, h, :], rh[pb:pb + C, h, :])
                return d

            for it in items:
                Msb, Nsb, Xs, M2, N2, AkkT, ArkT, ArbT = it["am"]
                ch = it["ch"]
                amat(ch, Msb, BHc, KTc, "su")
                amat(ch, Nsb, KTc, BHc, "sl")
                amat(ch, AkkT, KHc, KTc, "su")
                amat(ch, ArkT, KHc, RTc, "ui")
                amat(ch, ArbT, BHc, RTc, "ui")
                k.stt(Xs, Msb, -1.0, msk[kind + "id"][:].us(1).bc([128, 16, C]), MUL, ADD)
                it["p"] = [Msb, Nsb, M2, N2]
            p_ = 2
            while p_ < C:
                lastlv = (p_ * 2 >= C)
                for it in items:
                    Mp, Nn, Mo, No = it["p"]
                    d = mm3(it["ch"], Mp, Nn)
                    evac(No, v3(d))
                if not lastlv:
                    for it in items:
                        Mp, Nn, Mo, No = it["p"]
                        d = mm3(it["ch"], Nn, Mp)
                        evac(Mo, v3(d))
                for it in items:
                    Mp, Nn, Mo, No = it["p"]
                    Xs = it["am"][2]
                    d = mm3(it["ch"], No, Xs)
                    k.tt("dve", Xs, v3(d), Xs, ADD)
                    it["p"] = [Mo, No, Mp, Nn]
                p_ *= 2

        if kind == "p":
            items_all = [dict(ch=ch, am=ams[g]) for g, ch in enumerate(grs)]
            phase1(items_all)
        for g, ch in enumerate(grs):
            if kind == "p":
                it = items_all[g]
            else:
                it = dict(ch=ch, am=ams[0])
                phase1([it])
            Msb, Nsb, Xs, M2, N2, AkkT, ArkT, ArbT = it["am"]
            if kind == "s":
                for (c0, pb, sidx) in ch:
                    H32, Hbf = st_view(sidx % 4)
                    Sl = Slds[sidx % 2]
                    k.dma(Sl[0:64, :].rr("p (h k) -> p h k", h=16), A["state_rwkv"][j, sidx].rearrange("h v k -> v h k"))
                    b = bank()
                    for c in range(8):
                        k.tr(b[:, c * 64:(c + 1) * 64], Sl[0:64, c * 128:(c + 1) * 128], identf[0:64, 0:64])
                    evac(H32, b[:, 0:512].rr("p (c v) -> p c v", c=8))
                    mask_state(H32, Hbf)
            for ci, (c0, pb, sidx) in enumerate(ch):
                n = c0 // C
                if kind == "p":
                    H32, Hbf = st_view(0 if j == 0 else 3)
                else:
                    H32, Hbf = st_view(sidx % 4)
                d = bank(2)
                for h in range(16):
                    c, hb = hd(h)
                    k.mm(d[pb:pb + C, h * 64:(h + 1) * 64], KTc[c][:, c0:c0 + C], Hbf[h % 2][:, c, :],
                         start=(h % 8 == 0), stop=False, skip_group_check=True)
                for h in range(16):
                    k.mm(d[pb:pb + C, h * 64:(h + 1) * 64], AkkT[pb:pb + C, hs(h), :], VT[g][pb:pb + C, h * 64:(h + 1) * 64],
                         start=False, stop=True, skip_group_check=True)
                k.copy("act", RHSsb[pb:pb + C, :], d[pb:pb + C, 0:1024])
                d = bank(2)
                for h in range(16):
                    k.mm(d[pb:pb + C, h * 64:(h + 1) * 64], Xs[pb:pb + C, hs(h), :], RHSsb[pb:pb + C, h * 64:(h + 1) * 64])
                k.act(Usb[pb:pb + C, :], d[pb:pb + C, 0:1024], AF.Identity, scale=-1.0)
                yps = bank()
                for h in range(16):
                    c, hb = hd(h)
                    k.mm(yps[hb:hb + 64, c * C:(c + 1) * C], Hbf[h % 2][:, c, :], RTc[c][:, c0:c0 + C],
                         start=(h < 2), stop=False, skip_group_check=True)
                for h in range(16):
                    c, hb = hd(h)
                    o = yps[hb:hb + 64, c * C:(c + 1) * C]
                    k.mm(o, VT[g][pb:pb + C, h * 64:(h + 1) * 64], ArkT[pb:pb + C, hs(h), :], start=False, stop=False, skip_group_check=True)
                    k.mm(o, Usb[pb:pb + C, h * 64:(h + 1) * 64], ArbT[pb:pb + C, hs(h), :], start=False, stop=True, skip_group_check=True)
                evac(YT[:, :, c0:c0 + C], yps[:, 0:8 * C].rr("p (c w) -> p c w", c=8))
                hps = bank()
                for h in range(16):
                    c, hb = hd(h)
                    o = hps[hb:hb + 64, c * 64:(c + 1) * 64]
                    k.mm(o, KHT[g][pb:pb + C, c * 128 + hb:c * 128 + hb + 64], VT[g][pb:pb + C, h * 64:(h + 1) * 64], start=True, stop=False)
                    k.mm(o, BHT[g][pb:pb + C, c * 128 + hb:c * 128 + hb + 64], Usb[pb:pb + C, h * 64:(h + 1) * 64], start=False, stop=True)
                k.tt("dve", H32, hps[:, 0:512].rr("p (c v) -> p c v", c=8), H32, ADD)
                k.tt("dve", H32, H32, PCt[:, :, n:n + 1].bc([128, 8, 64]), MUL)
                if kind == "p":
                    mask_state(H32, Hbf)
                fin_p = (kind == "p" and tl.last and col0 + Wm == tl.W and g == len(grs) - 1 and ci == len(ch) - 1)
                if kind == "s" or fin_p:
                    dram = A["s_rwkv"][j, sidx] if kind == "s" else A["p_rwkv"][j]
                    d = bank(2)
                    for c in range(8):
                        k.tr(d[0:64, c * 128:(c + 1) * 128], H32[:, c, :], identf[:])
                    evac(Sst[0:64, :], d[0:64, 0:1024])
                    k.dma(dram.rearrange("h v k -> v h k"), Sst[0:64, :].rr("p (h k) -> p h k", h=16), is_output=True)

        if STOP <= 5:
            return
        tq0 = AR.take(Wm * 16)
        tq1 = AR.take(Wm * 16)
        Q0, Q1 = g4(tq0), g4(tq1)
        for gq in range(2):
            c4 = gq * 4
            y4 = YT[:, c4:c4 + 4, :]
            d = bank(2)
            for mi in range(4):
                k.mm(reg(d, mi), bones[:], YTc[c4 + mi], start=st(mi), stop=True, skip_group_check=True)
            k.stt(Q0, g4(d[:, 0:W4]), -1.0 / 64, y4, MUL, ADD)
            qb = tq1.bitcast(BF16)[:, 0:W4]
            k.act(g4(qb), Q0, AF.Square)
            d = bank(2)
            for mi in range(4):
                k.mm(reg(d, mi), bonesb[:], qb[:, mi * Wm:(mi + 1) * Wm], start=st(mi), stop=True, skip_group_check=True)
            k.act(Q1, g4(d[:, 0:W4]), AF.Ln, bias=cst[:, 2:3], scale=1.0 / 64)
            k.act(Q1, Q1, AF.Exp, scale=-0.5)
            k.tt("dve", Q0, Q0, Q1, MUL)
            k.tt("dve", Q0, Q0, pb4("lnxw%d" % j, c4), MUL)
            k.tt("dve", Q0, Q0, pb4("lnxb%d" % j, c4), ADD)
            k.tt("dve", Q0, Q0, BV[:, c4:c4 + 4, :], ADD)
            k.tt("dve", YG[:, c4:c4 + 4, :], Q0, G[:, c4:c4 + 4, :], MUL)
        ln_begin(Wm)
        for col in (0, 512):
            bs = acc_group(A["rw_wo"][j], 8, col, 512, lambda kc: YGc[kc], Wm)
            for mi in range(4):
                ln_add(col // 128 + mi, bs[mi][:, 0:Wm], col0)
        ln_finish("ln1_w%d" % l, "ln1_b%d" % l, col0)

    def mk_tiles():
        tiles = []
        for ti in range(4):
            tl = Tl()
            tl.kind, tl.W, tl.nseq, tl.L, tl.t0, tl.last = "p", 512, 1, 512, ti * 512, (ti == 3)
            tiles.append(tl)
        tl = Tl()
        tl.kind, tl.W, tl.nseq, tl.L, tl.t0, tl.last = "s", 64, 16, 4, 0, True
        tiles.append(tl)
        return tiles

    def run_all(layers=(0, 1, 2, 3), tiles=None):
        for tl in (tiles or mk_tiles()):
            load_tile(tl)
            row0 = tl.t0 if tl.kind == "p" else 2048
            for l in layers:
                if l % 3 == 0:
                    if tl.kind == "p":
                        for half in (0, 256):
                            rwkv_mixer(l, l // 3, tl, half, 256)
                    else:
                        rwkv_mixer(l, l // 3, tl, 0, 64)
                elif l == 1:
                    gla_like(l, tl, HGCFG)
                else:
                    gla_like(l, tl, GLCFG)
                dbg_dump2(2 * l, tl, row0)
                ffn_sublayer(l, tl)
                dbg_dump2(2 * l + 1, tl, row0)
            if tl.kind == "p":
                store_tile(tl, A["y_prompt"], 0)
            else:
                store_tile(tl, A["y_sample"], 0)

    def dbg_dump2(idx, tl, row0):
        if not dbg_n or idx >= dbg_n:
            return
        AR.reset()
        if tl.kind == "p":
            for jj in range(4):
                store_rows(A["dbg"][idx][row0 + jj * 128: row0 + (jj + 1) * 128, :],
                           lambda c, jj=jj: x32[:, c, jj * 128:(jj + 1) * 128], 128)
        else:
            store_rows(A["dbg"][idx][2048:2112, :], lambda c: x32[:, c, 0:64], 64)

    return k, A, locals()


_NC_CACHE = {}


def _get_nc():
    if "nc" not in _NC_CACHE:
        nc = bass.Bass("TRN2", target_bir_lowering=False)
        k, A, L = build(nc, dbg_n=0)
        L["run_all"]()
        k.finish()
        k.close()
        _NC_CACHE["nc"] = nc
    return _NC_CACHE["nc"]


def kernel(**inputs):
    n = 8
    f32 = np.float32
    inp = {kk: np.asarray(v) for kk, v in inputs.items()}
    in_maps = []
    for c in range(n):
        s = slice(16 * c, 16 * (c + 1))
        m = {}
        m["x_prompt"] = np.ascontiguousarray(inp["x_prompt"][c], dtype=f32)
        m["x_sample"] = np.ascontiguousarray(inp["x_sample"][s], dtype=f32).reshape(64, 1024)
        m["state_rwkv"] = np.ascontiguousarray(inp["state_rwkv"][:, s], dtype=f32)
        m["state_rwkv_shift"] = np.ascontiguousarray(inp["state_rwkv_shift"][:, s], dtype=f32)
        m["state_hgrn"] = np.ascontiguousarray(inp["state_hgrn"][0, s], dtype=f32)
        m["state_gla"] = np.ascontiguousarray(inp["state_gla"][0, s], dtype=f32)
        m["state_ffn_conv"] = np.ascontiguousarray(inp["state_ffn_conv"][:, s], dtype=f32)
        for w in WEIGHT_SHAPES:
            m[w] = np.ascontiguousarray(inp[w], dtype=f32)
        in_maps.append(m)
    nc = _get_nc()
    res = run_bass_kernel_spmd(nc, in_maps, core_ids=list(range(n)))
    R = res.results
    y_prompt = np.stack([R[c]["y_prompt"] for c in range(n)], 0)
    y_sample = np.concatenate([R[c]["y_sample"].reshape(16, 4, 1024) for c in range(n)], 0)
    p_rwkv = np.stack([R[c]["p_rwkv"] for c in range(n)], 1)
    p_shift = np.stack([R[c]["p_shift"] for c in range(n)], 1)
    p_hgrn = np.stack([R[c]["p_hgrn"] for c in range(n)], 0)[None]
    p_gla = np.stack([R[c]["p_gla"] for c in range(n)], 0)[None]
    p_conv = np.stack([R[c]["p_conv"] for c in range(n)], 1)
    s_rwkv = np.concatenate([R[c]["s_rwkv"] for c in range(n)], 1)
    s_shift = np.concatenate([R[c]["s_shift"] for c in range(n)], 1)
    s_hgrn = np.concatenate([R[c]["s_hgrn"] for c in range(n)], 0)[None]
    s_gla = np.concatenate([R[c]["s_gla"] for c in range(n)], 0)[None]
    s_conv = np.concatenate([R[c]["s_conv"] for c in range(n)], 1)
    outs = (y_prompt, y_sample, p_rwkv, p_shift, p_hgrn, p_gla, p_conv, s_rwkv, s_shift, s_hgrn, s_gla, s_conv)
    return tuple(np.ascontiguousarray(o, dtype=f32) for o in outs)
```

```python
import contextlib
import numpy as np
import concourse.bass as bass
import concourse.mybir as mybir

F32 = mybir.dt.float32
BF16 = mybir.dt.bfloat16
AF = mybir.ActivationFunctionType
ALU = mybir.AluOpType
AX = mybir.AxisListType

SAME_ENGINE_SYNC = True
N_DMA_SEMS = 40


class T:
    def __init__(self, tile, name):
        self.t = tile
        self.name = name
        self.lw = None
        self.rd = {}

    def __getitem__(self, idx):
        return V(self.t[idx], self)

    def sub(self, name):
        return T(self.t, self.name + "." + name)


class V:
    def __init__(self, ap, owner):
        self.ap = ap
        self.o = owner

    def __getitem__(self, idx):
        return V(self.ap[idx], self.o)

    def rr(self, pat, **kw):
        return V(self.ap.rearrange(pat, **kw), self.o)

    def bc(self, shape):
        return V(self.ap.broadcast_to(shape), self.o)

    def us(self, axis):
        return V(self.ap.unsqueeze(axis), self.o)

    def bitcast(self, dt):
        return V(self.ap.bitcast(dt), self.o)

    @property
    def shape(self):
        return self.ap.shape


def _own(vs):
    r = []
    for v in vs:
        if isinstance(v, V):
            if isinstance(v.o, (list, tuple)):
                r.extend(v.o)
            else:
                r.append(v.o)
    return r


class KB:
    def __init__(self, nc):
        self.nc = nc
        self.es = contextlib.ExitStack()
        self.eng = {"pe": nc.tensor, "act": nc.scalar, "dve": nc.vector, "pool": nc.gpsimd, "sp": nc.sync}
        self.sem = {}
        self.cnt = {}
        self.waited = {e: {} for e in self.eng}
        for e in self.eng:
            self.sem[e] = self.es.enter_context(nc.semaphore("s_" + e))
            self.cnt[e] = 0
        self.dsem = [self.es.enter_context(nc.semaphore("d%d" % i)) for i in range(N_DMA_SEMS)]
        self.dcnt = [0] * N_DMA_SEMS
        self.ndma = 0
        self.n_ins = 0
        self.n_wait = 0
        self.out_events = []

    def sb(self, name, shape, dt=F32):
        return T(self.es.enter_context(self.nc.sbuf_tensor(name, list(shape), dt)), name)

    def ps(self, name, shape, dt=F32):
        return T(self.es.enter_context(self.nc.psum_tensor(name, list(shape), dt)), name)

    def _wait(self, e, ev):
        en, sem, val, sid = ev
        if en == e and (e == "pe" or not SAME_ENGINE_SYNC):
            return
        w = self.waited[e]
        if w.get(sid, 0) >= val:
            return
        w[sid] = val
        self.eng[e].wait_ge(sem, val)
        self.n_wait += 1

    def _sync(self, e, reads, writes):
        for t in reads:
            if t.lw is not None:
                self._wait(e, t.lw)
        for t in writes:
            if t.lw is not None:
                self._wait(e, t.lw)
            for ev in t.rd.values():
                self._wait(e, ev)

    def _post(self, e, ev, reads, writes, rkey=None):
        for t in reads:
            t.rd[rkey or e] = ev
        for t in writes:
            t.lw = ev
            t.rd = {}

    def emit(self, e, fn, reads, writes):
        reads = _own(reads)
        writes = _own(writes)
        self._sync(e, reads, writes)
        ins = fn()
        self.cnt[e] += 1
        ins.then_inc(self.sem[e], 1)
        ev = (e, self.sem[e], self.cnt[e], "e_" + e)
        self._post(e, ev, reads, writes)
        self.n_ins += 1
        return ev

    def mm(self, out, lhsT, rhs, start=True, stop=True, **kw):
        return self.emit("pe", lambda: self.nc.tensor.matmul(out.ap, lhsT=lhsT.ap, rhs=rhs.ap, start=start, stop=stop, **kw),
                         [lhsT, rhs], [out])

    def tr(self, out, in_, ident):
        return self.emit("pe", lambda: self.nc.tensor.transpose(out.ap, in_.ap, ident.ap), [in_, ident], [out])

    def act(self, out, in_, func, bias=0.0, scale=1.0, e="act"):
        b = bias.ap if isinstance(bias, V) else bias
        s = scale.ap if isinstance(scale, V) else scale
        return self.emit("act", lambda: self.nc.scalar.activation(out=out.ap, in_=in_.ap, func=func, bias=b, scale=s),
                         [in_, bias, scale], [out])

    def tt(self, e, out, in0, in1, op):
        return self.emit(e, lambda: self.eng[e].tensor_tensor(out=out.ap, in0=in0.ap, in1=in1.ap, op=op), [in0, in1], [out])

    def ts(self, e, out, in0, s1, op0, s2=None, op1=None):
        a1 = s1.ap if isinstance(s1, V) else s1
        a2 = s2.ap if isinstance(s2, V) else s2
        kw = {}
        if op1 is not None:
            kw["op1"] = op1
        return self.emit(e, lambda: self.eng[e].tensor_scalar(out=out.ap, in0=in0.ap, scalar1=a1, scalar2=a2, op0=op0, **kw),
                         [in0, s1, s2], [out])

    def stt(self, out, in0, scalar, in1, op0, op1):
        a = scalar.ap if isinstance(scalar, V) else scalar
        return self.emit("dve", lambda: self.nc.vector.scalar_tensor_tensor(out=out.ap, in0=in0.ap, scalar=a, in1=in1.ap, op0=op0, op1=op1),
                         [in0, scalar, in1], [out])

    def copy(self, e, out, in_):
        if e == "act":
            return self.act(out, in_, AF.Copy)
        return self.emit(e, lambda: self.eng[e].tensor_copy(out=out.ap, in_=in_.ap), [in_], [out])

    def memset(self, e, out, val):
        return self.emit(e, lambda: self.eng[e].memset(out.ap, val), [], [out])

    def scan(self, out, d0, d1, init, op0, op1):
        i = init.ap if isinstance(init, V) else init
        return self.emit("dve", lambda: self.nc.vector.tensor_tensor_scan(out=out.ap, data0=d0.ap, data1=d1.ap, initial=i, op0=op0, op1=op1),
                         [d0, d1, init], [out])

    def recip(self, out, in_):
        return self.emit("dve", lambda: self.nc.vector.reciprocal(out=out.ap, in_=in_.ap), [in_], [out])

    def reduce(self, out, in_, op=ALU.add, axis=AX.X):
        return self.emit("dve", lambda: self.nc.vector.tensor_reduce(out=out.ap, in_=in_.ap, axis=axis, op=op), [in_], [out])

    def dma(self, out, in_, is_output=False, extra_reads=(), extra_writes=(), q="sp", **kw):
        e = "pool" if (is_output and q == "sp") else q
        reads = _own([in_]) + list(extra_reads)
        writes = _own([out]) + list(extra_writes)
        self._sync(e, reads, writes)
        j = self.ndma % N_DMA_SEMS
        self.ndma += 1
        sem = self.dsem[j]
        if self.dcnt[j] > 0:
            self._wait(e, ("dma", sem, self.dcnt[j], "d%d" % j))
        self.dcnt[j] += 16
        oa = out.ap if isinstance(out, V) else out
        ia = in_.ap if isinstance(in_, V) else in_
        ins = self.eng[e].dma_start(out=oa, in_=ia, **kw)
        ins.then_inc(sem, 16)
        ev = ("dma", sem, self.dcnt[j], "d%d" % j)
        self._post(e, ev, reads, writes, rkey="dma%d" % self.ndma)
        if is_output:
            self.out_events.append(ev)
        self.n_ins += 1
        return ev

    def finish(self):
        for ev in self.out_events:
            self._wait("sp", ev)
        for j in range(N_DMA_SEMS):
            if self.dcnt[j] > 0:
                self._wait("sp", ("dma", self.dsem[j], self.dcnt[j], "d%d" % j))
        for e in ("pe", "act", "dve", "pool"):
            if self.cnt[e] > 0:
                self._wait("sp", (e, self.sem[e], self.cnt[e], "e_" + e))

    def close(self):
        self.es.close()


import math
from concourse.bass_utils import run_bass_kernel_spmd

DN_ALPHA = 8.0 ** 0.25
C0 = math.exp(-0.5)
D = 1024
FF = 2816
NF = 22
LN_EPS = 1e-5
RMS_EPS = 1e-5
LNX_EPS = 64e-5
MUL, ADD, SUB = ALU.mult, ALU.add, ALU.subtract

WEIGHT_SHAPES = {
    "rw_mix": (2, 6, 1024), "rw_wr": (2, 1024, 1024), "rw_wk": (2, 1024, 1024), "rw_wv": (2, 1024, 1024),
    "rw_wo": (2, 1024, 1024), "rw_w0": (2, 1024), "rw_w1": (2, 1024, 64), "rw_w2": (2, 64, 1024),
    "rw_a0": (2, 1024), "rw_a1": (2, 1024, 64), "rw_a2": (2, 64, 1024), "rw_g1": (2, 1024, 160),
    "rw_g2": (2, 160, 1024), "rw_k_k": (2, 1024), "rw_k_a": (2, 1024), "rw_r_k": (2, 16, 64),
    "rw_lnx_w": (2, 1024), "rw_lnx_b": (2, 1024), "rw_v0": (1, 1024), "rw_v1": (1, 1024, 32),
    "rw_v2": (1, 32, 1024), "hg_wq": (1, 1024, 1024), "hg_wf": (1, 1024, 1024), "hg_wi": (1, 1024, 1024),
    "hg_wg": (1, 1024, 1024), "hg_wo": (1, 1024, 1024), "hg_norm_w": (1, 128), "hg_lb_param": (4, 1024),
    "gl_wq": (1, 1024, 512), "gl_wk": (1, 1024, 512), "gl_wv": (1, 1024, 1024), "gl_wg": (1, 1024, 1024),
    "gl_gk1": (1, 1024, 16), "gl_gk2": (1, 16, 512), "gl_gk_b": (1, 512), "gl_wo": (1, 1024, 1024),
    "gl_norm_w": (1, 256), "ffn_wu": (4, 1024, 2816), "ffn_wg": (4, 1024, 2816), "ffn_conv_w": (4, 3, 2816),
    "ffn_conv_b": (4, 2816), "ffn_wd": (4, 2816, 1024), "ln1_w": (4, 1024), "ln1_b": (4, 1024),
    "ln2_w": (4, 1024), "ln2_b": (4, 1024),
}
IN_SHAPES = {
    "x_prompt": (2048, 1024), "x_sample": (64, 1024), "state_rwkv": (2, 16, 16, 64, 64),
    "state_rwkv_shift": (2, 16, 1024), "state_hgrn": (16, 8, 128, 128), "state_gla": (16, 4, 128, 256),
    "state_ffn_conv": (4, 16, 2, 2816),
}
OUT_SHAPES = {
    "y_prompt": (2048, 1024), "y_sample": (64, 1024), "p_rwkv": (2, 16, 64, 64), "p_shift": (2, 1024),
    "p_hgrn": (8, 128, 128), "p_gla": (4, 128, 256), "p_conv": (4, 2, 2816),
    "s_rwkv": (2, 16, 16, 64, 64), "s_shift": (2, 16, 1024), "s_hgrn": (16, 8, 128, 128),
    "s_gla": (16, 4, 128, 256), "s_conv": (4, 16, 2, 2816),
}


class Arena:
    def __init__(self, k, name, nkb):
        self.t = k.es.enter_context(k.nc.sbuf_tensor(name, [128, nkb * 256], F32))
        self.slots = [T(self.t, "%s%d" % (name, i)) for i in range(nkb)]
        self.nkb = nkb
        self.p = 0

    def reset(self, p=0):
        self.p = p

    def take(self, nbytes, dt=F32, pat=None, parts=128, **kw):
        nkb = (nbytes + 1023) // 1024
        off = self.p
        self.p += nkb
        assert self.p <= self.nkb, "arena overflow %d > %d" % (self.p, self.nkb)
        ap = self.t[0:parts, off * 256: off * 256 + nbytes // 4]
        if dt == BF16:
            ap = ap.bitcast(BF16)
        if pat:
            ap = ap.rearrange(pat, **kw)
        return V(ap, self.slots[off:off + nkb])

    def buf(self, n, w, dt=F32):
        es = 2 if dt == BF16 else 4
        off = self.p
        full = self.take(n * w * es, dt, "p (n w) -> p n w", n=n)
        ch = []
        for c in range(n):
            b0 = c * w * es
            b1 = (c + 1) * w * es
            ch.append(V(full.ap[:, c, :], self.slots[off + b0 // 1024: off + (b1 + 1023) // 1024]))
        return full, ch


def build(nc, dbg_n=0):
    k = KB(nc)
    A = {}
    for n, s in list(IN_SHAPES.items()) + list(WEIGHT_SHAPES.items()):
        A[n] = nc.dram_tensor(n, list(s), F32, kind="ExternalInput").ap()
    for n, s in OUT_SHAPES.items():
        A[n] = nc.dram_tensor(n, list(s), F32, kind="ExternalOutput").ap()
    if dbg_n:
        A["dbg"] = nc.dram_tensor("dbg", [dbg_n, 2112, 1024], F32, kind="ExternalOutput").ap()

    psf_t = k.es.enter_context(nc.psum_tensor("psf", [128, 3584], F32))
    banks = [T(psf_t, "bank%d" % i) for i in range(7)]
    psb = k.ps("psb", [128, 1024], BF16)
    bp = [0]

    def bank(n=1, parts=slice(0, 128)):
        p = bp[0]
        if n == 2:
            if p % 2:
                p += 1
            if p + 2 > 6:
                p = 0
        else:
            if p >= 7:
                p = 0
        bp[0] = p + n
        return V(psf_t[parts, p * 512:(p + n) * 512], banks[p:p + n])

    identf = k.sb("identf", [128, 128])
    identb = k.sb("identb", [128, 128], BF16)
    onesf = k.sb("onesf", [128, 512])
    bones = k.sb("bones", [128, 128])
    onesb = k.sb("onesb", [128, 128], BF16)
    bonesb = k.sb("bonesb", [128, 128], BF16)
    P = k.sb("P", [128, 1024])
    x32t = k.sb("x32", [128, 8 * 512])
    xbft = k.sb("xbf", [128, 8 * 512], BF16)
    vft = k.sb("vf", [128, 8 * 512], BF16)
    WB = [k.sb("wb%d" % i, [128, 1024], BF16) for i in range(8)]
    ST32 = [k.sb("st32_%d" % i, [128, 1024]) for i in range(4)]
    STB = [k.sb("stb_%d" % i, [128, 1024], BF16) for i in range(4)]
    shiftP = [k.sb("shp%d" % i, [128, 8]) for i in range(2)]
    ZH = k.sb("zh", [128, 4 * NF * 2])
    ZHS = k.sb("zhs", [128, NF * 32])
    msk = {}
    for kind, C, blk in (("p", 64, 64), ("s", 4, 32)):
        for nm in ("su", "ui", "sl", "id"):
            msk[kind + nm] = k.sb("m_%s_%s" % (kind, nm), [128, C])
    rmask = {"p": k.sb("rm_p", [128, 1024]), "s": k.sb("rm_s", [128, 256])}
    AR = Arena(k, "ar", 118)

    k.memset("dve", onesf[:], 1.0)
    k.memset("dve", onesb[:], 1.0)
    k.memset("dve", bonesb[:], 0.0)
    k.memset("dve", bonesb[0:64, 0:64], 1.0)
    k.memset("dve", bonesb[64:128, 64:128], 1.0)
    k.memset("dve", bones[:], 0.0)
    k.memset("dve", bones[0:64, 0:64], 1.0)
    k.memset("dve", bones[64:128, 64:128], 1.0)
    k.emit("pool", lambda: nc.gpsimd.affine_select(out=identf.t[:], in_=onesf.t[:, 0:128], pattern=[[-1, 128]],
                                                   compare_op=ALU.is_equal, fill=0.0, base=0, channel_multiplier=1),
           [onesf[:]], [identf[:]])
    k.copy("dve", identb[:], identf[:])
    for kind, C, blk in (("p", 64, 64), ("s", 4, 32)):
        for nm, pat, cm, op, base in (("su", 1, -1, ALU.is_gt, 0), ("ui", 1, -1, ALU.is_ge, 0),
                                      ("sl", -1, 1, ALU.is_gt, 0), ("id", 1, -1, ALU.is_equal, 0)):
            m = msk[kind + nm]
            for b in range(128 // blk):
                k.emit("pool", lambda m=m, b=b, blk=blk, C=C, pat=pat, cm=cm, op=op: nc.gpsimd.affine_select(
                    out=m.t[b * blk:(b + 1) * blk, :], in_=onesf.t[b * blk:(b + 1) * blk, 0:C], pattern=[[pat, C]],
                    compare_op=op, fill=0.0, base=0, channel_multiplier=cm), [onesf[:]], [m[:]])
    k.memset("dve", rmask["p"][:], 1.0)
    k.memset("dve", rmask["p"][:].rr("p (n c) -> p n c", c=64)[:, :, 0:1], 0.0)
    k.memset("dve", rmask["s"][:], 1.0)
    k.memset("dve", rmask["s"][:].rr("p (n c) -> p n c", c=4)[:, :, 0:1], 0.0)
    for t_ in ST32:
        k.memset("dve", t_[:], 0.0)
    for t_ in STB:
        k.memset("dve", t_[:], 0.0)
    for t_ in shiftP:
        k.memset("dve", t_[:], 0.0)
    k.memset("dve", ZH[:], 0.0)

    plist = []

    def addp(name, ap2d, nrows):
        plist.append((name, ap2d, nrows))

    for l in range(2):
        addp("mix%d" % l, A["rw_mix"][l].rearrange("j (c p) -> (j c) p", p=128), 48)
        for nm, src in (("w0", "rw_w0"), ("a0", "rw_a0"), ("kk", "rw_k_k"), ("ka", "rw_k_a"),
                        ("lnxw", "rw_lnx_w"), ("lnxb", "rw_lnx_b")):
            addp("%s%d" % (nm, l), A[src][l].rearrange("(c p) -> c p", p=128), 8)
        addp("rk%d" % l, A["rw_r_k"][l].rearrange("(c a) b -> c (a b)", a=2), 8)
    addp("v0", A["rw_v0"][0].rearrange("(c p) -> c p", p=128), 8)
    addp("hgn", A["hg_norm_w"], 1)
    addp("lbp", A["hg_lb_param"].rearrange("l (c p) -> (l c) p", p=128), 32)
    addp("gkb", A["gl_gk_b"][0].rearrange("(c p) -> c p", p=128), 4)
    addp("gln", A["gl_norm_w"][0].rearrange("(c p) -> c p", p=128), 2)
    for l in range(4):
        addp("cw%d" % l, A["ffn_conv_w"][l].rearrange("j (c p) -> (j c) p", p=128), 66)
        addp("cb%d" % l, A["ffn_conv_b"][l].rearrange("(c p) -> c p", p=128), 22)
        for nm in ("ln1_w", "ln1_b", "ln2_w", "ln2_b"):
            addp("%s%d" % (nm, l), A[nm][l].rearrange("(c p) -> c p", p=128), 8)
    pcol = {}
    slot, row = 0, 0
    place = []
    for name, ap2d, nrows in plist:
        if row + nrows > 128:
            slot += 1
            row = 0
        pcol[name] = slot * 128 + row
        place.append((slot, row, ap2d, nrows))
        row += nrows
    nslots = slot + 1
    assert nslots * 128 + 64 <= 1024
    AR.reset()
    PR = AR.take(nslots * 512, F32, "p (s m) -> p s m", s=nslots)
    k.memset("dve", PR, 0.0)
    for (s_, r_, ap2d, nrows) in place:
        k.dma(PR[r_:r_ + nrows, s_, :], ap2d)
    for s_ in range(nslots):
        b = bank()
        k.tr(b[:, 0:128], PR[:, s_, :], identf[:])
        k.copy("dve", P[:, s_ * 128:(s_ + 1) * 128], b[:, 0:128])
    dcol = nslots * 128

    def pc(name, off=0, n=1):
        c = pcol[name] + off
        return P[:, c:c + n]

    for l in range(2):
        pcol["oka%d" % l] = dcol
        k.ts("dve", P[:, dcol:dcol + 8], pc("ka%d" % l, 0, 8), -1.0, MUL, 1.0, ADD)
        dcol += 8
    pcol["ngkb"] = dcol
    k.ts("dve", P[:, dcol:dcol + 4], pc("gkb", 0, 4), -1.0, MUL)
    dcol += 4
    pcol["lbe"] = dcol
    k.act(P[:, dcol:dcol + 32], pc("lbp", 0, 32), AF.Exp)
    lbe = P[:, dcol:dcol + 32]
    dcol += 32
    pcol["lb1"] = dcol
    pcol["olb1"] = dcol + 8
    lsum = P[:, dcol + 16:dcol + 24]
    k.tt("dve", lsum, lbe[:, 0:8], lbe[:, 8:16], ADD)
    k.tt("dve", lsum, lsum, lbe[:, 16:24], ADD)
    k.tt("dve", lsum, lsum, lbe[:, 24:32], ADD)
    k.recip(lsum, lsum)
    k.tt("dve", P[:, dcol:dcol + 8], lbe[:, 8:16], lsum, MUL)
    k.ts("dve", P[:, dcol + 8:dcol + 16], P[:, dcol:dcol + 8], -1.0, MUL, 1.0, ADD)
    dcol += 24
    assert dcol <= 1024

    x32 = x32t[:].rr("p (c w) -> p c w", c=8)
    xbf = xbft[:].rr("p (c w) -> p c w", c=8)
    vf = vft[:].rr("p (c w) -> p c w", c=8)

    wctr = [0]

    class WV:
        def __init__(self, ap, trk):
            self.ap = ap
            self.trk = trk

        def __getitem__(self, i):
            return WV(self.ap[i], self.trk)

        def rearrange(self, pat, **kw):
            return WV(self.ap.rearrange(pat, **kw), self.trk)

    class WTn:
        def __init__(self, ap, trks):
            self.ap = ap
            self.trks = trks

        def __getitem__(self, l):
            return WV(self.ap[l], self.trks[l])

    MATW = ["rw_wr", "rw_wk", "rw_wv", "rw_wo", "rw_w1", "rw_w2", "rw_a1", "rw_a2", "rw_g1", "rw_g2", "rw_v1", "rw_v2",
            "hg_wq", "hg_wf", "hg_wi", "hg_wg", "hg_wo", "gl_wq", "gl_wk", "gl_wv", "gl_wg", "gl_gk1", "gl_gk2", "gl_wo",
            "ffn_wu", "ffn_wg", "ffn_wd"]
    shadow = {}
    for n_ in MATW:
        sh = nc.dram_tensor(n_ + "_bf", list(WEIGHT_SHAPES[n_]), BF16, kind="Internal").ap()
        shadow[n_] = WTn(sh, [T(None, "%s_%d" % (n_, l_)) for l_ in range(WEIGHT_SHAPES[n_][0])])

    def cast_w(n_, l_):
        src = A[n_][l_]
        dst = shadow[n_].ap[l_]
        cols = WEIGHT_SHAPES[n_][2]
        if cols > 2048:
            src = src.rearrange("k (a b) -> (k a) b", a=4)
            dst = dst.rearrange("k (a b) -> (k a) b", a=4)
        k.dma(dst, src, extra_writes=[shadow[n_].trks[l_]], q="pool")

    RWN = ["rw_w1", "rw_a1", "rw_g1", "rw_wr", "rw_wk", "rw_wv", "rw_w2", "rw_a2", "rw_g2", "rw_wo"]
    FFN = ["ffn_wu", "ffn_wg", "ffn_wd"]
    for n_ in RWN:
        cast_w(n_, 0)
    for n_ in FFN:
        cast_w(n_, 0)
    for n_ in ["hg_wq", "hg_wf", "hg_wg", "hg_wi", "hg_wo"]:
        cast_w(n_, 0)
    for n_ in FFN:
        cast_w(n_, 1)
    for n_ in ["gl_wq", "gl_gk1", "gl_gk2", "gl_wk", "gl_wg", "gl_wv", "gl_wo"]:
        cast_w(n_, 0)
    for n_ in FFN:
        cast_w(n_, 2)
    for n_ in RWN:
        cast_w(n_, 1)
    for n_ in ["rw_v1", "rw_v2"]:
        cast_w(n_, 0)
    for n_ in FFN:
        cast_w(n_, 3)
    for n_ in MATW:
        A[n_] = shadow[n_]

    def wload(wv, p, kk, m):
        i = wctr[0]
        wctr[0] += 1
        assert kk * m <= 1024
        wb = WB[i % 8][0:p, 0:kk * m].rr("p (k m) -> p k m", k=kk)
        k.dma(wb, wv.ap, extra_reads=[wv.trk])
        return wb

    def wmat(w2d, k0, c0, m):
        return wload(w2d[k0 * 128:(k0 + 1) * 128, c0:c0 + m].rearrange("p (o m) -> p o m", o=1), 128, 1, m)[:, 0, :]

    F32R = mybir.dt.float32r

    def mmr(out, lhsT, rhs, **kw):
        return k.mm(out, lhsT, rhs, **kw)

    def r32(v):
        return v

    evac_rr = [0]

    def evac(out, in_):
        evac_rr[0] += 1
        k.copy("act" if evac_rr[0] % 2 else "dve", out, in_)

    def proj_fm(w2d, nk, ncols, rhs_fn, W, consume, col0=0):
        c = 0
        while c < ncols:
            g = min(512, ncols - c)
            nm = g // 128
            bs = [bank() for _ in range(nm)]
            for kc in range(nk):
                wb = wmat(w2d, kc, col0 + c, g)
                for mi in range(nm):
                    k.mm(bs[mi][:, 0:W], wb[:, mi * 128:(mi + 1) * 128], rhs_fn(kc), start=(kc == 0), stop=(kc == nk - 1))
            for mi in range(nm):
                consume((c // 128) + mi, bs[mi][:, 0:W])
            c += g

    def ln_sublayer(h_chunks, W, wname, bname):
        Z, Zc = AR.buf(8, W, F32)
        sq = AR.take(W * 4)
        for c in range(8):
            k.stt(Zc[c], x32[:, c, 0:W], DN_ALPHA, h_chunks[c], MUL, ADD)
        mu = bank()
        for c in range(8):
            k.mm(mu[:, 0:W], onesf[:, 0:128], Zc[c], start=(c == 0), stop=(c == 7))
        for c in range(8):
            k.stt(Zc[c], mu[:, 0:W], -1.0 / D, Zc[c], MUL, ADD)
        var = bank()
        for c in range(8):
            sqb = sq.bitcast(BF16)[:, 0:W]
            k.act(sqb, Zc[c], AF.Square)
            k.mm(var[:, 0:W], onesb[:], sqb, start=(c == 0), stop=(c == 7))
        rstd = AR.take(W * 4)
        k.act(rstd, var[:, 0:W], AF.Sqrt, bias=LN_EPS, scale=1.0 / D)
        k.recip(rstd, rstd)
        for c in range(8):
            k.tt("dve", Zc[c], Zc[c], rstd, MUL)
            k.ts("dve", x32[:, c, 0:W], Zc[c], pc(wname, c), MUL, pc(bname, c), ADD)
            k.copy("act", xbf[:, c, 0:W], x32[:, c, 0:W])


    cst = k.sb("cst", [128, 8])
    k.memset("dve", cst[:, 0:1], LN_EPS)
    k.memset("dve", cst[:, 1:2], RMS_EPS)
    k.memset("dve", cst[:, 2:3], LNX_EPS)
    k.memset("dve", cst[:, 3:4], 1.0)
    k.memset("dve", cst[:, 4:5], 0.0)
    k.memset("dve", cst[:, 7:8], 1e-18)
    k.memset("dve", cst[:, 5:7], 0.0)
    k.memset("dve", cst[0:64, 5:6], 1.0)
    k.memset("dve", cst[64:128, 6:7], 1.0)

    class Tl:
        pass

    def acc_group(w2d, nk, col, g, rhs_fn, W, extra=None):
        nm = (g + 127) // 128
        bs = [bank() for _ in range(nm)]
        for kc in range(nk):
            if nk % 2 == 0:
                if kc % 2 == 0:
                    wb2 = wload(w2d[kc * 128:(kc + 2) * 128, col:col + g].rearrange("(o p) m -> p o m", p=128), 128, 2, g)
                wb = wb2[:, kc % 2, :]
            else:
                wb = wmat(w2d, kc, col, g)
            for mi in range(nm):
                mw = min(128, g - mi * 128)
                k.mm(bs[mi][0:mw, 0:W], wb[:, mi * 128:mi * 128 + mw], rhs_fn(kc), start=(kc == 0), stop=(kc == nk - 1))
            if extra is not None:
                extra(kc, wb)
        return bs

    ln_state = {}

    def ln_begin(W):
        Z, Zc = AR.buf(8, W, F32)
        ln_state["Zc"] = Zc
        ln_state["Z"] = Z
        ln_state["W"] = W

    def ln_add(c, h_ps, col0=0):
        W = ln_state["W"]
        k.stt(ln_state["Zc"][c], x32[:, c, col0:col0 + W], DN_ALPHA, h_ps, MUL, ADD)

    def ln_finish(wname, bname, col0=0):
        W = ln_state["W"]
        Zc = ln_state["Zc"]
        sq = [AR.take(W * 2, BF16) for _ in range(2)]
        mean = AR.take(W * 4)
        rstd = AR.take(W * 4)
        mu = bank()
        ss = bank()
        for c in range(8):
            k.act(sq[c % 2], Zc[c], AF.Square)
            k.mm(mu[:, 0:W], onesf[:, 0:128], Zc[c], start=(c == 0), stop=(c == 7))
            k.mm(ss[:, 0:W], onesb[:], sq[c % 2], start=(c == 0), stop=(c == 7))
        k.act(mean, mu[:, 0:W], AF.Identity, scale=1.0 / D)
        k.act(rstd, mean, AF.Square)
        k.stt(rstd, ss[:, 0:W], 1.0 / D, rstd, MUL, SUB)
        k.act(rstd, rstd, AF.Ln, bias=cst[:, 0:1])
        k.act(rstd, rstd, AF.Exp, scale=-0.5)
        if W <= 64:
            Z = ln_state["Z"]
            k.tt("dve", Z, Z, mean.us(1).bc([128, 8, W]), SUB)
            k.tt("dve", Z, Z, rstd.us(1).bc([128, 8, W]), MUL)
            wc = pcol[wname]
            bc_ = pcol[bname]
            k.tt("dve", Z, Z, P[:, wc:wc + 8].us(2).bc([128, 8, W]), MUL)
            k.tt("dve", x32[:, :, col0:col0 + W], Z, P[:, bc_:bc_ + 8].us(2).bc([128, 8, W]), ADD)
            k.copy("act", xbf[:, :, col0:col0 + W], x32[:, :, col0:col0 + W])
            return
        for c in range(8):
            k.tt("dve", Zc[c], Zc[c], mean, SUB)
            k.tt("dve", Zc[c], Zc[c], rstd, MUL)
            k.act(x32[:, c, col0:col0 + W], Zc[c], AF.Identity, bias=pc(bname, c), scale=pc(wname, c))
            k.act(xbf[:, c, col0:col0 + W], Zc[c], AF.Identity, bias=pc(bname, c), scale=pc(wname, c))

    def load_tile(tl):
        AR.reset()
        if tl.kind == "p":
            for j in range(4):
                XL = AR.take(4096)
                k.dma(XL, A["x_prompt"][tl.t0 + j * 128: tl.t0 + (j + 1) * 128, :])
                for hb in range(2):
                    b = bank()
                    for cc in range(4):
                        c = hb * 4 + cc
                        k.tr(b[:, cc * 128:(cc + 1) * 128], XL[:, c * 128:(c + 1) * 128], identf[:])
                    evac(x32[:, hb * 4:(hb + 1) * 4, j * 128:(j + 1) * 128], b[:, 0:512].rr("p (c w) -> p c w", c=4))
        else:
            XL = AR.take(4096)
            k.dma(XL[0:64, :], A["x_sample"][:, :])
            b = bank()
            for c in range(8):
                k.tr(b[:, c * 64:(c + 1) * 64], XL[0:64, c * 128:(c + 1) * 128], identf[0:64, 0:64])
            evac(x32[:, :, 0:64], b[:, 0:512].rr("p (c w) -> p c w", c=8))
        k.copy("act", xbf[:, :, 0:tl.W], x32[:, :, 0:tl.W])

    def store_rows(dram_rows, src_fn, n, ncol_chunks=8, is_output=True):
        YO = AR.take(ncol_chunks * 512)
        c = 0
        while c < ncol_chunks:
            g = min(4, ncol_chunks - c)
            b = bank()
            for cc in range(g):
                k.tr(b[0:n, cc * 128:(cc + 1) * 128], src_fn(c + cc), identf[:])
            evac(YO[0:n, c * 128:(c + g) * 128], b[0:n, 0:g * 128])
            c += g
        k.dma(dram_rows, YO[0:n, 0:ncol_chunks * 128], is_output=is_output)

    def store_tile(tl, dram, row0):
        AR.reset()
        if tl.kind == "p":
            for j in range(4):
                store_rows(dram[row0 + tl.t0 + j * 128: row0 + tl.t0 + (j + 1) * 128, :],
                           lambda c, j=j: x32[:, c, j * 128:(j + 1) * 128], 128)
        else:
            store_rows(dram[row0:row0 + 64, :], lambda c: x32[:, c, 0:64], 64)

    def ffn_sublayer(l, tl):
        W, nseq, L = tl.W, tl.nseq, tl.L
        AR.reset()
        MT, MTc = AR.buf(NF, W, BF16)
        zcs = [AR.take((W + 2 * nseq) * 4, F32, "p (s l) -> p s l", s=nseq) for _ in range(3)]
        ubs = [AR.take(W * 4, F32, "p (s l) -> p s l", s=nseq) for _ in range(3)]
        t1 = AR.take(W * 4, F32, "p (s l) -> p s l", s=nseq)
        s1 = AR.take(W * 4, F32, "p (s l) -> p s l", s=nseq)
        want_tm = (tl.kind == "s") or tl.last
        if want_tm:
            nsel = 64 if tl.kind == "s" else 2
            ZTM = AR.take(FF * 4)
            sel0 = 0 if tl.kind == "s" else W - 2
        if tl.kind == "s":
            ZR = AR.take(FF * 4)
            k.dma(ZR[0:32, :], A["state_ffn_conv"][l].rearrange("s j f -> (s j) f"))
            for f0 in range(0, NF, 16):
                g = min(16, NF - f0)
                b = bank()
                for ff in range(g):
                    k.tr(b[:, ff * 32:(ff + 1) * 32], ZR[0:32, (f0 + ff) * 128:(f0 + ff + 1) * 128], identf[0:32, 0:32])
                evac(ZHS[:, f0 * 32:(f0 + g) * 32], b[:, 0:g * 32])
        cw = pcol["cw%d" % l]
        cb = pcol["cb%d" % l]
        col = 0
        if tl.kind == "s":
            def a4(nb):
                return AR.take(nb, F32, "p (m s l) -> p m s l", m=4, s=16)
            ZC4, U4, T4, TM4, S4 = a4(4 * 16 * 6 * 4), a4(1024), a4(1024), a4(1024), a4(1024)
            while col < FF:
                g = min(512, FF - col)
                nm = g // 128
                f0 = col // 128
                bu, bz, ztb = bank(), bank(), bank()
                for (wname, bk, tm) in (("ffn_wu", bu, False), ("ffn_wg", bz, True)):
                    for kc in range(8):
                        wb = wmat(A[wname][l], kc, col, g)
                        for mi in range(nm):
                            k.mm(bk[:, mi * 64:(mi + 1) * 64], wb[:, mi * 128:(mi + 1) * 128], xbf[:, kc, 0:64],
                                 start=(kc == 0 and mi == 0), stop=(kc == 7), skip_group_check=True)
                        if tm:
                            k.mm(ztb[0:64, 0:g], xbf[:, kc, 0:64], wb, start=(kc == 0), stop=(kc == 7))
                evac(ZTM[0:64, col:col + g], ztb[0:64, 0:g])

                def v4(b_):
                    return b_[:, 0:nm * 64].rr("p (m s l) -> p m s l", m=nm, s=16)

                def pb(base):
                    return P[:, base + f0:base + f0 + nm].us(2).us(3).bc([128, nm, 16, 4])
                k.copy("act", ZC4[:, 0:nm, :, 2:6], v4(bz))
                k.copy("act", U4[:, 0:nm], v4(bu))
                k.copy("dve", ZC4[:, 0:nm, :, 0:2], ZHS[:, f0 * 32:(f0 + nm) * 32].rr("p (m s j) -> p m s j", m=nm, j=2))
                k.tt("dve", T4[:, 0:nm], ZC4[:, 0:nm, :, 2:6], pb(cw + 2 * NF), MUL)
                k.tt("dve", T4[:, 0:nm], T4[:, 0:nm], pb(cb), ADD)
                k.tt("dve", TM4[:, 0:nm], ZC4[:, 0:nm, :, 1:5], pb(cw + NF), MUL)
                k.tt("dve", T4[:, 0:nm], T4[:, 0:nm], TM4[:, 0:nm], ADD)
                k.tt("dve", TM4[:, 0:nm], ZC4[:, 0:nm, :, 0:4], pb(cw), MUL)
                k.tt("dve", T4[:, 0:nm], T4[:, 0:nm], TM4[:, 0:nm], ADD)
                k.act(S4[:, 0:nm], T4[:, 0:nm], AF.Silu)
                k.tt("dve", MT[:, f0:f0 + nm, :].rr("p m (s l) -> p m s l", s=16), S4[:, 0:nm], U4[:, 0:nm], MUL)
                col += g
        while col < FF:
            g = min(384, FF - col)
            bu = acc_group(A["ffn_wu"][l], 8, col, g, lambda kc: xbf[:, kc, 0:W], W)
            ztb = bank() if want_tm else None

            def extra(kc, wb):
                k.mm(ztb[0:nsel, 0:g], xbf[:, kc, sel0:sel0 + nsel], wb, start=(kc == 0), stop=(kc == 7))
            bz = acc_group(A["ffn_wg"][l], 8, col, g, lambda kc: xbf[:, kc, 0:W], W, extra if want_tm else None)
            if want_tm:
                evac(ZTM[0:nsel, col:col + g], ztb[0:nsel, 0:g])
            for mi in range(g // 128):
                k.copy("act", zcs[mi][:, :, 2:L + 2], bz[mi][:, 0:W].rr("p (s l) -> p s l", s=nseq))
                k.copy("act", ubs[mi], bu[mi][:, 0:W].rr("p (s l) -> p s l", s=nseq))
            for mi in range(g // 128):
                fi = col // 128 + mi
                zc = zcs[mi]
                if tl.kind == "p":
                    zh = ZH[:, (l * NF + fi) * 2:(l * NF + fi) * 2 + 2]
                    k.copy("dve", zc[:, 0, 0:2], zh)
                    k.copy("dve", zh, zc[:, 0, L:L + 2])
                else:
                    k.copy("dve", zc[:, :, 0:2], ZHS[:, fi * 32:(fi + 1) * 32].rr("p (s j) -> p s j", j=2))
                k.ts("dve", t1, zc[:, :, 2:L + 2], P[:, cw + 2 * NF + fi:cw + 2 * NF + fi + 1], MUL, P[:, cb + fi:cb + fi + 1], ADD)
                k.stt(t1, zc[:, :, 1:L + 1], P[:, cw + NF + fi:cw + NF + fi + 1], t1, MUL, ADD)
                k.stt(t1, zc[:, :, 0:L], P[:, cw + fi:cw + fi + 1], t1, MUL, ADD)
                k.act(s1, t1, AF.Silu)
                k.tt("dve", MTc[fi].rr("p (s l) -> p s l", s=nseq), s1, ubs[mi], MUL)
            col += g
        if want_tm:
            if tl.kind == "s":
                for j in range(2):
                    k.dma(A["s_conv"][l, :, j, :], ZTM[2 + j:64:4, :], is_output=True)
            else:
                k.dma(A["p_conv"][l], ZTM[0:2, :], is_output=True)
        ln_begin(W)
        for col in (0, 512):
            bs = acc_group(A["ffn_wd"][l], NF, col, 512, lambda kc: MTc[kc], W)
            for mi in range(4):
                ln_add(col // 128 + mi, bs[mi][:, 0:W])
        ln_finish("ln2_w%d" % l, "ln2_b%d" % l)

    def dbg_dump(idx, tl, row0):
        if not dbg_n or idx >= dbg_n:
            return
        store_tile(tl, A["dbg"][idx], row0)


    def groups_of(kind, Wm):
        gs = []
        if kind == "p":
            for g in range(Wm // 128):
                gs.append([(128 * g, 0, None), (128 * g + 64, 64, None)])
        else:
            for s0 in range(0, 16, 3):
                gs.append([(4 * (s0 + jj), 32 * jj, s0 + jj) for jj in range(min(3, 16 - s0))])
        return gs

    def to_tokmajor(dst, srcs, kind, g, ch):
        n = len(srcs)
        for i in range(n):
            if kind == "p":
                k.tr(psb[:, i * 128:(i + 1) * 128], srcs[i][:, 128 * g:128 * g + 128], identb[:])
            else:
                for (c0, pb, sidx) in ch:
                    k.tr(psb[pb:pb + 4, i * 128:(i + 1) * 128], srcs[i][:, c0:c0 + 4], identb[:])
        evac(dst[:, 0:n * 128], psb[:, 0:n * 128])

    def gla_like(l, tl, cfg):
        W, kind = tl.W, tl.kind
        H, VC = cfg["H"], cfg["VC"]
        hg = cfg["hg"]
        C = 64 if kind == "p" else 4
        nch = W // C
        ref = (C - 1) // 2
        scale = 128.0 ** -0.5
        AR.reset()
        QI, QIc = AR.buf(H, W, BF16)
        KI, KIc = AR.buf(H, W, BF16)
        QE, QEc = AR.buf(H, W, BF16)
        KD, KDc = AR.buf(H, W, BF16)
        GS, GSc = AR.buf(8, W, BF16)
        OT, OTc = AR.buf(8, W, F32)
        OG, OGc = AR.buf(8, W, BF16)
        ELt = AR.take(H * nch * 4, F32, "p (h n) -> p h n", h=H)
        tA, tB, tC, tD = [AR.take(W * 4) for _ in range(4)]
        grs = groups_of(kind, W)
        VT = [AR.take(2048, BF16) for _ in grs]
        KDT = [AR.take(H * 256, BF16) for _ in grs]
        ATsb = AR.take(H * C * 2, BF16, "p (h c) -> p h c", h=H)
        HK = AR.take(W * 2, BF16)
        FB, FBc = AR.buf(8 if hg else 4, W, F32)
        xr = lambda kc: xbf[:, kc, 0:W]

        for col in range(0, H * 128, 512):
            bs = acc_group(cfg["wq"], 8, col, 512, xr, W)
            for mi in range(4):
                h = col // 128 + mi
                if hg:
                    k.act(tA, bs[mi][:, 0:W], AF.Silu)
                    k.ts("dve", OGc[h], tA, scale, MUL)
                else:
                    k.act(OGc[h], bs[mi][:, 0:W], AF.Identity, scale=scale)

        def finalize(h, lg, kraw, qraw):
            b = tD
            k.scan(b, rmask[kind][:, 0:W], lg, 0.0, MUL, ADD)
            b3 = b.rr("p (n c) -> p n c", c=C)
            tA3 = tA.rr("p (n c) -> p n c", c=C)
            k.tt("dve", tA3, b3, b3[:, :, ref:ref + 1].bc([128, nch, C]), SUB)
            k.act(tB, tA, AF.Exp)
            k.tt("dve", QIc[h], qraw, tB, MUL)
            k.act(tB, tA, AF.Exp, scale=-1.0)
            k.tt("dve", KIc[h], kraw, tB, MUL)
            k.act(tB, b, AF.Exp)
            k.tt("dve", QEc[h], qraw, tB, MUL)
            k.tt("dve", tA3, b3[:, :, C - 1:C].bc([128, nch, C]), b3, SUB)
            k.act(tB, tA, AF.Exp)
            k.tt("dve", KDc[h], kraw, tB, MUL)
            k.act(ELt[:, h, :], b3[:, :, C - 1], AF.Exp)

        def gate_block(col):
            bs = acc_group(cfg["wg"], 8, col, 512, xr, W)
            for mi in range(4):
                k.act(GSc[col // 128 + mi], bs[mi][:, 0:W], AF.Silu)

        def v_block(col):
            vb = [bank() for _ in grs]
            for kc in range(8):
                wb = wmat(cfg["wv"], kc, col, 512)
                for g, ch in enumerate(grs):
                    if kind == "p":
                        k.mm(vb[g][:, 0:512], xbf[:, kc, 128 * g:128 * g + 128], wb, start=(kc == 0), stop=(kc == 7))
                    else:
                        for (c0, pb, sidx) in ch:
                            k.mm(vb[g][pb:pb + C, 0:512], xbf[:, kc, c0:c0 + C], wb, start=(kc == 0), stop=(kc == 7))
            for g in range(len(grs)):
                evac(VT[g][:, col:col + 512], vb[g][:, 0:512])

        if hg:
            for col in range(0, 1024, 512):
                bs = acc_group(cfg["wk"], 8, col, 512, xr, W)
                for mi in range(4):
                    k.act(FBc[col // 128 + mi], bs[mi][:, 0:W], AF.Sigmoid)
            gate_block(0)
            gate_block(512)
            v_block(0)
            v_block(512)
            for h in range(8):
                k.ts("dve", tA, FBc[h], pc("olb1", h), MUL, pc("lb1", h), ADD)
                k.act(tB, tA, AF.Ln)
                k.ts("dve", tC, tA, -1.0, MUL, 1.0, ADD)
                finalize(h, tB, tC, OGc[h])
        else:
            g1 = wload(A["gl_gk1"][0].rearrange("(k p) m -> p k m", p=128), 128, 8, 16)
            b = bank()
            for kc in range(8):
                k.mm(b[0:16, 0:W], g1[:, kc, :], xr(kc), start=(kc == 0), stop=(kc == 7))
            k.copy("act", HK[0:16, :], b[0:16, 0:W])
            g2 = wload(A["gl_gk2"][0].rearrange("p (o m) -> p o m", o=1), 16, 1, 512)[:, 0, :]
            LG, LGc = AR.buf(4, W, F32)
            for h in range(4):
                b = bank()
                k.mm(b[:, 0:W], g2[:, h * 128:(h + 1) * 128], HK[0:16, :])
                k.act(tA, b[:, 0:W], AF.Exp, bias=pc("ngkb", h), scale=-1.0)
                k.act(tB, tA, AF.Ln, bias=cst[:, 3:4])
                k.ts("dve", LGc[h], tB, -1.0 / 16.0, MUL)
            bs = acc_group(cfg["wk"], 8, 0, 512, xr, W)
            for h in range(4):
                k.copy("act", FBc[h], bs[h][:, 0:W])
            gate_block(0)
            gate_block(512)
            v_block(0)
            v_block(512)
            for h in range(4):
                finalize(h, LGc[h], FBc[h], OGc[h])

        for g in range(len(grs)):
            to_tokmajor(KDT[g], KDc, kind, g, grs[g])

        hv = H * VC * 128
        st_dram_in, st_dram_out_p, st_dram_out_s = cfg["st_in"], cfg["st_out_p"], cfg["st_out_s"]

        def st_view(i):
            return (ST32[i][:, 0:hv].rr("p (h v) -> p h v", h=H), STB[i][:, 0:hv].rr("p (h v) -> p h v", h=H))

        for g, ch in enumerate(grs):
            if kind == "s":
                for (c0, pb, sidx) in ch:
                    S32, Sbf = st_view(sidx % 4)
                    k.dma(S32, st_dram_in[sidx].rearrange("h k v -> k h v"))
                    k.copy("act", Sbf, S32)
            atp = bank()
            for (c0, pb, sidx) in ch:
                for h in range(H):
                    k.mm(atp[pb:pb + C, h * C:(h + 1) * C], KIc[h][:, c0:c0 + C], QIc[h][:, c0:c0 + C])
            k.tt("dve", ATsb, atp[:, 0:H * C].rr("p (h c) -> p h c", h=H), msk[kind + "ui"][:].us(1).bc([128, H, C]), MUL)
            for ci, (c0, pb, sidx) in enumerate(ch):
                n = c0 // C
                if kind == "p":
                    S32, Sbf = st_view(cfg["st"])
                else:
                    S32, Sbf = st_view(sidx % 4)
                ops = bank()
                for h in range(H):
                    for jv in range(VC):
                        cc = h * VC + jv
                        k.mm(ops[:, cc * C:(cc + 1) * C], VT[g][pb:pb + C, cc * 128:(cc + 1) * 128], ATsb[pb:pb + C, h, :],
                             start=(cc == 0), stop=False, skip_group_check=True)
                for h in range(H):
                    for jv in range(VC):
                        cc = h * VC + jv
                        k.mm(ops[:, cc * C:(cc + 1) * C], Sbf[:, h, jv * 128:(jv + 1) * 128], QEc[h][:, c0:c0 + C],
                             start=False, stop=True, skip_group_check=True)
                evac(OT[:, :, c0:c0 + C], ops[:, 0:8 * C].rr("p (c w) -> p c w", c=8))
                sps = bank(2)
                for h in range(H):
                    k.mm(sps[:, h * VC * 128:(h + 1) * VC * 128], KDT[g][pb:pb + C, h * 128:(h + 1) * 128],
                         VT[g][pb:pb + C, h * VC * 128:(h + 1) * VC * 128])
                for h in range(H):
                    k.stt(S32[:, h, :], S32[:, h, :], ELt[:, h, n:n + 1], sps[:, h * VC * 128:(h + 1) * VC * 128], MUL, ADD)
                if kind == "p":
                    k.copy("act", Sbf, S32)
                if kind == "s":
                    k.dma(st_dram_out_s[sidx].rearrange("h k v -> k h v"), S32, is_output=True)
                elif tl.last and g == len(grs) - 1 and ci == len(ch) - 1:
                    k.dma(st_dram_out_p.rearrange("h k v -> k h v"), S32, is_output=True)

        for h in range(H):
            ms = bank()
            for jv in range(VC):
                tAb = tA.bitcast(BF16)[:, 0:W]
                k.act(tAb, OTc[h * VC + jv], AF.Square)
                k.mm(ms[:, 0:W], onesb[:], tAb, start=(jv == 0), stop=(jv == VC - 1))
            k.act(tB, ms[:, 0:W], AF.Sqrt, bias=cst[:, 1:2], scale=1.0 / (VC * 128))
            k.recip(tB, tB)
            for jv in range(VC):
                cc = h * VC + jv
                k.stt(tC, OTc[cc], pc(cfg["norm"], jv), tB, MUL, MUL)
                k.tt("dve", OGc[cc], tC, GSc[cc], MUL)
        AR.reset(0)
        ln_begin(W)
        for col in (0, 512):
            bs = acc_group(cfg["wo"], 8, col, 512, lambda kc: OGc[kc], W)
            for mi in range(4):
                ln_add(col // 128 + mi, bs[mi][:, 0:W])
        ln_finish("ln1_w%d" % l, "ln1_b%d" % l)

    HGCFG = dict(hg=True, H=8, VC=1, wq=A["hg_wq"][0], wk=A["hg_wf"][0], wv=A["hg_wi"][0], wg=A["hg_wg"][0],
                 wo=A["hg_wo"][0], norm="hgn", st=1, st_in=A["state_hgrn"], st_out_p=A["p_hgrn"], st_out_s=A["s_hgrn"])
    GLCFG = dict(hg=False, H=4, VC=2, wq=A["gl_wq"][0], wk=A["gl_wk"][0], wv=A["gl_wv"][0], wg=A["gl_wg"][0],
                 wo=A["gl_wo"][0], norm="gln", st=2, st_in=A["state_gla"], st_out_p=A["p_gla"], st_out_s=A["s_gla"])


    def rwkv_mixer(l, j, tl, col0, Wm):
        kind = tl.kind
        C = 64 if kind == "p" else 4
        nch = Wm // C
        AR.reset()
        RT, RTc = AR.buf(8, Wm, BF16)
        KH, KHc = AR.buf(8, Wm, BF16)
        BH, BHc = AR.buf(8, Wm, BF16)
        KT, KTc = AR.buf(8, Wm, BF16)
        VB, VBc = AR.buf(8, Wm, BF16)
        G, Gc = AR.buf(8, Wm, BF16)
        BV, BVc = AR.buf(8, Wm, F32)
        PCt = AR.take(8 * nch * 4, F32, "p (c n) -> p c n", c=8)
        mark = AR.p
        XX, XXc = AR.buf(8, Wm, BF16)
        XR, XRc = AR.buf(8, Wm, BF16)
        XK, XKc = AR.buf(8, Wm, BF16)
        XV, XVc = AR.buf(8, Wm, BF16)
        XT, XTc = AR.buf(8, Wm, BF16)
        HW = AR.take(Wm * 2, BF16)
        HA = AR.take(Wm * 2, BF16)
        HG0 = AR.take(Wm * 2, BF16)
        HG1 = AR.take(Wm * 2, BF16)
        HV = AR.take(Wm * 2, BF16)
        RAWr, RAWrc = AR.buf(4, Wm, F32)
        RAWk, RAWkc = AR.buf(4, Wm, F32)
        RAWv, RAWvc = AR.buf(4, Wm, F32)
        t = [AR.take(Wm * 16) for _ in range(8)]
        xv_ = x32[:, :, col0:col0 + Wm]

        if kind == "p":
            k.tt("dve", XX[:, :, 1:Wm], xv_[:, :, 0:Wm - 1], xv_[:, :, 1:Wm], SUB)
            k.tt("dve", XX[:, :, 0], shiftP[j][:, :], xv_[:, :, 0], SUB)
            k.copy("dve", shiftP[j][:, :], xv_[:, :, Wm - 1])
            if tl.last and col0 + Wm == tl.W:
                store_rows(A["p_shift"][j:j + 1, :], lambda c: shiftP[j][:, c:c + 1], 1)
        else:
            SR = AR.take(4096)
            shS = AR.take(512, F32, "p (c s) -> p c s", c=8)
            k.dma(SR[0:16, :], A["state_rwkv_shift"][j])
            b = bank()
            for c in range(8):
                k.tr(b[:, c * 16:(c + 1) * 16], SR[0:16, c * 128:(c + 1) * 128], identf[0:16, 0:16])
            evac(shS, b[:, 0:128].rr("p (c s) -> p c s", c=8))
            x4 = xv_.rr("p c (s t) -> p c s t", t=4)
            XX4 = XX.rr("p c (s t) -> p c s t", t=4)
            for c in range(8):
                k.tt("dve", XX4[:, c, :, 1:4], x4[:, c, :, 0:3], x4[:, c, :, 1:4], SUB)
            k.tt("dve", XX4[:, :, :, 0], shS, x4[:, :, :, 0], SUB)
            store_rows(A["s_shift"][j], lambda c: x32[:, c, 3:64:4], 16)
        mc = pcol["mix%d" % j]

        def mix(dst_c, jj):
            for c in range(8):
                k.stt(dst_c[c], XXc[c], P[:, mc + jj * 8 + c:mc + jj * 8 + c + 1], x32[:, c, col0:col0 + Wm], MUL, ADD)

        def lora1(w3d, m, dst, pb, func, src_c):
            wb = wload(w3d, 128, 8, m)
            b = bank()
            for kc in range(8):
                k.mm(b[pb:pb + m, 0:Wm], wb[:, kc, :], src_c[kc], start=(kc == 0), stop=(kc == 7))
            k.act(dst[pb:pb + m, :], b[pb:pb + m, 0:Wm], func)

        mix(XTc, 1)
        lora1(A["rw_w1"][j].rearrange("(k p) m -> p k m", p=128), 64, HW, 0, AF.Tanh, XTc)
        mix(XTc, 4)
        lora1(A["rw_a1"][j].rearrange("(k p) m -> p k m", p=128), 64, HA, 0, AF.Copy, XTc)
        mix(XTc, 5)
        g1v = A["rw_g1"][j].rearrange("(k p) m -> p k m", p=128)
        lora1(g1v[:, :, 0:64], 64, HG0, 0, AF.Sigmoid, XTc)
        lora1(g1v[:, :, 64:128], 64, HG0, 64, AF.Sigmoid, XTc)
        lora1(g1v[:, :, 128:160], 32, HG1, 0, AF.Sigmoid, XTc)
        mix(XRc, 0)
        mix(XKc, 2)
        mix(XVc, 3)
        if j > 0:
            lora1(A["rw_v1"][0].rearrange("(k p) m -> p k m", p=128), 32, HV, 0, AF.Copy, XVc)

        def wrow(w2d, r0, r1, col):
            return wload(w2d[r0:r1, col:col + 512].rearrange("p (o m) -> p o m", o=1), r1 - r0, 1, 512)[:, 0, :]

        def g4(v):
            return v.rr("p (c w) -> p c w", c=4)

        def pb4(name, c0):
            cc = pcol[name] + c0
            return P[:, cc:cc + 4].us(2).bc([128, 4, Wm])

        def reg(d, mi):
            return d[:, mi * Wm:(mi + 1) * Wm]

        def st(mi):
            return (mi * Wm) % 512 == 0

        W4 = 4 * Wm
        for gq in range(2):
            col = gq * 512
            c4 = gq * 4
            for (wn, xs, raw) in (("rw_wr", XRc, RAWr), ("rw_wk", XKc, RAWk), ("rw_wv", XVc, RAWv)):
                d = bank(2)
                for kc in range(8):
                    wb = wmat(A[wn][j], kc, col, 512)
                    for mi in range(4):
                        k.mm(reg(d, mi), wb[:, mi * 128:(mi + 1) * 128], xs[kc], start=(kc == 0 and st(mi)), stop=(kc == 7),
                             skip_group_check=True)
                evac(raw, g4(d[:, 0:W4]))
            r4, k4, v4 = RAWr, RAWk, RAWv
            T0, T1, T2, T3, T4, T5, T6, T7 = [g4(x) for x in t]
            d = bank(2)
            wb = wrow(A["rw_w2"][j], 0, 64, col)
            for mi in range(4):
                k.mm(reg(d, mi), wb[:, mi * 128:(mi + 1) * 128], HW[0:64, :], start=st(mi), stop=True, skip_group_check=True)
            k.tt("dve", T0, g4(d[:, 0:W4]), pb4("w0%d" % j, c4), ADD)
            k.act(T0, T0, AF.Sigmoid)
            d = bank(2)
            wb = wrow(A["rw_a2"][j], 0, 64, col)
            for mi in range(4):
                k.mm(reg(d, mi), wb[:, mi * 128:(mi + 1) * 128], HA[0:64, :], start=st(mi), stop=True, skip_group_check=True)
            k.tt("dve", T1, g4(d[:, 0:W4]), pb4("a0%d" % j, c4), ADD)
            k.act(T1, T1, AF.Sigmoid)
            if j == 0:
                k.copy("act", vf[:, c4:c4 + 4, col0:col0 + Wm], v4)
            else:
                d = bank(2)
                wb = wrow(A["rw_v2"][0], 0, 32, col)
                for mi in range(4):
                    k.mm(reg(d, mi), wb[:, mi * 128:(mi + 1) * 128], HV[0:32, :], start=st(mi), stop=True, skip_group_check=True)
                k.tt("dve", T2, g4(d[:, 0:W4]), pb4("v0", c4), ADD)
                k.act(T2, T2, AF.Sigmoid)
                k.tt("dve", T3, vf[:, c4:c4 + 4, col0:col0 + Wm], v4, SUB)
                k.tt("dve", T3, T3, T2, MUL)
                k.tt("dve", v4, v4, T3, ADD)
            k.copy("act", VB[:, c4:c4 + 4, :], v4)
            d = bank(2)
            wb = wrow(A["rw_g2"][j], 0, 128, col)
            for mi in range(4):
                k.mm(reg(d, mi), wb[:, mi * 128:(mi + 1) * 128], HG0[:, :], start=st(mi), stop=False, skip_group_check=True)
            wb = wrow(A["rw_g2"][j], 128, 160, col)
            for mi in range(4):
                k.mm(reg(d, mi), wb[:, mi * 128:(mi + 1) * 128], HG1[0:32, :], start=False, stop=True, skip_group_check=True)
            k.copy("act", G[:, c4:c4 + 4, :], g4(d[:, 0:W4]))
            k.tt("dve", T2, k4, pb4("kk%d" % j, c4), MUL)
            sb = t[7].bitcast(BF16)[:, 0:W4]
            k.act(g4(sb), T2, AF.Square)
            d = bank(2)
            for mi in range(4):
                k.mm(reg(d, mi), bonesb[:], sb[:, mi * Wm:(mi + 1) * Wm], start=st(mi), stop=True, skip_group_check=True)
            k.act(T3, g4(d[:, 0:W4]), AF.Ln, bias=cst[:, 7:8])
            k.act(T3, T3, AF.Exp, scale=-0.5)
            k.tt("dve", T2, T2, T3, MUL)
            k.tt("dve", T3, T1, pb4("ka%d" % j, c4), MUL)
            k.tt("dve", T3, T3, pb4("oka%d" % j, c4), ADD)
            k.tt("dve", T4, k4, T3, MUL)
            k.tt("dve", T5, T2, T1, MUL)
            k.scan(t[6], rmask[kind][:, 0:W4], t[0], 0.0, MUL, ADD)
            k.act(T7, T6, AF.Exp, scale=-C0)
            k.tt("dve", RT[:, c4:c4 + 4, :], r4, T7, MUL)
            k.copy("dve", PCt[:, c4:c4 + 4, :], T7.rr("p c (n q) -> p c n q", q=C)[:, :, :, C - 1])
            k.tt("dve", T3, T6, T0, SUB)
            k.act(T3, T3, AF.Exp, scale=-C0)
            k.tt("dve", KT[:, c4:c4 + 4, :], T2, T3, MUL)
            k.act(T7, T6, AF.Exp, scale=C0)
            k.tt("dve", KH[:, c4:c4 + 4, :], T4, T7, MUL)
            k.tt("dve", BH[:, c4:c4 + 4, :], T5, T7, MUL)
            k.tt("dve", T3, r4, pb4("rk%d" % j, c4), MUL)
            sb = t[7].bitcast(BF16)[:, 0:W4]
            k.tt("dve", g4(sb), T3, T4, MUL)
            d = bank(2)
            for mi in range(4):
                k.mm(reg(d, mi), bonesb[:], sb[:, mi * Wm:(mi + 1) * Wm], start=st(mi), stop=True, skip_group_check=True)
            k.tt("dve", BV[:, c4:c4 + 4, :], g4(d[:, 0:W4]), v4, MUL)

        STOP = 9
        if STOP <= 1:
            return
        AR.reset(mark)
        grs = groups_of(kind, Wm)
        VT = [AR.take(2048, BF16) for _ in grs]
        KHT = [AR.take(2048, BF16) for _ in grs]
        BHT = [AR.take(2048, BF16) for _ in grs]
        n_am = len(grs) if kind == "p" else 1
        ams = [[AR.take(16 * C * 2, BF16, "p (h c) -> p h c", h=16) for _ in range(8)] for _ in range(n_am)]
        RHSsb = AR.take(2048, BF16)
        Usb = AR.take(2048, BF16)
        YT, YTc = AR.buf(8, Wm, F32)
        YG, YGc = AR.buf(8, Wm, BF16)
        Slds = [AR.take(4096), AR.take(4096)] if kind == "s" else [None, None]
        Sst = AR.take(4096)
        for g in range(len(grs)):
            to_tokmajor(VT[g], VBc, kind, g, grs[g])
            to_tokmajor(KHT[g], KHc, kind, g, grs[g])
            to_tokmajor(BHT[g], BHc, kind, g, grs[g])

        if STOP <= 2:
            return

        def hd(h):
            return h // 2, (h % 2) * 64

        def hs(h):
            return (h % 2) * 8 + h // 2

        def v3(d):
            return d[:, 0:1024].rr("p (h c) -> p h c", h=16)[:, :, 0:C]

        def st_view(i):
            return (ST32[i][:, 0:512].rr("p (c v) -> p c v", c=8),
                    (STB[i][:, 0:512].rr("p (c v) -> p c v", c=8), STB[i][:, 512:1024].rr("p (c v) -> p c v", c=8)))

        def mask_state(H32, Hm):
            k.ts("dve", Hm[0], H32, cst[:, 5:6], MUL)
            k.ts("dve", Hm[1], H32, cst[:, 6:7], MUL)

        def phase1(items):
            def amat(ch, dst, lh, rh, mk):
                d = bank(2)
                for (c0, pb, sidx) in ch:
                    for h in range(16):
                        c, hb = hd(h)
                        k.mm(d[pb:pb + C, hs(h) * 64:hs(h) * 64 + C], lh[c][hb:hb + 64, c0:c0 + C], rh[c][hb:hb + 64, c0:c0 + C])
                k.tt("dve", dst, v3(d), msk[kind + mk][:].us(1).bc([128, 16, C]), MUL)

            def mm3(ch, lh, rh):
                d = bank(2)
                for (c0, pb, sidx) in ch:
                    for h in range(16):
                        k.mm(d[pb:pb + C, h * 64:h * 64 + C], lh[pb:pb + C, h, :], rh[pb:pb + C, h, :])
                return d

            for it in items:
                Msb, Nsb, Xs, M2, N2, AkkT, ArkT, ArbT = it["am"]
                ch = it["ch"]
                amat(ch, Msb, BHc, KTc, "su")
                amat(ch, Nsb, KTc, BHc, "sl")
                amat(ch, AkkT, KHc, KTc, "su")
                amat(ch, ArkT, KHc, RTc, "ui")
                amat(ch, ArbT, BHc, RTc, "ui")
                k.stt(Xs, Msb, -1.0, msk[kind + "id"][:].us(1).bc([128, 16, C]), MUL, ADD)
                it["p"] = [Msb, Nsb, M2, N2]
            p_ = 2
            while p_ < C:
                lastlv = (p_ * 2 >= C)
                for it in items:
                    Mp, Nn, Mo, No = it["p"]
                    d = mm3(it["ch"], Mp, Nn)
                    evac(No, v3(d))
                if not lastlv:
                    for it in items:
                        Mp, Nn, Mo, No = it["p"]
                        d = mm3(it["ch"], Nn, Mp)
                        evac(Mo, v3(d))
                for it in items:
                    Mp, Nn, Mo, No = it["p"]
                    Xs = it["am"][2]
                    d = mm3(it["ch"], No, Xs)
                    k.tt("dve", Xs, v3(d), Xs, ADD)
                    it["p"] = [Mo, No, Mp, Nn]
                p_ *= 2

        if kind == "p":
            items_all = [dict(ch=ch, am=ams[g]) for g, ch in enumerate(grs)]
            phase1(items_all)
        for g, ch in enumerate(grs):
            if kind == "p":
                it = items_all[g]
            else:
                it = dict(ch=ch, am=ams[0])
                phase1([it])
            Msb, Nsb, Xs, M2, N2, AkkT, ArkT, ArbT = it["am"]
            if kind == "s":
                for (c0, pb, sidx) in ch:
                    H32, Hbf = st_view(sidx % 4)
                    Sl = Slds[sidx % 2]
                    k.dma(Sl[0:64, :].rr("p (h k) -> p h k", h=16), A["state_rwkv"][j, sidx].rearrange("h v k -> v h k"))
                    b = bank()
                    for c in range(8):
                        k.tr(b[:, c * 64:(c + 1) * 64], Sl[0:64, c * 128:(c + 1) * 128], identf[0:64, 0:64])
                    evac(H32, b[:, 0:512].rr("p (c v) -> p c v", c=8))
                    mask_state(H32, Hbf)
            for ci, (c0, pb, sidx) in enumerate(ch):
                n = c0 // C
                if kind == "p":
                    H32, Hbf = st_view(0 if j == 0 else 3)
                else:
                    H32, Hbf = st_view(sidx % 4)
                d = bank(2)
                for h in range(16):
                    c, hb = hd(h)
                    k.mm(d[pb:pb + C, h * 64:(h + 1) * 64], KTc[c][:, c0:c0 + C], Hbf[h % 2][:, c, :],
                         start=(h % 8 == 0), stop=False, skip_group_check=True)
                for h in range(16):
                    k.mm(d[pb:pb + C, h * 64:(h + 1) * 64], AkkT[pb:pb + C, hs(h), :], VT[g][pb:pb + C, h * 64:(h + 1) * 64],
                         start=False, stop=True, skip_group_check=True)
                k.copy("act", RHSsb[pb:pb + C, :], d[pb:pb + C, 0:1024])
                d = bank(2)
                for h in range(16):
                    k.mm(d[pb:pb + C, h * 64:(h + 1) * 64], Xs[pb:pb + C, hs(h), :], RHSsb[pb:pb + C, h * 64:(h + 1) * 64])
                k.act(Usb[pb:pb + C, :], d[pb:pb + C, 0:1024], AF.Identity, scale=-1.0)
                yps = bank()
                for h in range(16):
                    c, hb = hd(h)
                    k.mm(yps[hb:hb + 64, c * C:(c + 1) * C], Hbf[h % 2][:, c, :], RTc[c][:, c0:c0 + C],
                         start=(h < 2), stop=False, skip_group_check=True)
                for h in range(16):
                    c, hb = hd(h)
                    o = yps[hb:hb + 64, c * C:(c + 1) * C]
                    k.mm(o, VT[g][pb:pb + C, h * 64:(h + 1) * 64], ArkT[pb:pb + C, hs(h), :], start=False, stop=False, skip_group_check=True)
                    k.mm(o, Usb[pb:pb + C, h * 64:(h + 1) * 64], ArbT[pb:pb + C, hs(h), :], start=False, stop=True, skip_group_check=True)
                evac(YT[:, :, c0:c0 + C], yps[:, 0:8 * C].rr("p (c w) -> p c w", c=8))
                hps = bank()
                for h in range(16):
                    c, hb = hd(h)
                    o = hps[hb:hb + 64, c * 64:(c + 1) * 64]
                    k.mm(o, KHT[g][pb:pb + C, c * 128 + hb:c * 128 + hb + 64], VT[g][pb:pb + C, h * 64:(h + 1) * 64], start=True, stop=False)
                    k.mm(o, BHT[g][pb:pb + C, c * 128 + hb:c * 128 + hb + 64], Usb[pb:pb + C, h * 64:(h + 1) * 64], start=False, stop=True)
                k.tt("dve", H32, hps[:, 0:512].rr("p (c v) -> p c v", c=8), H32, ADD)
                k.tt("dve", H32, H32, PCt[:, :, n:n + 1].bc([128, 8, 64]), MUL)
                if kind == "p":
                    mask_state(H32, Hbf)
                fin_p = (kind == "p" and tl.last and col0 + Wm == tl.W and g == len(grs) - 1 and ci == len(ch) - 1)
                if kind == "s" or fin_p:
                    dram = A["s_rwkv"][j, sidx] if kind == "s" else A["p_rwkv"][j]
                    d = bank(2)
                    for c in range(8):
                        k.tr(d[0:64, c * 128:(c + 1) * 128], H32[:, c, :], identf[:])
                    evac(Sst[0:64, :], d[0:64, 0:1024])
                    k.dma(dram.rearrange("h v k -> v h k"), Sst[0:64, :].rr("p (h k) -> p h k", h=16), is_output=True)

        if STOP <= 5:
            return
        tq0 = AR.take(Wm * 16)
        tq1 = AR.take(Wm * 16)
        Q0, Q1 = g4(tq0), g4(tq1)
        for gq in range(2):
            c4 = gq * 4
            y4 = YT[:, c4:c4 + 4, :]
            d = bank(2)
            for mi in range(4):
                k.mm(reg(d, mi), bones[:], YTc[c4 + mi], start=st(mi), stop=True, skip_group_check=True)
            k.stt(Q0, g4(d[:, 0:W4]), -1.0 / 64, y4, MUL, ADD)
            qb = tq1.bitcast(BF16)[:, 0:W4]
            k.act(g4(qb), Q0, AF.Square)
            d = bank(2)
            for mi in range(4):
                k.mm(reg(d, mi), bonesb[:], qb[:, mi * Wm:(mi + 1) * Wm], start=st(mi), stop=True, skip_group_check=True)
            k.act(Q1, g4(d[:, 0:W4]), AF.Ln, bias=cst[:, 2:3], scale=1.0 / 64)
            k.act(Q1, Q1, AF.Exp, scale=-0.5)
            k.tt("dve", Q0, Q0, Q1, MUL)
            k.tt("dve", Q0, Q0, pb4("lnxw%d" % j, c4), MUL)
            k.tt("dve", Q0, Q0, pb4("lnxb%d" % j, c4), ADD)
            k.tt("dve", Q0, Q0, BV[:, c4:c4 + 4, :], ADD)
            k.tt("dve", YG[:, c4:c4 + 4, :], Q0, G[:, c4:c4 + 4, :], MUL)
        ln_begin(Wm)
        for col in (0, 512):
            bs = acc_group(A["rw_wo"][j], 8, col, 512, lambda kc: YGc[kc], Wm)
            for mi in range(4):
                ln_add(col // 128 + mi, bs[mi][:, 0:Wm], col0)
        ln_finish("ln1_w%d" % l, "ln1_b%d" % l, col0)

    def mk_tiles():
        tiles = []
        for ti in range(4):
            tl = Tl()
            tl.kind, tl.W, tl.nseq, tl.L, tl.t0, tl.last = "p", 512, 1, 512, ti * 512, (ti == 3)
            tiles.append(tl)
        tl = Tl()
        tl.kind, tl.W, tl.nseq, tl.L, tl.t0, tl.last = "s", 64, 16, 4, 0, True
        tiles.append(tl)
        return tiles

    def run_all(layers=(0, 1, 2, 3), tiles=None):
        for tl in (tiles or mk_tiles()):
            load_tile(tl)
            row0 = tl.t0 if tl.kind == "p" else 2048
            for l in layers:
                if l % 3 == 0:
                    if tl.kind == "p":
                        for half in (0, 256):
                            rwkv_mixer(l, l // 3, tl, half, 256)
                    else:
                        rwkv_mixer(l, l // 3, tl, 0, 64)
                elif l == 1:
                    gla_like(l, tl, HGCFG)
                else:
                    gla_like(l, tl, GLCFG)
                dbg_dump2(2 * l, tl, row0)
                ffn_sublayer(l, tl)
                dbg_dump2(2 * l + 1, tl, row0)
            if tl.kind == "p":
                store_tile(tl, A["y_prompt"], 0)
            else:
                store_tile(tl, A["y_sample"], 0)

    def dbg_dump2(idx, tl, row0):
        if not dbg_n or idx >= dbg_n:
            return
        AR.reset()
        if tl.kind == "p":
            for jj in range(4):
                store_rows(A["dbg"][idx][row0 + jj * 128: row0 + (jj + 1) * 128, :],
                           lambda c, jj=jj: x32[:, c, jj * 128:(jj + 1) * 128], 128)
        else:
            store_rows(A["dbg"][idx][2048:2112, :], lambda c: x32[:, c, 0:64], 64)

    return k, A, locals()


_NC_CACHE = {}


def _get_nc():
    if "nc" not in _NC_CACHE:
        nc = bass.Bass("TRN2", target_bir_lowering=False)
        k, A, L = build(nc, dbg_n=0)
        L["run_all"]()
        k.finish()
        k.close()
        _NC_CACHE["nc"] = nc
    return _NC_CACHE["nc"]


def kernel(**inputs):
    n = 8
    f32 = np.float32
    inp = {kk: np.asarray(v) for kk, v in inputs.items()}
    in_maps = []
    for c in range(n):
        s = slice(16 * c, 16 * (c + 1))
        m = {}
        m["x_prompt"] = np.ascontiguousarray(inp["x_prompt"][c], dtype=f32)
        m["x_sample"] = np.ascontiguousarray(inp["x_sample"][s], dtype=f32).reshape(64, 1024)
        m["state_rwkv"] = np.ascontiguousarray(inp["state_rwkv"][:, s], dtype=f32)
        m["state_rwkv_shift"] = np.ascontiguousarray(inp["state_rwkv_shift"][:, s], dtype=f32)
        m["state_hgrn"] = np.ascontiguousarray(inp["state_hgrn"][0, s], dtype=f32)
        m["state_gla"] = np.ascontiguousarray(inp["state_gla"][0, s], dtype=f32)
        m["state_ffn_conv"] = np.ascontiguousarray(inp["state_ffn_conv"][:, s], dtype=f32)
        for w in WEIGHT_SHAPES:
            m[w] = np.ascontiguousarray(inp[w], dtype=f32)
        in_maps.append(m)
    nc = _get_nc()
    res = run_bass_kernel_spmd(nc, in_maps, core_ids=list(range(n)))
    R = res.results
    y_prompt = np.stack([R[c]["y_prompt"] for c in range(n)], 0)
    y_sample = np.concatenate([R[c]["y_sample"].reshape(16, 4, 1024) for c in range(n)], 0)
    p_rwkv = np.stack([R[c]["p_rwkv"] for c in range(n)], 1)
    p_shift = np.stack([R[c]["p_shift"] for c in range(n)], 1)
    p_hgrn = np.stack([R[c]["p_hgrn"] for c in range(n)], 0)[None]
    p_gla = np.stack([R[c]["p_gla"] for c in range(n)], 0)[None]
    p_conv = np.stack([R[c]["p_conv"] for c in range(n)], 1)
    s_rwkv = np.concatenate([R[c]["s_rwkv"] for c in range(n)], 1)
    s_shift = np.concatenate([R[c]["s_shift"] for c in range(n)], 1)
    s_hgrn = np.concatenate([R[c]["s_hgrn"] for c in range(n)], 0)[None]
    s_gla = np.concatenate([R[c]["s_gla"] for c in range(n)], 0)[None]
    s_conv = np.concatenate([R[c]["s_conv"] for c in range(n)], 1)
    outs = (y_prompt, y_sample, p_rwkv, p_shift, p_hgrn, p_gla, p_conv, s_rwkv, s_shift, s_hgrn, s_gla, s_conv)
    return tuple(np.ascontiguousarray(o, dtype=f32) for o in outs)
```

```python
import contextlib
import numpy as np
import concourse.bass as bass
import concourse.mybir as mybir

F32 = mybir.dt.float32
BF16 = mybir.dt.bfloat16
AF = mybir.ActivationFunctionType
ALU = mybir.AluOpType
AX = mybir.AxisListType

SAME_ENGINE_SYNC = True
N_DMA_SEMS = 40


class T:
    def __init__(self, tile, name):
        self.t = tile
        self.name = name
        self.lw = None
        self.rd = {}

    def __getitem__(self, idx):
        return V(self.t[idx], self)

    def sub(self, name):
        return T(self.t, self.name + "." + name)


class V:
    def __init__(self, ap, owner):
        self.ap = ap
        self.o = owner

    def __getitem__(self, idx):
        return V(self.ap[idx], self.o)

    def rr(self, pat, **kw):
        return V(self.ap.rearrange(pat, **kw), self.o)

    def bc(self, shape):
        return V(self.ap.broadcast_to(shape), self.o)

    def us(self, axis):
        return V(self.ap.unsqueeze(axis), self.o)

    def bitcast(self, dt):
        return V(self.ap.bitcast(dt), self.o)

    @property
    def shape(self):
        return self.ap.shape


def _own(vs):
    r = []
    for v in vs:
        if isinstance(v, V):
            if isinstance(v.o, (list, tuple)):
                r.extend(v.o)
            else:
                r.append(v.o)
    return r


class KB:
    def __init__(self, nc):
        self.nc = nc
        self.es = contextlib.ExitStack()
        self.eng = {"pe": nc.tensor, "act": nc.scalar, "dve": nc.vector, "pool": nc.gpsimd, "sp": nc.sync}
        self.sem = {}
        self.cnt = {}
        self.waited = {e: {} for e in self.eng}
        for e in self.eng:
            self.sem[e] = self.es.enter_context(nc.semaphore("s_" + e))
            self.cnt[e] = 0
        self.dsem = [self.es.enter_context(nc.semaphore("d%d" % i)) for i in range(N_DMA_SEMS)]
        self.dcnt = [0] * N_DMA_SEMS
        self.ndma = 0
        self.n_ins = 0
        self.n_wait = 0
        self.out_events = []

    def sb(self, name, shape, dt=F32):
        return T(self.es.enter_context(self.nc.sbuf_tensor(name, list(shape), dt)), name)

    def ps(self, name, shape, dt=F32):
        return T(self.es.enter_context(self.nc.psum_tensor(name, list(shape), dt)), name)

    def _wait(self, e, ev):
        en, sem, val, sid = ev
        if en == e and (e == "pe" or not SAME_ENGINE_SYNC):
            return
        w = self.waited[e]
        if w.get(sid, 0) >= val:
            return
        w[sid] = val
        self.eng[e].wait_ge(sem, val)
        self.n_wait += 1

    def _sync(self, e, reads, writes):
        for t in reads:
            if t.lw is not None:
                self._wait(e, t.lw)
        for t in writes:
            if t.lw is not None:
                self._wait(e, t.lw)
            for ev in t.rd.values():
                self._wait(e, ev)

    def _post(self, e, ev, reads, writes, rkey=None):
        for t in reads:
            t.rd[rkey or e] = ev
        for t in writes:
            t.lw = ev
            t.rd = {}

    def emit(self, e, fn, reads, writes):
        reads = _own(reads)
        writes = _own(writes)
        self._sync(e, reads, writes)
        ins = fn()
        self.cnt[e] += 1
        ins.then_inc(self.sem[e], 1)
        ev = (e, self.sem[e], self.cnt[e], "e_" + e)
        self._post(e, ev, reads, writes)
        self.n_ins += 1
        return ev

    def mm(self, out, lhsT, rhs, start=True, stop=True, **kw):
        return self.emit("pe", lambda: self.nc.tensor.matmul(out.ap, lhsT=lhsT.ap, rhs=rhs.ap, start=start, stop=stop, **kw),
                         [lhsT, rhs], [out])

    def tr(self, out, in_, ident):
        return self.emit("pe", lambda: self.nc.tensor.transpose(out.ap, in_.ap, ident.ap), [in_, ident], [out])

    def act(self, out, in_, func, bias=0.0, scale=1.0, e="act"):
        b = bias.ap if isinstance(bias, V) else bias
        s = scale.ap if isinstance(scale, V) else scale
        return self.emit("act", lambda: self.nc.scalar.activation(out=out.ap, in_=in_.ap, func=func, bias=b, scale=s),
                         [in_, bias, scale], [out])

    def tt(self, e, out, in0, in1, op):
        return self.emit(e, lambda: self.eng[e].tensor_tensor(out=out.ap, in0=in0.ap, in1=in1.ap, op=op), [in0, in1], [out])

    def ts(self, e, out, in0, s1, op0, s2=None, op1=None):
        a1 = s1.ap if isinstance(s1, V) else s1
        a2 = s2.ap if isinstance(s2, V) else s2
        kw = {}
        if op1 is not None:
            kw["op1"] = op1
        return self.emit(e, lambda: self.eng[e].tensor_scalar(out=out.ap, in0=in0.ap, scalar1=a1, scalar2=a2, op0=op0, **kw),
                         [in0, s1, s2], [out])

    def stt(self, out, in0, scalar, in1, op0, op1):
        a = scalar.ap if isinstance(scalar, V) else scalar
        return self.emit("dve", lambda: self.nc.vector.scalar_tensor_tensor(out=out.ap, in0=in0.ap, scalar=a, in1=in1.ap, op0=op0, op1=op1),
                         [in0, scalar, in1], [out])

    def copy(self, e, out, in_):
        if e == "act":
            return self.act(out, in_, AF.Copy)
        return self.emit(e, lambda: self.eng[e].tensor_copy(out=out.ap, in_=in_.ap), [in_], [out])

    def memset(self, e, out, val):
        return self.emit(e, lambda: self.eng[e].memset(out.ap, val), [], [out])

    def scan(self, out, d0, d1, init, op0, op1):
        i = init.ap if isinstance(init, V) else init
        return self.emit("dve", lambda: self.nc.vector.tensor_tensor_scan(out=out.ap, data0=d0.ap, data1=d1.ap, initial=i, op0=op0, op1=op1),
                         [d0, d1, init], [out])

    def recip(self, out, in_):
        return self.emit("dve", lambda: self.nc.vector.reciprocal(out=out.ap, in_=in_.ap), [in_], [out])

    def reduce(self, out, in_, op=ALU.add, axis=AX.X):
        return self.emit("dve", lambda: self.nc.vector.tensor_reduce(out=out.ap, in_=in_.ap, axis=axis, op=op), [in_], [out])

    def dma(self, out, in_, is_output=False, extra_reads=(), extra_writes=(), q="sp", **kw):
        e = "pool" if (is_output and q == "sp") else q
        reads = _own([in_]) + list(extra_reads)
        writes = _own([out]) + list(extra_writes)
        self._sync(e, reads, writes)
        j = self.ndma % N_DMA_SEMS
        self.ndma += 1
        sem = self.dsem[j]
        if self.dcnt[j] > 0:
            self._wait(e, ("dma", sem, self.dcnt[j], "d%d" % j))
        self.dcnt[j] += 16
        oa = out.ap if isinstance(out, V) else out
        ia = in_.ap if isinstance(in_, V) else in_
        ins = self.eng[e].dma_start(out=oa, in_=ia, **kw)
        ins.then_inc(sem, 16)
        ev = ("dma", sem, self.dcnt[j], "d%d" % j)
        self._post(e, ev, reads, writes, rkey="dma%d" % self.ndma)
        if is_output:
            self.out_events.append(ev)
        self.n_ins += 1
        return ev

    def finish(self):
        for ev in self.out_events:
            self._wait("sp", ev)
        for j in range(N_DMA_SEMS):
            if self.dcnt[j] > 0:
                self._wait("sp", ("dma", self.dsem[j], self.dcnt[j], "d%d" % j))
        for e in ("pe", "act", "dve", "pool"):
            if self.cnt[e] > 0:
                self._wait("sp", (e, self.sem[e], self.cnt[e], "e_" + e))

    def close(self):
        self.es.close()


import math
from concourse.bass_utils import run_bass_kernel_spmd

DN_ALPHA = 8.0 ** 0.25
C0 = math.exp(-0.5)
D = 1024
FF = 2816
NF = 22
LN_EPS = 1e-5
RMS_EPS = 1e-5
LNX_EPS = 64e-5
MUL, ADD, SUB = ALU.mult, ALU.add, ALU.subtract

WEIGHT_SHAPES = {
    "rw_mix": (2, 6, 1024), "rw_wr": (2, 1024, 1024), "rw_wk": (2, 1024, 1024), "rw_wv": (2, 1024, 1024),
    "rw_wo": (2, 1024, 1024), "rw_w0": (2, 1024), "rw_w1": (2, 1024, 64), "rw_w2": (2, 64, 1024),
    "rw_a0": (2, 1024), "rw_a1": (2, 1024, 64), "rw_a2": (2, 64, 1024), "rw_g1": (2, 1024, 160),
    "rw_g2": (2, 160, 1024), "rw_k_k": (2, 1024), "rw_k_a": (2, 1024), "rw_r_k": (2, 16, 64),
    "rw_lnx_w": (2, 1024), "rw_lnx_b": (2, 1024), "rw_v0": (1, 1024), "rw_v1": (1, 1024, 32),
    "rw_v2": (1, 32, 1024), "hg_wq": (1, 1024, 1024), "hg_wf": (1, 1024, 1024), "hg_wi": (1, 1024, 1024),
    "hg_wg": (1, 1024, 1024), "hg_wo": (1, 1024, 1024), "hg_norm_w": (1, 128), "hg_lb_param": (4, 1024),
    "gl_wq": (1, 1024, 512), "gl_wk": (1, 1024, 512), "gl_wv": (1, 1024, 1024), "gl_wg": (1, 1024, 1024),
    "gl_gk1": (1, 1024, 16), "gl_gk2": (1, 16, 512), "gl_gk_b": (1, 512), "gl_wo": (1, 1024, 1024),
    "gl_norm_w": (1, 256), "ffn_wu": (4, 1024, 2816), "ffn_wg": (4, 1024, 2816), "ffn_conv_w": (4, 3, 2816),
    "ffn_conv_b": (4, 2816), "ffn_wd": (4, 2816, 1024), "ln1_w": (4, 1024), "ln1_b": (4, 1024),
    "ln2_w": (4, 1024), "ln2_b": (4, 1024),
}
IN_SHAPES = {
    "x_prompt": (2048, 1024), "x_sample": (64, 1024), "state_rwkv": (2, 16, 16, 64, 64),
    "state_rwkv_shift": (2, 16, 1024), "state_hgrn": (16, 8, 128, 128), "state_gla": (16, 4, 128, 256),
    "state_ffn_conv": (4, 16, 2, 2816),
}
OUT_SHAPES = {
    "y_prompt": (2048, 1024), "y_sample": (64, 1024), "p_rwkv": (2, 16, 64, 64), "p_shift": (2, 1024),
    "p_hgrn": (8, 128, 128), "p_gla": (4, 128, 256), "p_conv": (4, 2, 2816),
    "s_rwkv": (2, 16, 16, 64, 64), "s_shift": (2, 16, 1024), "s_hgrn": (16, 8, 128, 128),
    "s_gla": (16, 4, 128, 256), "s_conv": (4, 16, 2, 2816),
}


class Arena:
    def __init__(self, k, name, nkb):
        self.t = k.es.enter_context(k.nc.sbuf_tensor(name, [128, nkb * 256], F32))
        self.slots = [T(self.t, "%s%d" % (name, i)) for i in range(nkb)]
        self.nkb = nkb
        self.p = 0

    def reset(self, p=0):
        self.p = p

    def take(self, nbytes, dt=F32, pat=None, parts=128, **kw):
        nkb = (nbytes + 1023) // 1024
        off = self.p
        self.p += nkb
        assert self.p <= self.nkb, "arena overflow %d > %d" % (self.p, self.nkb)
        ap = self.t[0:parts, off * 256: off * 256 + nbytes // 4]
        if dt == BF16:
            ap = ap.bitcast(BF16)
        if pat:
            ap = ap.rearrange(pat, **kw)
        return V(ap, self.slots[off:off + nkb])

    def buf(self, n, w, dt=F32):
        es = 2 if dt == BF16 else 4
        off = self.p
        full = self.take(n * w * es, dt, "p (n w) -> p n w", n=n)
        ch = []
        for c in range(n):
            b0 = c * w * es
            b1 = (c + 1) * w * es
            ch.append(V(full.ap[:, c, :], self.slots[off + b0 // 1024: off + (b1 + 1023) // 1024]))
        return full, ch


def build(nc, dbg_n=0):
    k = KB(nc)
    A = {}
    for n, s in list(IN_SHAPES.items()) + list(WEIGHT_SHAPES.items()):
        A[n] = nc.dram_tensor(n, list(s), F32, kind="ExternalInput").ap()
    for n, s in OUT_SHAPES.items():
        A[n] = nc.dram_tensor(n, list(s), F32, kind="ExternalOutput").ap()
    if dbg_n:
        A["dbg"] = nc.dram_tensor("dbg", [dbg_n, 2112, 1024], F32, kind="ExternalOutput").ap()

    psf_t = k.es.enter_context(nc.psum_tensor("psf", [128, 3584], F32))
    banks = [T(psf_t, "bank%d" % i) for i in range(7)]
    psb = k.ps("psb", [128, 1024], BF16)
    bp = [0]

    def bank(n=1, parts=slice(0, 128)):
        p = bp[0]
        if n == 2:
            if p % 2:
                p += 1
            if p + 2 > 6:
                p = 0
        else:
            if p >= 7:
                p = 0
        bp[0] = p + n
        return V(psf_t[parts, p * 512:(p + n) * 512], banks[p:p + n])

    identf = k.sb("identf", [128, 128])
    identb = k.sb("identb", [128, 128], BF16)
    onesf = k.sb("onesf", [128, 512])
    bones = k.sb("bones", [128, 128])
    onesb = k.sb("onesb", [128, 128], BF16)
    bonesb = k.sb("bonesb", [128, 128], BF16)
    P = k.sb("P", [128, 1024])
    x32t = k.sb("x32", [128, 8 * 512])
    xbft = k.sb("xbf", [128, 8 * 512], BF16)
    vft = k.sb("vf", [128, 8 * 512], BF16)
    WB = [k.sb("wb%d" % i, [128, 1024], BF16) for i in range(8)]
    ST32 = [k.sb("st32_%d" % i, [128, 1024]) for i in range(4)]
    STB = [k.sb("stb_%d" % i, [128, 1024], BF16) for i in range(4)]
    shiftP = [k.sb("shp%d" % i, [128, 8]) for i in range(2)]
    ZH = k.sb("zh", [128, 4 * NF * 2])
    ZHS = k.sb("zhs", [128, NF * 32])
    msk = {}
    for kind, C, blk in (("p", 64, 64), ("s", 4, 32)):
        for nm in ("su", "ui", "sl", "id"):
            msk[kind + nm] = k.sb("m_%s_%s" % (kind, nm), [128, C])
    rmask = {"p": k.sb("rm_p", [128, 1024]), "s": k.sb("rm_s", [128, 256])}
    AR = Arena(k, "ar", 118)

    k.memset("dve", onesf[:], 1.0)
    k.memset("dve", onesb[:], 1.0)
    k.memset("dve", bonesb[:], 0.0)
    k.memset("dve", bonesb[0:64, 0:64], 1.0)
    k.memset("dve", bonesb[64:128, 64:128], 1.0)
    k.memset("dve", bones[:], 0.0)
    k.memset("dve", bones[0:64, 0:64], 1.0)
    k.memset("dve", bones[64:128, 64:128], 1.0)
    k.emit("pool", lambda: nc.gpsimd.affine_select(out=identf.t[:], in_=onesf.t[:, 0:128], pattern=[[-1, 128]],
                                                   compare_op=ALU.is_equal, fill=0.0, base=0, channel_multiplier=1),
           [onesf[:]], [identf[:]])
    k.copy("dve", identb[:], identf[:])
    for kind, C, blk in (("p", 64, 64), ("s", 4, 32)):
        for nm, pat, cm, op, base in (("su", 1, -1, ALU.is_gt, 0), ("ui", 1, -1, ALU.is_ge, 0),
                                      ("sl", -1, 1, ALU.is_gt, 0), ("id", 1, -1, ALU.is_equal, 0)):
            m = msk[kind + nm]
            for b in range(128 // blk):
                k.emit("pool", lambda m=m, b=b, blk=blk, C=C, pat=pat, cm=cm, op=op: nc.gpsimd.affine_select(
                    out=m.t[b * blk:(b + 1) * blk, :], in_=onesf.t[b * blk:(b + 1) * blk, 0:C], pattern=[[pat, C]],
                    compare_op=op, fill=0.0, base=0, channel_multiplier=cm), [onesf[:]], [m[:]])
    k.memset("dve", rmask["p"][:], 1.0)
    k.memset("dve", rmask["p"][:].rr("p (n c) -> p n c", c=64)[:, :, 0:1], 0.0)
    k.memset("dve", rmask["s"][:], 1.0)
    k.memset("dve", rmask["s"][:].rr("p (n c) -> p n c", c=4)[:, :, 0:1], 0.0)
    for t_ in ST32:
        k.memset("dve", t_[:], 0.0)
    for t_ in STB:
        k.memset("dve", t_[:], 0.0)
    for t_ in shiftP:
        k.memset("dve", t_[:], 0.0)
    k.memset("dve", ZH[:], 0.0)

    plist = []

    def addp(name, ap2d, nrows):
        plist.append((name, ap2d, nrows))

    for l in range(2):
        addp("mix%d" % l, A["rw_mix"][l].rearrange("j (c p) -> (j c) p", p=128), 48)
        for nm, src in (("w0", "rw_w0"), ("a0", "rw_a0"), ("kk", "rw_k_k"), ("ka", "rw_k_a"),
                        ("lnxw", "rw_lnx_w"), ("lnxb", "rw_lnx_b")):
            addp("%s%d" % (nm, l), A[src][l].rearrange("(c p) -> c p", p=128), 8)
        addp("rk%d" % l, A["rw_r_k"][l].rearrange("(c a) b -> c (a b)", a=2), 8)
    addp("v0", A["rw_v0"][0].rearrange("(c p) -> c p", p=128), 8)
    addp("hgn", A["hg_norm_w"], 1)
    addp("lbp", A["hg_lb_param"].rearrange("l (c p) -> (l c) p", p=128), 32)
    addp("gkb", A["gl_gk_b"][0].rearrange("(c p) -> c p", p=128), 4)
    addp("gln", A["gl_norm_w"][0].rearrange("(c p) -> c p", p=128), 2)
    for l in range(4):
        addp("cw%d" % l, A["ffn_conv_w"][l].rearrange("j (c p) -> (j c) p", p=128), 66)
        addp("cb%d" % l, A["ffn_conv_b"][l].rearrange("(c p) -> c p", p=128), 22)
        for nm in ("ln1_w", "ln1_b", "ln2_w", "ln2_b"):
            addp("%s%d" % (nm, l), A[nm][l].rearrange("(c p) -> c p", p=128), 8)
    pcol = {}
    slot, row = 0, 0
    place = []
    for name, ap2d, nrows in plist:
        if row + nrows > 128:
            slot += 1
            row = 0
        pcol[name] = slot * 128 + row
        place.append((slot, row, ap2d, nrows))
        row += nrows
    nslots = slot + 1
    assert nslots * 128 + 64 <= 1024
    AR.reset()
    PR = AR.take(nslots * 512, F32, "p (s m) -> p s m", s=nslots)
    k.memset("dve", PR, 0.0)
    for (s_, r_, ap2d, nrows) in place:
        k.dma(PR[r_:r_ + nrows, s_, :], ap2d)
    for s_ in range(nslots):
        b = bank()
        k.tr(b[:, 0:128], PR[:, s_, :], identf[:])
        k.copy("dve", P[:, s_ * 128:(s_ + 1) * 128], b[:, 0:128])
    dcol = nslots * 128

    def pc(name, off=0, n=1):
        c = pcol[name] + off
        return P[:, c:c + n]

    for l in range(2):
        pcol["oka%d" % l] = dcol
        k.ts("dve", P[:, dcol:dcol + 8], pc("ka%d" % l, 0, 8), -1.0, MUL, 1.0, ADD)
        dcol += 8
    pcol["ngkb"] = dcol
    k.ts("dve", P[:, dcol:dcol + 4], pc("gkb", 0, 4), -1.0, MUL)
    dcol += 4
    pcol["lbe"] = dcol
    k.act(P[:, dcol:dcol + 32], pc("lbp", 0, 32), AF.Exp)
    lbe = P[:, dcol:dcol + 32]
    dcol += 32
    pcol["lb1"] = dcol
    pcol["olb1"] = dcol + 8
    lsum = P[:, dcol + 16:dcol + 24]
    k.tt("dve", lsum, lbe[:, 0:8], lbe[:, 8:16], ADD)
    k.tt("dve", lsum, lsum, lbe[:, 16:24], ADD)
    k.tt("dve", lsum, lsum, lbe[:, 24:32], ADD)
    k.recip(lsum, lsum)
    k.tt("dve", P[:, dcol:dcol + 8], lbe[:, 8:16], lsum, MUL)
    k.ts("dve", P[:, dcol + 8:dcol + 16], P[:, dcol:dcol + 8], -1.0, MUL, 1.0, ADD)
    dcol += 24
    assert dcol <= 1024

    x32 = x32t[:].rr("p (c w) -> p c w", c=8)
    xbf = xbft[:].rr("p (c w) -> p c w", c=8)
    vf = vft[:].rr("p (c w) -> p c w", c=8)

    wctr = [0]

    class WV:
        def __init__(self, ap, trk):
            self.ap = ap
            self.trk = trk

        def __getitem__(self, i):
            return WV(self.ap[i], self.trk)

        def rearrange(self, pat, **kw):
            return WV(self.ap.rearrange(pat, **kw), self.trk)

    class WTn:
        def __init__(self, ap, trks):
            self.ap = ap
            self.trks = trks

        def __getitem__(self, l):
            return WV(self.ap[l], self.trks[l])

    MATW = ["rw_wr", "rw_wk", "rw_wv", "rw_wo", "rw_w1", "rw_w2", "rw_a1", "rw_a2", "rw_g1", "rw_g2", "rw_v1", "rw_v2",
            "hg_wq", "hg_wf", "hg_wi", "hg_wg", "hg_wo", "gl_wq", "gl_wk", "gl_wv", "gl_wg", "gl_gk1", "gl_gk2", "gl_wo",
            "ffn_wu", "ffn_wg", "ffn_wd"]
    shadow = {}
    for n_ in MATW:
        sh = nc.dram_tensor(n_ + "_bf", list(WEIGHT_SHAPES[n_]), BF16, kind="Internal").ap()
        shadow[n_] = WTn(sh, [T(None, "%s_%d" % (n_, l_)) for l_ in range(WEIGHT_SHAPES[n_][0])])

    def cast_w(n_, l_):
        src = A[n_][l_]
        dst = shadow[n_].ap[l_]
        cols = WEIGHT_SHAPES[n_][2]
        if cols > 2048:
            src = src.rearrange("k (a b) -> (k a) b", a=4)
            dst = dst.rearrange("k (a b) -> (k a) b", a=4)
        k.dma(dst, src, extra_writes=[shadow[n_].trks[l_]], q="pool")

    RWN = ["rw_w1", "rw_a1", "rw_g1", "rw_wr", "rw_wk", "rw_wv", "rw_w2", "rw_a2", "rw_g2", "rw_wo"]
    FFN = ["ffn_wu", "ffn_wg", "ffn_wd"]
    for n_ in RWN:
        cast_w(n_, 0)
    for n_ in FFN:
        cast_w(n_, 0)
    for n_ in ["hg_wq", "hg_wf", "hg_wg", "hg_wi", "hg_wo"]:
        cast_w(n_, 0)
    for n_ in FFN:
        cast_w(n_, 1)
    for n_ in ["gl_wq", "gl_gk1", "gl_gk2", "gl_wk", "gl_wg", "gl_wv", "gl_wo"]:
        cast_w(n_, 0)
    for n_ in FFN:
        cast_w(n_, 2)
    for n_ in RWN:
        cast_w(n_, 1)
    for n_ in ["rw_v1", "rw_v2"]:
        cast_w(n_, 0)
    for n_ in FFN:
        cast_w(n_, 3)
    for n_ in MATW:
        A[n_] = shadow[n_]

    def wload(wv, p, kk, m):
        i = wctr[0]
        wctr[0] += 1
        assert kk * m <= 1024
        wb = WB[i % 8][0:p, 0:kk * m].rr("p (k m) -> p k m", k=kk)
        k.dma(wb, wv.ap, extra_reads=[wv.trk])
        return wb

    w2state = {}

    def wmat2(w2d, kc, c0, m):
        if kc % 2 == 0:
            w2state["wb"] = wload(w2d[kc * 128:(kc + 2) * 128, c0:c0 + m].rearrange("(o p) m -> p o m", p=128), 128, 2, m)
        return w2state["wb"][:, kc % 2, :]

    def wmat(w2d, k0, c0, m):
        return wload(w2d[k0 * 128:(k0 + 1) * 128, c0:c0 + m].rearrange("p (o m) -> p o m", o=1), 128, 1, m)[:, 0, :]

    F32R = mybir.dt.float32r

    def mmr(out, lhsT, rhs, **kw):
        return k.mm(out, lhsT, rhs, **kw)

    def r32(v):
        return v

    evac_rr = [0]

    def evac(out, in_):
        evac_rr[0] += 1
        k.copy("act" if evac_rr[0] % 2 else "dve", out, in_)

    def proj_fm(w2d, nk, ncols, rhs_fn, W, consume, col0=0):
        c = 0
        while c < ncols:
            g = min(512, ncols - c)
            nm = g // 128
            bs = [bank() for _ in range(nm)]
            for kc in range(nk):
                wb = wmat(w2d, kc, col0 + c, g)
                for mi in range(nm):
                    k.mm(bs[mi][:, 0:W], wb[:, mi * 128:(mi + 1) * 128], rhs_fn(kc), start=(kc == 0), stop=(kc == nk - 1))
            for mi in range(nm):
                consume((c // 128) + mi, bs[mi][:, 0:W])
            c += g

    def ln_sublayer(h_chunks, W, wname, bname):
        Z, Zc = AR.buf(8, W, F32)
        sq = AR.take(W * 4)
        for c in range(8):
            k.stt(Zc[c], x32[:, c, 0:W], DN_ALPHA, h_chunks[c], MUL, ADD)
        mu = bank()
        for c in range(8):
            k.mm(mu[:, 0:W], onesf[:, 0:128], Zc[c], start=(c == 0), stop=(c == 7))
        for c in range(8):
            k.stt(Zc[c], mu[:, 0:W], -1.0 / D, Zc[c], MUL, ADD)
        var = bank()
        for c in range(8):
            sqb = sq.bitcast(BF16)[:, 0:W]
            k.act(sqb, Zc[c], AF.Square)
            k.mm(var[:, 0:W], onesb[:], sqb, start=(c == 0), stop=(c == 7))
        rstd = AR.take(W * 4)
        k.act(rstd, var[:, 0:W], AF.Sqrt, bias=LN_EPS, scale=1.0 / D)
        k.recip(rstd, rstd)
        for c in range(8):
            k.tt("dve", Zc[c], Zc[c], rstd, MUL)
            k.ts("dve", x32[:, c, 0:W], Zc[c], pc(wname, c), MUL, pc(bname, c), ADD)
            k.copy("act", xbf[:, c, 0:W], x32[:, c, 0:W])


    cst = k.sb("cst", [128, 8])
    k.memset("dve", cst[:, 0:1], LN_EPS)
    k.memset("dve", cst[:, 1:2], RMS_EPS)
    k.memset("dve", cst[:, 2:3], LNX_EPS)
    k.memset("dve", cst[:, 3:4], 1.0)
    k.memset("dve", cst[:, 4:5], 0.0)
    k.memset("dve", cst[:, 7:8], 1e-18)
    k.memset("dve", cst[:, 5:7], 0.0)
    k.memset("dve", cst[0:64, 5:6], 1.0)
    k.memset("dve", cst[64:128, 6:7], 1.0)

    class Tl:
        pass

    def acc_group(w2d, nk, col, g, rhs_fn, W, extra=None):
        nm = (g + 127) // 128
        bs = [bank() for _ in range(nm)]
        for kc in range(nk):
            if nk % 2 == 0:
                if kc % 2 == 0:
                    wb2 = wload(w2d[kc * 128:(kc + 2) * 128, col:col + g].rearrange("(o p) m -> p o m", p=128), 128, 2, g)
                wb = wb2[:, kc % 2, :]
            else:
                wb = wmat(w2d, kc, col, g)
            for mi in range(nm):
                mw = min(128, g - mi * 128)
                k.mm(bs[mi][0:mw, 0:W], wb[:, mi * 128:mi * 128 + mw], rhs_fn(kc), start=(kc == 0), stop=(kc == nk - 1))
            if extra is not None:
                extra(kc, wb)
        return bs

    ln_state = {}

    def ln_begin(W):
        Z, Zc = AR.buf(8, W, F32)
        ln_state["Zc"] = Zc
        ln_state["Z"] = Z
        ln_state["W"] = W

    def ln_add(c, h_ps, col0=0):
        W = ln_state["W"]
        k.stt(ln_state["Zc"][c], x32[:, c, col0:col0 + W], DN_ALPHA, h_ps, MUL, ADD)

    def ln_finish(wname, bname, col0=0):
        W = ln_state["W"]
        Zc = ln_state["Zc"]
        sq = [AR.take(W * 2, BF16) for _ in range(2)]
        mean = AR.take(W * 4)
        rstd = AR.take(W * 4)
        mu = bank()
        ss = bank()
        for c in range(8):
            k.act(sq[c % 2], Zc[c], AF.Square)
            k.mm(mu[:, 0:W], onesf[:, 0:128], Zc[c], start=(c == 0), stop=(c == 7))
            k.mm(ss[:, 0:W], onesb[:], sq[c % 2], start=(c == 0), stop=(c == 7))
        k.act(mean, mu[:, 0:W], AF.Identity, scale=1.0 / D)
        k.act(rstd, mean, AF.Square)
        k.stt(rstd, ss[:, 0:W], 1.0 / D, rstd, MUL, SUB)
        k.act(rstd, rstd, AF.Ln, bias=cst[:, 0:1])
        k.act(rstd, rstd, AF.Exp, scale=-0.5)
        if W <= 64:
            Z = ln_state["Z"]
            k.tt("dve", Z, Z, mean.us(1).bc([128, 8, W]), SUB)
            k.tt("dve", Z, Z, rstd.us(1).bc([128, 8, W]), MUL)
            wc = pcol[wname]
            bc_ = pcol[bname]
            k.tt("dve", Z, Z, P[:, wc:wc + 8].us(2).bc([128, 8, W]), MUL)
            k.tt("dve", x32[:, :, col0:col0 + W], Z, P[:, bc_:bc_ + 8].us(2).bc([128, 8, W]), ADD)
            k.copy("act", xbf[:, :, col0:col0 + W], x32[:, :, col0:col0 + W])
            return
        for c in range(8):
            k.tt("dve", Zc[c], Zc[c], mean, SUB)
            k.tt("dve", Zc[c], Zc[c], rstd, MUL)
            k.act(x32[:, c, col0:col0 + W], Zc[c], AF.Identity, bias=pc(bname, c), scale=pc(wname, c))
            k.act(xbf[:, c, col0:col0 + W], Zc[c], AF.Identity, bias=pc(bname, c), scale=pc(wname, c))

    def load_tile(tl):
        AR.reset()
        if tl.kind == "p":
            for j in range(4):
                XL = AR.take(4096)
                k.dma(XL, A["x_prompt"][tl.t0 + j * 128: tl.t0 + (j + 1) * 128, :])
                for hb in range(2):
                    b = bank()
                    for cc in range(4):
                        c = hb * 4 + cc
                        k.tr(b[:, cc * 128:(cc + 1) * 128], XL[:, c * 128:(c + 1) * 128], identf[:])
                    evac(x32[:, hb * 4:(hb + 1) * 4, j * 128:(j + 1) * 128], b[:, 0:512].rr("p (c w) -> p c w", c=4))
        else:
            XL = AR.take(4096)
            k.dma(XL[0:64, :], A["x_sample"][:, :])
            b = bank()
            for c in range(8):
                k.tr(b[:, c * 64:(c + 1) * 64], XL[0:64, c * 128:(c + 1) * 128], identf[0:64, 0:64])
            evac(x32[:, :, 0:64], b[:, 0:512].rr("p (c w) -> p c w", c=8))
        k.copy("act", xbf[:, :, 0:tl.W], x32[:, :, 0:tl.W])

    def store_rows(dram_rows, src_fn, n, ncol_chunks=8, is_output=True):
        YO = AR.take(ncol_chunks * 512)
        c = 0
        while c < ncol_chunks:
            g = min(4, ncol_chunks - c)
            b = bank()
            for cc in range(g):
                k.tr(b[0:n, cc * 128:(cc + 1) * 128], src_fn(c + cc), identf[:])
            evac(YO[0:n, c * 128:(c + g) * 128], b[0:n, 0:g * 128])
            c += g
        k.dma(dram_rows, YO[0:n, 0:ncol_chunks * 128], is_output=is_output)

    def store_tile(tl, dram, row0):
        AR.reset()
        if tl.kind == "p":
            for j in range(4):
                store_rows(dram[row0 + tl.t0 + j * 128: row0 + tl.t0 + (j + 1) * 128, :],
                           lambda c, j=j: x32[:, c, j * 128:(j + 1) * 128], 128)
        else:
            store_rows(dram[row0:row0 + 64, :], lambda c: x32[:, c, 0:64], 64)

    def ffn_sublayer(l, tl):
        W, nseq, L = tl.W, tl.nseq, tl.L
        AR.reset()
        MT, MTc = AR.buf(NF, W, BF16)
        zcs = [AR.take((W + 2 * nseq) * 4, F32, "p (s l) -> p s l", s=nseq) for _ in range(3)]
        ubs = [AR.take(W * 4, F32, "p (s l) -> p s l", s=nseq) for _ in range(3)]
        t1 = AR.take(W * 4, F32, "p (s l) -> p s l", s=nseq)
        s1 = AR.take(W * 4, F32, "p (s l) -> p s l", s=nseq)
        want_tm = (tl.kind == "s") or tl.last
        if want_tm:
            nsel = 64 if tl.kind == "s" else 2
            ZTM = AR.take(FF * 4)
            sel0 = 0 if tl.kind == "s" else W - 2
        if tl.kind == "s":
            ZR = AR.take(FF * 4)
            k.dma(ZR[0:32, :], A["state_ffn_conv"][l].rearrange("s j f -> (s j) f"))
            for f0 in range(0, NF, 16):
                g = min(16, NF - f0)
                b = bank()
                for ff in range(g):
                    k.tr(b[:, ff * 32:(ff + 1) * 32], ZR[0:32, (f0 + ff) * 128:(f0 + ff + 1) * 128], identf[0:32, 0:32])
                evac(ZHS[:, f0 * 32:(f0 + g) * 32], b[:, 0:g * 32])
        cw = pcol["cw%d" % l]
        cb = pcol["cb%d" % l]
        col = 0
        if tl.kind == "s":
            def a4(nb):
                return AR.take(nb, F32, "p (m s l) -> p m s l", m=4, s=16)
            ZC4, U4, T4, TM4, S4 = a4(4 * 16 * 6 * 4), a4(1024), a4(1024), a4(1024), a4(1024)
            while col < FF:
                g = min(512, FF - col)
                nm = g // 128
                f0 = col // 128
                bu, bz, ztb = bank(), bank(), bank()
                for (wname, bk, tm) in (("ffn_wu", bu, False), ("ffn_wg", bz, True)):
                    for kc in range(8):
                        wb = wmat2(A[wname][l], kc, col, g)
                        for mi in range(nm):
                            k.mm(bk[:, mi * 64:(mi + 1) * 64], wb[:, mi * 128:(mi + 1) * 128], xbf[:, kc, 0:64],
                                 start=(kc == 0 and mi == 0), stop=(kc == 7), skip_group_check=True)
                        if tm:
                            k.mm(ztb[0:64, 0:g], xbf[:, kc, 0:64], wb, start=(kc == 0), stop=(kc == 7))
                evac(ZTM[0:64, col:col + g], ztb[0:64, 0:g])

                def v4(b_):
                    return b_[:, 0:nm * 64].rr("p (m s l) -> p m s l", m=nm, s=16)

                def pb(base):
                    return P[:, base + f0:base + f0 + nm].us(2).us(3).bc([128, nm, 16, 4])
                k.copy("act", ZC4[:, 0:nm, :, 2:6], v4(bz))
                k.copy("act", U4[:, 0:nm], v4(bu))
                k.copy("dve", ZC4[:, 0:nm, :, 0:2], ZHS[:, f0 * 32:(f0 + nm) * 32].rr("p (m s j) -> p m s j", m=nm, j=2))
                k.tt("dve", T4[:, 0:nm], ZC4[:, 0:nm, :, 2:6], pb(cw + 2 * NF), MUL)
                k.tt("dve", T4[:, 0:nm], T4[:, 0:nm], pb(cb), ADD)
                k.tt("dve", TM4[:, 0:nm], ZC4[:, 0:nm, :, 1:5], pb(cw + NF), MUL)
                k.tt("dve", T4[:, 0:nm], T4[:, 0:nm], TM4[:, 0:nm], ADD)
                k.tt("dve", TM4[:, 0:nm], ZC4[:, 0:nm, :, 0:4], pb(cw), MUL)
                k.tt("dve", T4[:, 0:nm], T4[:, 0:nm], TM4[:, 0:nm], ADD)
                k.act(S4[:, 0:nm], T4[:, 0:nm], AF.Silu)
                k.tt("dve", MT[:, f0:f0 + nm, :].rr("p m (s l) -> p m s l", s=16), S4[:, 0:nm], U4[:, 0:nm], MUL)
                col += g
        while col < FF:
            g = min(384, FF - col)
            bu = acc_group(A["ffn_wu"][l], 8, col, g, lambda kc: xbf[:, kc, 0:W], W)
            ztb = bank() if want_tm else None

            def extra(kc, wb):
                k.mm(ztb[0:nsel, 0:g], xbf[:, kc, sel0:sel0 + nsel], wb, start=(kc == 0), stop=(kc == 7))
            bz = acc_group(A["ffn_wg"][l], 8, col, g, lambda kc: xbf[:, kc, 0:W], W, extra if want_tm else None)
            if want_tm:
                evac(ZTM[0:nsel, col:col + g], ztb[0:nsel, 0:g])
            for mi in range(g // 128):
                k.copy("act", zcs[mi][:, :, 2:L + 2], bz[mi][:, 0:W].rr("p (s l) -> p s l", s=nseq))
                k.copy("act", ubs[mi], bu[mi][:, 0:W].rr("p (s l) -> p s l", s=nseq))
            for mi in range(g // 128):
                fi = col // 128 + mi
                zc = zcs[mi]
                if tl.kind == "p":
                    zh = ZH[:, (l * NF + fi) * 2:(l * NF + fi) * 2 + 2]
                    k.copy("dve", zc[:, 0, 0:2], zh)
                    k.copy("dve", zh, zc[:, 0, L:L + 2])
                else:
                    k.copy("dve", zc[:, :, 0:2], ZHS[:, fi * 32:(fi + 1) * 32].rr("p (s j) -> p s j", j=2))
                k.ts("dve", t1, zc[:, :, 2:L + 2], P[:, cw + 2 * NF + fi:cw + 2 * NF + fi + 1], MUL, P[:, cb + fi:cb + fi + 1], ADD)
                k.stt(t1, zc[:, :, 1:L + 1], P[:, cw + NF + fi:cw + NF + fi + 1], t1, MUL, ADD)
                k.stt(t1, zc[:, :, 0:L], P[:, cw + fi:cw + fi + 1], t1, MUL, ADD)
                k.act(s1, t1, AF.Silu)
                k.tt("dve", MTc[fi].rr("p (s l) -> p s l", s=nseq), s1, ubs[mi], MUL)
            col += g
        if want_tm:
            if tl.kind == "s":
                for j in range(2):
                    k.dma(A["s_conv"][l, :, j, :], ZTM[2 + j:64:4, :], is_output=True)
            else:
                k.dma(A["p_conv"][l], ZTM[0:2, :], is_output=True)
        ln_begin(W)
        for col in (0, 512):
            bs = acc_group(A["ffn_wd"][l], NF, col, 512, lambda kc: MTc[kc], W)
            for mi in range(4):
                ln_add(col // 128 + mi, bs[mi][:, 0:W])
        ln_finish("ln2_w%d" % l, "ln2_b%d" % l)

    def dbg_dump(idx, tl, row0):
        if not dbg_n or idx >= dbg_n:
            return
        store_tile(tl, A["dbg"][idx], row0)


    def groups_of(kind, Wm):
        gs = []
        if kind == "p":
            for g in range(Wm // 128):
                gs.append([(128 * g, 0, None), (128 * g + 64, 64, None)])
        else:
            for s0 in range(0, 16, 3):
                gs.append([(4 * (s0 + jj), 32 * jj, s0 + jj) for jj in range(min(3, 16 - s0))])
        return gs

    def to_tokmajor(dst, srcs, kind, g, ch):
        n = len(srcs)
        for i in range(n):
            if kind == "p":
                k.tr(psb[:, i * 128:(i + 1) * 128], srcs[i][:, 128 * g:128 * g + 128], identb[:])
            else:
                for (c0, pb, sidx) in ch:
                    k.tr(psb[pb:pb + 4, i * 128:(i + 1) * 128], srcs[i][:, c0:c0 + 4], identb[:])
        evac(dst[:, 0:n * 128], psb[:, 0:n * 128])

    def gla_like(l, tl, cfg):
        W, kind = tl.W, tl.kind
        H, VC = cfg["H"], cfg["VC"]
        hg = cfg["hg"]
        C = 64 if kind == "p" else 4
        nch = W // C
        ref = (C - 1) // 2
        scale = 128.0 ** -0.5
        AR.reset()
        QI, QIc = AR.buf(H, W, BF16)
        KI, KIc = AR.buf(H, W, BF16)
        QE, QEc = AR.buf(H, W, BF16)
        KD, KDc = AR.buf(H, W, BF16)
        GS, GSc = AR.buf(8, W, BF16)
        OT, OTc = AR.buf(8, W, F32)
        OG, OGc = AR.buf(8, W, BF16)
        ELt = AR.take(H * nch * 4, F32, "p (h n) -> p h n", h=H)
        tA, tB, tC, tD = [AR.take(W * 4) for _ in range(4)]
        grs = groups_of(kind, W)
        VT = [AR.take(2048, BF16) for _ in grs]
        KDT = [AR.take(H * 256, BF16) for _ in grs]
        ATsb = AR.take(H * C * 2, BF16, "p (h c) -> p h c", h=H)
        HK = AR.take(W * 2, BF16)
        FB, FBc = AR.buf(8 if hg else 4, W, F32)
        xr = lambda kc: xbf[:, kc, 0:W]

        for col in range(0, H * 128, 512):
            bs = acc_group(cfg["wq"], 8, col, 512, xr, W)
            for mi in range(4):
                h = col // 128 + mi
                if hg:
                    k.act(tA, bs[mi][:, 0:W], AF.Silu)
                    k.ts("dve", OGc[h], tA, scale, MUL)
                else:
                    k.act(OGc[h], bs[mi][:, 0:W], AF.Identity, scale=scale)

        def finalize(h, lg, kraw, qraw):
            b = tD
            k.scan(b, rmask[kind][:, 0:W], lg, 0.0, MUL, ADD)
            b3 = b.rr("p (n c) -> p n c", c=C)
            tA3 = tA.rr("p (n c) -> p n c", c=C)
            k.tt("dve", tA3, b3, b3[:, :, ref:ref + 1].bc([128, nch, C]), SUB)
            k.act(tB, tA, AF.Exp)
            k.tt("dve", QIc[h], qraw, tB, MUL)
            k.act(tB, tA, AF.Exp, scale=-1.0)
            k.tt("dve", KIc[h], kraw, tB, MUL)
            k.act(tB, b, AF.Exp)
            k.tt("dve", QEc[h], qraw, tB, MUL)
            k.tt("dve", tA3, b3[:, :, C - 1:C].bc([128, nch, C]), b3, SUB)
            k.act(tB, tA, AF.Exp)
            k.tt("dve", KDc[h], kraw, tB, MUL)
            k.act(ELt[:, h, :], b3[:, :, C - 1], AF.Exp)

        def gate_block(col):
            bs = acc_group(cfg["wg"], 8, col, 512, xr, W)
            for mi in range(4):
                k.act(GSc[col // 128 + mi], bs[mi][:, 0:W], AF.Silu)

        def v_block(col):
            vb = [bank() for _ in grs]
            for kc in range(8):
                wb = wmat2(cfg["wv"], kc, col, 512)
                for g, ch in enumerate(grs):
                    if kind == "p":
                        k.mm(vb[g][:, 0:512], xbf[:, kc, 128 * g:128 * g + 128], wb, start=(kc == 0), stop=(kc == 7))
                    else:
                        for (c0, pb, sidx) in ch:
                            k.mm(vb[g][pb:pb + C, 0:512], xbf[:, kc, c0:c0 + C], wb, start=(kc == 0), stop=(kc == 7))
            for g in range(len(grs)):
                evac(VT[g][:, col:col + 512], vb[g][:, 0:512])

        if hg:
            for col in range(0, 1024, 512):
                bs = acc_group(cfg["wk"], 8, col, 512, xr, W)
                for mi in range(4):
                    k.act(FBc[col // 128 + mi], bs[mi][:, 0:W], AF.Sigmoid)
            gate_block(0)
            gate_block(512)
            v_block(0)
            v_block(512)
            for h in range(8):
                k.ts("dve", tA, FBc[h], pc("olb1", h), MUL, pc("lb1", h), ADD)
                k.act(tB, tA, AF.Ln)
                k.ts("dve", tC, tA, -1.0, MUL, 1.0, ADD)
                finalize(h, tB, tC, OGc[h])
        else:
            g1 = wload(A["gl_gk1"][0].rearrange("(k p) m -> p k m", p=128), 128, 8, 16)
            b = bank()
            for kc in range(8):
                k.mm(b[0:16, 0:W], g1[:, kc, :], xr(kc), start=(kc == 0), stop=(kc == 7))
            k.copy("act", HK[0:16, :], b[0:16, 0:W])
            g2 = wload(A["gl_gk2"][0].rearrange("p (o m) -> p o m", o=1), 16, 1, 512)[:, 0, :]
            LG, LGc = AR.buf(4, W, F32)
            for h in range(4):
                b = bank()
                k.mm(b[:, 0:W], g2[:, h * 128:(h + 1) * 128], HK[0:16, :])
                k.act(tA, b[:, 0:W], AF.Exp, bias=pc("ngkb", h), scale=-1.0)
                k.act(tB, tA, AF.Ln, bias=cst[:, 3:4])
                k.ts("dve", LGc[h], tB, -1.0 / 16.0, MUL)
            bs = acc_group(cfg["wk"], 8, 0, 512, xr, W)
            for h in range(4):
                k.copy("act", FBc[h], bs[h][:, 0:W])
            gate_block(0)
            gate_block(512)
            v_block(0)
            v_block(512)
            for h in range(4):
                finalize(h, LGc[h], FBc[h], OGc[h])

        for g in range(len(grs)):
            to_tokmajor(KDT[g], KDc, kind, g, grs[g])

        hv = H * VC * 128
        st_dram_in, st_dram_out_p, st_dram_out_s = cfg["st_in"], cfg["st_out_p"], cfg["st_out_s"]

        def st_view(i):
            return (ST32[i][:, 0:hv].rr("p (h v) -> p h v", h=H), STB[i][:, 0:hv].rr("p (h v) -> p h v", h=H))

        for g, ch in enumerate(grs):
            if kind == "s":
                for (c0, pb, sidx) in ch:
                    S32, Sbf = st_view(sidx % 4)
                    k.dma(S32, st_dram_in[sidx].rearrange("h k v -> k h v"))
                    k.copy("act", Sbf, S32)
            atp = bank()
            for (c0, pb, sidx) in ch:
                for h in range(H):
                    k.mm(atp[pb:pb + C, h * C:(h + 1) * C], KIc[h][:, c0:c0 + C], QIc[h][:, c0:c0 + C])
            k.tt("dve", ATsb, atp[:, 0:H * C].rr("p (h c) -> p h c", h=H), msk[kind + "ui"][:].us(1).bc([128, H, C]), MUL)
            for ci, (c0, pb, sidx) in enumerate(ch):
                n = c0 // C
                if kind == "p":
                    S32, Sbf = st_view(cfg["st"])
                else:
                    S32, Sbf = st_view(sidx % 4)
                ops = bank()
                for h in range(H):
                    for jv in range(VC):
                        cc = h * VC + jv
                        k.mm(ops[:, cc * C:(cc + 1) * C], VT[g][pb:pb + C, cc * 128:(cc + 1) * 128], ATsb[pb:pb + C, h, :],
                             start=(cc == 0), stop=False, skip_group_check=True)
                for h in range(H):
                    for jv in range(VC):
                        cc = h * VC + jv
                        k.mm(ops[:, cc * C:(cc + 1) * C], Sbf[:, h, jv * 128:(jv + 1) * 128], QEc[h][:, c0:c0 + C],
                             start=False, stop=True, skip_group_check=True)
                evac(OT[:, :, c0:c0 + C], ops[:, 0:8 * C].rr("p (c w) -> p c w", c=8))
                sps = bank(2)
                for h in range(H):
                    k.mm(sps[:, h * VC * 128:(h + 1) * VC * 128], KDT[g][pb:pb + C, h * 128:(h + 1) * 128],
                         VT[g][pb:pb + C, h * VC * 128:(h + 1) * VC * 128])
                for h in range(H):
                    k.stt(S32[:, h, :], S32[:, h, :], ELt[:, h, n:n + 1], sps[:, h * VC * 128:(h + 1) * VC * 128], MUL, ADD)
                if kind == "p":
                    k.copy("act", Sbf, S32)
                if kind == "s":
                    k.dma(st_dram_out_s[sidx].rearrange("h k v -> k h v"), S32, is_output=True)
                elif tl.last and g == len(grs) - 1 and ci == len(ch) - 1:
                    k.dma(st_dram_out_p.rearrange("h k v -> k h v"), S32, is_output=True)

        for h in range(H):
            ms = bank()
            for jv in range(VC):
                tAb = tA.bitcast(BF16)[:, 0:W]
                k.act(tAb, OTc[h * VC + jv], AF.Square)
                k.mm(ms[:, 0:W], onesb[:], tAb, start=(jv == 0), stop=(jv == VC - 1))
            k.act(tB, ms[:, 0:W], AF.Sqrt, bias=cst[:, 1:2], scale=1.0 / (VC * 128))
            k.recip(tB, tB)
            for jv in range(VC):
                cc = h * VC + jv
                k.stt(tC, OTc[cc], pc(cfg["norm"], jv), tB, MUL, MUL)
                k.tt("dve", OGc[cc], tC, GSc[cc], MUL)
        AR.reset(0)
        ln_begin(W)
        for col in (0, 512):
            bs = acc_group(cfg["wo"], 8, col, 512, lambda kc: OGc[kc], W)
            for mi in range(4):
                ln_add(col // 128 + mi, bs[mi][:, 0:W])
        ln_finish("ln1_w%d" % l, "ln1_b%d" % l)

    HGCFG = dict(hg=True, H=8, VC=1, wq=A["hg_wq"][0], wk=A["hg_wf"][0], wv=A["hg_wi"][0], wg=A["hg_wg"][0],
                 wo=A["hg_wo"][0], norm="hgn", st=1, st_in=A["state_hgrn"], st_out_p=A["p_hgrn"], st_out_s=A["s_hgrn"])
    GLCFG = dict(hg=False, H=4, VC=2, wq=A["gl_wq"][0], wk=A["gl_wk"][0], wv=A["gl_wv"][0], wg=A["gl_wg"][0],
                 wo=A["gl_wo"][0], norm="gln", st=2, st_in=A["state_gla"], st_out_p=A["p_gla"], st_out_s=A["s_gla"])


    def rwkv_mixer(l, j, tl, col0, Wm):
        kind = tl.kind
        C = 64 if kind == "p" else 4
        nch = Wm // C
        AR.reset()
        RT, RTc = AR.buf(8, Wm, BF16)
        KH, KHc = AR.buf(8, Wm, BF16)
        BH, BHc = AR.buf(8, Wm, BF16)
        KT, KTc = AR.buf(8, Wm, BF16)
        VB, VBc = AR.buf(8, Wm, BF16)
        G, Gc = AR.buf(8, Wm, BF16)
        BV, BVc = AR.buf(8, Wm, F32)
        PCt = AR.take(8 * nch * 4, F32, "p (c n) -> p c n", c=8)
        mark = AR.p
        XX, XXc = AR.buf(8, Wm, BF16)
        XR, XRc = AR.buf(8, Wm, BF16)
        XK, XKc = AR.buf(8, Wm, BF16)
        XV, XVc = AR.buf(8, Wm, BF16)
        XT, XTc = AR.buf(8, Wm, BF16)
        HW = AR.take(Wm * 2, BF16)
        HA = AR.take(Wm * 2, BF16)
        HG0 = AR.take(Wm * 2, BF16)
        HG1 = AR.take(Wm * 2, BF16)
        HV = AR.take(Wm * 2, BF16)
        RAWr, RAWrc = AR.buf(4, Wm, F32)
        RAWk, RAWkc = AR.buf(4, Wm, F32)
        RAWv, RAWvc = AR.buf(4, Wm, F32)
        t = [AR.take(Wm * 16) for _ in range(8)]
        xv_ = x32[:, :, col0:col0 + Wm]

        if kind == "p":
            k.tt("dve", XX[:, :, 1:Wm], xv_[:, :, 0:Wm - 1], xv_[:, :, 1:Wm], SUB)
            k.tt("dve", XX[:, :, 0], shiftP[j][:, :], xv_[:, :, 0], SUB)
            k.copy("dve", shiftP[j][:, :], xv_[:, :, Wm - 1])
            if tl.last and col0 + Wm == tl.W:
                store_rows(A["p_shift"][j:j + 1, :], lambda c: shiftP[j][:, c:c + 1], 1)
        else:
            SR = AR.take(4096)
            shS = AR.take(512, F32, "p (c s) -> p c s", c=8)
            k.dma(SR[0:16, :], A["state_rwkv_shift"][j])
            b = bank()
            for c in range(8):
                k.tr(b[:, c * 16:(c + 1) * 16], SR[0:16, c * 128:(c + 1) * 128], identf[0:16, 0:16])
            evac(shS, b[:, 0:128].rr("p (c s) -> p c s", c=8))
            x4 = xv_.rr("p c (s t) -> p c s t", t=4)
            XX4 = XX.rr("p c (s t) -> p c s t", t=4)
            for c in range(8):
                k.tt("dve", XX4[:, c, :, 1:4], x4[:, c, :, 0:3], x4[:, c, :, 1:4], SUB)
            k.tt("dve", XX4[:, :, :, 0], shS, x4[:, :, :, 0], SUB)
            store_rows(A["s_shift"][j], lambda c: x32[:, c, 3:64:4], 16)
        mc = pcol["mix%d" % j]

        def mix(dst_c, jj):
            for c in range(8):
                k.stt(dst_c[c], XXc[c], P[:, mc + jj * 8 + c:mc + jj * 8 + c + 1], x32[:, c, col0:col0 + Wm], MUL, ADD)

        def lora1(w3d, m, dst, pb, func, src_c):
            wb = wload(w3d, 128, 8, m)
            b = bank()
            for kc in range(8):
                k.mm(b[pb:pb + m, 0:Wm], wb[:, kc, :], src_c[kc], start=(kc == 0), stop=(kc == 7))
            k.act(dst[pb:pb + m, :], b[pb:pb + m, 0:Wm], func)

        mix(XTc, 1)
        lora1(A["rw_w1"][j].rearrange("(k p) m -> p k m", p=128), 64, HW, 0, AF.Tanh, XTc)
        mix(XTc, 4)
        lora1(A["rw_a1"][j].rearrange("(k p) m -> p k m", p=128), 64, HA, 0, AF.Copy, XTc)
        mix(XTc, 5)
        g1v = A["rw_g1"][j].rearrange("(k p) m -> p k m", p=128)
        lora1(g1v[:, :, 0:64], 64, HG0, 0, AF.Sigmoid, XTc)
        lora1(g1v[:, :, 64:128], 64, HG0, 64, AF.Sigmoid, XTc)
        lora1(g1v[:, :, 128:160], 32, HG1, 0, AF.Sigmoid, XTc)
        mix(XRc, 0)
        mix(XKc, 2)
        mix(XVc, 3)
        if j > 0:
            lora1(A["rw_v1"][0].rearrange("(k p) m -> p k m", p=128), 32, HV, 0, AF.Copy, XVc)

        def wrow(w2d, r0, r1, col):
            return wload(w2d[r0:r1, col:col + 512].rearrange("p (o m) -> p o m", o=1), r1 - r0, 1, 512)[:, 0, :]

        def g4(v):
            return v.rr("p (c w) -> p c w", c=4)

        def pb4(name, c0):
            cc = pcol[name] + c0
            return P[:, cc:cc + 4].us(2).bc([128, 4, Wm])

        def reg(d, mi):
            return d[:, mi * Wm:(mi + 1) * Wm]

        def st(mi):
            return (mi * Wm) % 512 == 0

        W4 = 4 * Wm
        for gq in range(2):
            col = gq * 512
            c4 = gq * 4
            for (wn, xs, raw) in (("rw_wr", XRc, RAWr), ("rw_wk", XKc, RAWk), ("rw_wv", XVc, RAWv)):
                d = bank(2)
                for kc in range(8):
                    wb = wmat2(A[wn][j], kc, col, 512)
                    for mi in range(4):
                        k.mm(reg(d, mi), wb[:, mi * 128:(mi + 1) * 128], xs[kc], start=(kc == 0 and st(mi)), stop=(kc == 7),
                             skip_group_check=True)
                evac(raw, g4(d[:, 0:W4]))
            r4, k4, v4 = RAWr, RAWk, RAWv
            T0, T1, T2, T3, T4, T5, T6, T7 = [g4(x) for x in t]
            d = bank(2)
            wb = wrow(A["rw_w2"][j], 0, 64, col)
            for mi in range(4):
                k.mm(reg(d, mi), wb[:, mi * 128:(mi + 1) * 128], HW[0:64, :], start=st(mi), stop=True, skip_group_check=True)
            k.tt("dve", T0, g4(d[:, 0:W4]), pb4("w0%d" % j, c4), ADD)
            k.act(T0, T0, AF.Sigmoid)
            d = bank(2)
            wb = wrow(A["rw_a2"][j], 0, 64, col)
            for mi in range(4):
                k.mm(reg(d, mi), wb[:, mi * 128:(mi + 1) * 128], HA[0:64, :], start=st(mi), stop=True, skip_group_check=True)
            k.tt("dve", T1, g4(d[:, 0:W4]), pb4("a0%d" % j, c4), ADD)
            k.act(T1, T1, AF.Sigmoid)
            if j == 0:
                k.copy("act", vf[:, c4:c4 + 4, col0:col0 + Wm], v4)
            else:
                d = bank(2)
                wb = wrow(A["rw_v2"][0], 0, 32, col)
                for mi in range(4):
                    k.mm(reg(d, mi), wb[:, mi * 128:(mi + 1) * 128], HV[0:32, :], start=st(mi), stop=True, skip_group_check=True)
                k.tt("dve", T2, g4(d[:, 0:W4]), pb4("v0", c4), ADD)
                k.act(T2, T2, AF.Sigmoid)
                k.tt("dve", T3, vf[:, c4:c4 + 4, col0:col0 + Wm], v4, SUB)
                k.tt("dve", T3, T3, T2, MUL)
                k.tt("dve", v4, v4, T3, ADD)
            k.copy("act", VB[:, c4:c4 + 4, :], v4)
            d = bank(2)
            wb = wrow(A["rw_g2"][j], 0, 128, col)
            for mi in range(4):
                k.mm(reg(d, mi), wb[:, mi * 128:(mi + 1) * 128], HG0[:, :], start=st(mi), stop=False, skip_group_check=True)
            wb = wrow(A["rw_g2"][j], 128, 160, col)
            for mi in range(4):
                k.mm(reg(d, mi), wb[:, mi * 128:(mi + 1) * 128], HG1[0:32, :], start=False, stop=True, skip_group_check=True)
            k.copy("act", G[:, c4:c4 + 4, :], g4(d[:, 0:W4]))
            k.tt("dve", T2, k4, pb4("kk%d" % j, c4), MUL)
            sb = t[7].bitcast(BF16)[:, 0:W4]
            k.act(g4(sb), T2, AF.Square)
            d = bank(2)
            for mi in range(4):
                k.mm(reg(d, mi), bonesb[:], sb[:, mi * Wm:(mi + 1) * Wm], start=st(mi), stop=True, skip_group_check=True)
            k.act(T3, g4(d[:, 0:W4]), AF.Ln, bias=cst[:, 7:8])
            k.act(T3, T3, AF.Exp, scale=-0.5)
            k.tt("dve", T2, T2, T3, MUL)
            k.tt("dve", T3, T1, pb4("ka%d" % j, c4), MUL)
            k.tt("dve", T3, T3, pb4("oka%d" % j, c4), ADD)
            k.tt("dve", T4, k4, T3, MUL)
            k.tt("dve", T5, T2, T1, MUL)
            k.scan(t[6], rmask[kind][:, 0:W4], t[0], 0.0, MUL, ADD)
            k.act(T7, T6, AF.Exp, scale=-C0)
            k.tt("dve", RT[:, c4:c4 + 4, :], r4, T7, MUL)
            k.copy("dve", PCt[:, c4:c4 + 4, :], T7.rr("p c (n q) -> p c n q", q=C)[:, :, :, C - 1])
            k.tt("dve", T3, T6, T0, SUB)
            k.act(T3, T3, AF.Exp, scale=-C0)
            k.tt("dve", KT[:, c4:c4 + 4, :], T2, T3, MUL)
            k.act(T7, T6, AF.Exp, scale=C0)
            k.tt("dve", KH[:, c4:c4 + 4, :], T4, T7, MUL)
            k.tt("dve", BH[:, c4:c4 + 4, :], T5, T7, MUL)
            k.tt("dve", T3, r4, pb4("rk%d" % j, c4), MUL)
            sb = t[7].bitcast(BF16)[:, 0:W4]
            k.tt("dve", g4(sb), T3, T4, MUL)
            d = bank(2)
            for mi in range(4):
                k.mm(reg(d, mi), bonesb[:], sb[:, mi * Wm:(mi + 1) * Wm], start=st(mi), stop=True, skip_group_check=True)
            k.tt("dve", BV[:, c4:c4 + 4, :], g4(d[:, 0:W4]), v4, MUL)

        STOP = 9
        if STOP <= 1:
            return
        AR.reset(mark)
        grs = groups_of(kind, Wm)
        VT = [AR.take(2048, BF16) for _ in grs]
        KHT = [AR.take(2048, BF16) for _ in grs]
        BHT = [AR.take(2048, BF16) for _ in grs]
        n_am = len(grs) if kind == "p" else 1
        ams = [[AR.take(16 * C * 2, BF16, "p (h c) -> p h c", h=16) for _ in range(8)] for _ in range(n_am)]
        RHSsb = AR.take(2048, BF16)
        Usb = AR.take(2048, BF16)
        YT, YTc = AR.buf(8, Wm, F32)
        YG, YGc = AR.buf(8, Wm, BF16)
        Slds = [AR.take(4096), AR.take(4096)] if kind == "s" else [None, None]
        Sst = AR.take(4096)
        for g in range(len(grs)):
            to_tokmajor(VT[g], VBc, kind, g, grs[g])
            to_tokmajor(KHT[g], KHc, kind, g, grs[g])
            to_tokmajor(BHT[g], BHc, kind, g, grs[g])

        if STOP <= 2:
            return

        def hd(h):
            return h // 2, (h % 2) * 64

        def hs(h):
            return (h % 2) * 8 + h // 2

        def v3(d):
            return d[:, 0:1024].rr("p (h c) -> p h c", h=16)[:, :, 0:C]

        def st_view(i):
            return (ST32[i][:, 0:512].rr("p (c v) -> p c v", c=8),
                    (STB[i][:, 0:512].rr("p (c v) -> p c v", c=8), STB[i][:, 512:1024].rr("p (c v) -> p c v", c=8)))

        def mask_state(H32, Hm):
            k.ts("dve", Hm[0], H32, cst[:, 5:6], MUL)
            k.ts("dve", Hm[1], H32, cst[:, 6:7], MUL)

        def phase1(items):
            def amat(ch, dst, lh, rh, mk):
                d = bank(2)
                for (c0, pb, sidx) in ch:
                    for h in range(16):
                        c, hb = hd(h)
                        k.mm(d[pb:pb + C, hs(h) * 64:hs(h) * 64 + C], lh[c][hb:hb + 64, c0:c0 + C], rh[c][hb:hb + 64, c0:c0 + C])
                k.tt("dve", dst, v3(d), msk[kind + mk][:].us(1).bc([128, 16, C]), MUL)

            def mm3(ch, lh, rh):
                d = bank(2)
                for (c0, pb, sidx) in ch:
                    for h in range(16):
                        k.mm(d[pb:pb + C, h * 64:h * 64 + C], lh[pb:pb + C, h, :], rh[pb:pb + C, h, :])
                return d

            for it in items:
                Msb, Nsb, Xs, M2, N2, AkkT, ArkT, ArbT = it["am"]
                ch = it["ch"]
                amat(ch, Msb, BHc, KTc, "su")
                amat(ch, Nsb, KTc, BHc, "sl")
                amat(ch, AkkT, KHc, KTc, "su")
                amat(ch, ArkT, KHc, RTc, "ui")
                amat(ch, ArbT, BHc, RTc, "ui")
                k.stt(Xs, Msb, -1.0, msk[kind + "id"][:].us(1).bc([128, 16, C]), MUL, ADD)
                it["p"] = [Msb, Nsb, M2, N2]
            p_ = 2
            while p_ < C:
                lastlv = (p_ * 2 >= C)
                for it in items:
                    Mp, Nn, Mo, No = it["p"]
                    d = mm3(it["ch"], Mp, Nn)
                    evac(No, v3(d))
                if not lastlv:
                    for it in items:
                        Mp, Nn, Mo, No = it["p"]
                        d = mm3(it["ch"], Nn, Mp)
                        evac(Mo, v3(d))
                for it in items:
                    Mp, Nn, Mo, No = it["p"]
                    Xs = it["am"][2]
                    d = mm3(it["ch"], No, Xs)
                    k.tt("dve", Xs, v3(d), Xs, ADD)
                    it["p"] = [Mo, No, Mp, Nn]
                p_ *= 2

        if kind == "p":
            items_all = [dict(ch=ch, am=ams[g]) for g, ch in enumerate(grs)]
            phase1(items_all)
        for g, ch in enumerate(grs):
            if kind == "p":
                it = items_all[g]
            else:
                it = dict(ch=ch, am=ams[0])
                phase1([it])
            Msb, Nsb, Xs, M2, N2, AkkT, ArkT, ArbT = it["am"]
            if kind == "s":
                for (c0, pb, sidx) in ch:
                    H32, Hbf = st_view(sidx % 4)
                    Sl = Slds[sidx % 2]
                    k.dma(Sl[0:64, :].rr("p (h k) -> p h k", h=16), A["state_rwkv"][j, sidx].rearrange("h v k -> v h k"))
                    b = bank()
                    for c in range(8):
                        k.tr(b[:, c * 64:(c + 1) * 64], Sl[0:64, c * 128:(c + 1) * 128], identf[0:64, 0:64])
                    evac(H32, b[:, 0:512].rr("p (c v) -> p c v", c=8))
                    mask_state(H32, Hbf)
            for ci, (c0, pb, sidx) in enumerate(ch):
                n = c0 // C
                if kind == "p":
                    H32, Hbf = st_view(0 if j == 0 else 3)
                else:
                    H32, Hbf = st_view(sidx % 4)
                d = bank(2)
                for h in range(16):
                    c, hb = hd(h)
                    k.mm(d[pb:pb + C, h * 64:(h + 1) * 64], KTc[c][:, c0:c0 + C], Hbf[h % 2][:, c, :],
                         start=(h % 8 == 0), stop=False, skip_group_check=True)
                for h in range(16):
                    k.mm(d[pb:pb + C, h * 64:(h + 1) * 64], AkkT[pb:pb + C, hs(h), :], VT[g][pb:pb + C, h * 64:(h + 1) * 64],
                         start=False, stop=True, skip_group_check=True)
                k.copy("act", RHSsb[pb:pb + C, :], d[pb:pb + C, 0:1024])
                d = bank(2)
                for h in range(16):
                    k.mm(d[pb:pb + C, h * 64:(h + 1) * 64], Xs[pb:pb + C, hs(h), :], RHSsb[pb:pb + C, h * 64:(h + 1) * 64])
                k.act(Usb[pb:pb + C, :], d[pb:pb + C, 0:1024], AF.Identity, scale=-1.0)
                yps = bank()
                for h in range(16):
                    c, hb = hd(h)
                    k.mm(yps[hb:hb + 64, c * C:(c + 1) * C], Hbf[h % 2][:, c, :], RTc[c][:, c0:c0 + C],
                         start=(h < 2), stop=False, skip_group_check=True)
                for h in range(16):
                    c, hb = hd(h)
                    o = yps[hb:hb + 64, c * C:(c + 1) * C]
                    k.mm(o, VT[g][pb:pb + C, h * 64:(h + 1) * 64], ArkT[pb:pb + C, hs(h), :], start=False, stop=False, skip_group_check=True)
                    k.mm(o, Usb[pb:pb + C, h * 64:(h + 1) * 64], ArbT[pb:pb + C, hs(h), :], start=False, stop=True, skip_group_check=True)
                evac(YT[:, :, c0:c0 + C], yps[:, 0:8 * C].rr("p (c w) -> p c w", c=8))
                hps = bank()
                for h in range(16):
                    c, hb = hd(h)
                    o = hps[hb:hb + 64, c * 64:(c + 1) * 64]
                    k.mm(o, KHT[g][pb:pb + C, c * 128 + hb:c * 128 + hb + 64], VT[g][pb:pb + C, h * 64:(h + 1) * 64], start=True, stop=False)
                    k.mm(o, BHT[g][pb:pb + C, c * 128 + hb:c * 128 + hb + 64], Usb[pb:pb + C, h * 64:(h + 1) * 64], start=False, stop=True)
                k.tt("dve", H32, hps[:, 0:512].rr("p (c v) -> p c v", c=8), H32, ADD)
                k.tt("dve", H32, H32, PCt[:, :, n:n + 1].bc([128, 8, 64]), MUL)
                if kind == "p":
                    mask_state(H32, Hbf)
                fin_p = (kind == "p" and tl.last and col0 + Wm == tl.W and g == len(grs) - 1 and ci == len(ch) - 1)
                if kind == "s" or fin_p:
                    dram = A["s_rwkv"][j, sidx] if kind == "s" else A["p_rwkv"][j]
                    d = bank(2)
                    for c in range(8):
                        k.tr(d[0:64, c * 128:(c + 1) * 128], H32[:, c, :], identf[:])
                    evac(Sst[0:64, :], d[0:64, 0:1024])
                    k.dma(dram.rearrange("h v k -> v h k"), Sst[0:64, :].rr("p (h k) -> p h k", h=16), is_output=True)

        if STOP <= 5:
            return
        tq0 = AR.take(Wm * 16)
        tq1 = AR.take(Wm * 16)
        Q0, Q1 = g4(tq0), g4(tq1)
        for gq in range(2):
            c4 = gq * 4
            y4 = YT[:, c4:c4 + 4, :]
            d = bank(2)
            for mi in range(4):
                k.mm(reg(d, mi), bones[:], YTc[c4 + mi], start=st(mi), stop=True, skip_group_check=True)
            k.stt(Q0, g4(d[:, 0:W4]), -1.0 / 64, y4, MUL, ADD)
            qb = tq1.bitcast(BF16)[:, 0:W4]
            k.act(g4(qb), Q0, AF.Square)
            d = bank(2)
            for mi in range(4):
                k.mm(reg(d, mi), bonesb[:], qb[:, mi * Wm:(mi + 1) * Wm], start=st(mi), stop=True, skip_group_check=True)
            k.act(Q1, g4(d[:, 0:W4]), AF.Ln, bias=cst[:, 2:3], scale=1.0 / 64)
            k.act(Q1, Q1, AF.Exp, scale=-0.5)
            k.tt("dve", Q0, Q0, Q1, MUL)
            k.tt("dve", Q0, Q0, pb4("lnxw%d" % j, c4), MUL)
            k.tt("dve", Q0, Q0, pb4("lnxb%d" % j, c4), ADD)
            k.tt("dve", Q0, Q0, BV[:, c4:c4 + 4, :], ADD)
            k.tt("dve", YG[:, c4:c4 + 4, :], Q0, G[:, c4:c4 + 4, :], MUL)
        ln_begin(Wm)
        for col in (0, 512):
            bs = acc_group(A["rw_wo"][j], 8, col, 512, lambda kc: YGc[kc], Wm)
            for mi in range(4):
                ln_add(col // 128 + mi, bs[mi][:, 0:Wm], col0)
        ln_finish("ln1_w%d" % l, "ln1_b%d" % l, col0)

    def mk_tiles():
        tiles = []
        for ti in range(4):
            tl = Tl()
            tl.kind, tl.W, tl.nseq, tl.L, tl.t0, tl.last = "p", 512, 1, 512, ti * 512, (ti == 3)
            tiles.append(tl)
        tl = Tl()
        tl.kind, tl.W, tl.nseq, tl.L, tl.t0, tl.last = "s", 64, 16, 4, 0, True
        tiles.append(tl)
        return tiles

    def run_all(layers=(0, 1, 2, 3), tiles=None):
        for tl in (tiles or mk_tiles()):
            load_tile(tl)
            row0 = tl.t0 if tl.kind == "p" else 2048
            for l in layers:
                if l % 3 == 0:
                    if tl.kind == "p":
                        for half in (0, 256):
                            rwkv_mixer(l, l // 3, tl, half, 256)
                    else:
                        rwkv_mixer(l, l // 3, tl, 0, 64)
                elif l == 1:
                    gla_like(l, tl, HGCFG)
                else:
                    gla_like(l, tl, GLCFG)
                dbg_dump2(2 * l, tl, row0)
                ffn_sublayer(l, tl)
                dbg_dump2(2 * l + 1, tl, row0)
            if tl.kind == "p":
                store_tile(tl, A["y_prompt"], 0)
            else:
                store_tile(tl, A["y_sample"], 0)

    def dbg_dump2(idx, tl, row0):
        if not dbg_n or idx >= dbg_n:
            return
        AR.reset()
        if tl.kind == "p":
            for jj in range(4):
                store_rows(A["dbg"][idx][row0 + jj * 128: row0 + (jj + 1) * 128, :],
                           lambda c, jj=jj: x32[:, c, jj * 128:(jj + 1) * 128], 128)
        else:
            store_rows(A["dbg"][idx][2048:2112, :], lambda c: x32[:, c, 0:64], 64)

    return k, A, locals()


_NC_CACHE = {}


def _get_nc():
    if "nc" not in _NC_CACHE:
        nc = bass.Bass("TRN2", target_bir_lowering=False)
        k, A, L = build(nc, dbg_n=0)
        L["run_all"]()
        k.finish()
        k.close()
        _NC_CACHE["nc"] = nc
    return _NC_CACHE["nc"]


def kernel(**inputs):
    n = 8
    f32 = np.float32
    inp = {kk: np.asarray(v) for kk, v in inputs.items()}
    in_maps = []
    for c in range(n):
        s = slice(16 * c, 16 * (c + 1))
        m = {}
        m["x_prompt"] = np.ascontiguousarray(inp["x_prompt"][c], dtype=f32)
        m["x_sample"] = np.ascontiguousarray(inp["x_sample"][s], dtype=f32).reshape(64, 1024)
        m["state_rwkv"] = np.ascontiguousarray(inp["state_rwkv"][:, s], dtype=f32)
        m["state_rwkv_shift"] = np.ascontiguousarray(inp["state_rwkv_shift"][:, s], dtype=f32)
        m["state_hgrn"] = np.ascontiguousarray(inp["state_hgrn"][0, s], dtype=f32)
        m["state_gla"] = np.ascontiguousarray(inp["state_gla"][0, s], dtype=f32)
        m["state_ffn_conv"] = np.ascontiguousarray(inp["state_ffn_conv"][:, s], dtype=f32)
        for w in WEIGHT_SHAPES:
            m[w] = np.ascontiguousarray(inp[w], dtype=f32)
        in_maps.append(m)
    nc = _get_nc()
    res = run_bass_kernel_spmd(nc, in_maps, core_ids=list(range(n)))
    R = res.results
    y_prompt = np.stack([R[c]["y_prompt"] for c in range(n)], 0)
    y_sample = np.concatenate([R[c]["y_sample"].reshape(16, 4, 1024) for c in range(n)], 0)
    p_rwkv = np.stack([R[c]["p_rwkv"] for c in range(n)], 1)
    p_shift = np.stack([R[c]["p_shift"] for c in range(n)], 1)
    p_hgrn = np.stack([R[c]["p_hgrn"] for c in range(n)], 0)[None]
    p_gla = np.stack([R[c]["p_gla"] for c in range(n)], 0)[None]
    p_conv = np.stack([R[c]["p_conv"] for c in range(n)], 1)
    s_rwkv = np.concatenate([R[c]["s_rwkv"] for c in range(n)], 1)
    s_shift = np.concatenate([R[c]["s_shift"] for c in range(n)], 1)
    s_hgrn = np.concatenate([R[c]["s_hgrn"] for c in range(n)], 0)[None]
    s_gla = np.concatenate([R[c]["s_gla"] for c in range(n)], 0)[None]
    s_conv = np.concatenate([R[c]["s_conv"] for c in range(n)], 1)
    outs = (y_prompt, y_sample, p_rwkv, p_shift, p_hgrn, p_gla, p_conv, s_rwkv, s_shift, s_hgrn, s_gla, s_conv)
    return tuple(np.ascontiguousarray(o, dtype=f32) for o in outs)
```
